# Optimizing a Trainium2 kernel written in Bass

```python
import math
import jax, jax.numpy as jnp
from jax import lax
import numpy as np

D_MODEL = 1024
BATCH = 8
SEQ = 2048
DEPTH = 1
DEC_BATCH = 128
DEC_SEQ = 1
PAST_LEN = 16384
PAGE_SIZE = 128

GLA_HEADS = 4
GLA_DK = D_MODEL // 2
GLA_DV = D_MODEL
GLA_HK = GLA_DK // GLA_HEADS
GLA_HV = GLA_DV // GLA_HEADS
GLA_LOWRANK = 16
GLA_TAU = 16.0
GLA_CHUNK = 64
GM_GROUPS = 4
GM_WIDTH = D_MODEL
GM_GC = GM_WIDTH // GM_GROUPS
GM_CHUNK = 128
D_FF = 2816
N_MOD = 9
EPS = 1e-6
IN_SPLITS = (GLA_DK, GLA_DK, GLA_DV, GLA_DV, GLA_LOWRANK, GM_WIDTH, GM_WIDTH, 2 * D_MODEL)
D_IN = 2 * GLA_DK + 2 * GLA_DV + GLA_LOWRANK + 2 * GM_WIDTH + 2 * D_MODEL

kernel_name = "hybrid_gla_chunk_gmlp_macaron_decoder_step"


def rmsnorm(x, w):
    xf = x.astype(jnp.float32)
    y = xf * lax.rsqrt(jnp.mean(xf * xf, axis=-1, keepdims=True) + EPS)
    return (y * w.astype(jnp.float32)).astype(x.dtype)


def layernorm(x, w, b):
    xf = x.astype(jnp.float32)
    mu = jnp.mean(xf, axis=-1, keepdims=True)
    d = xf - mu
    y = d * lax.rsqrt(jnp.mean(d * d, axis=-1, keepdims=True) + EPS)
    return (y * w.astype(jnp.float32) + b.astype(jnp.float32)).astype(x.dtype)


def modulate(h, shift, scale):
    return h * (1.0 + scale[:, None, :]) + shift[:, None, :]


def swiglu(h, w13, w2):
    a, b = jnp.split(h @ w13, 2, axis=-1)
    return (jax.nn.silu(a) * b) @ w2


def gla_chunked(q, k, v, g, s0):
    B, T, H, K = q.shape
    V = v.shape[-1]
    C = math.gcd(T, GLA_CHUNK)
    N = T // C

    def to_chunks(a):
        return jnp.moveaxis(a.reshape(B, N, C, H, a.shape[-1]), 1, 0)

    qc, kc, vc, gc = to_chunks(q), to_chunks(k), to_chunks(v), to_chunks(g)
    causal = jnp.tril(jnp.ones((C, C), dtype=bool))[None, :, :, None, None]

    def step(S, inp):
        qi, ki, vi, gi = inp
        b = jnp.cumsum(gi.astype(jnp.float32), axis=1)
        rel = b[:, :, None] - b[:, None, :]
        decay = jnp.exp(jnp.where(causal, rel, -jnp.inf))
        scores = jnp.einsum('bthk,bshk,btshk->bhts', qi, ki, decay)
        o = (jnp.einsum('bhts,bshv->bthv', scores, vi)
             + jnp.einsum('bthk,bhkv->bthv', qi * jnp.exp(b), S))
        b_last = b[:, -1]
        k_dec = ki * jnp.exp(b_last[:, None] - b)
        S_new = jnp.exp(b_last)[..., None] * S + jnp.einsum('bshk,bshv->bhkv', k_dec, vi)
        return S_new.astype(S.dtype), o.astype(vi.dtype)

    S, o = lax.scan(step, s0, (qc, kc, vc, gc))
    o = jnp.moveaxis(o, 0, 1).reshape(B, T, H, V)
    return o, S


def chunk_spatial_gate(u, v, w_s, b_s):
    B, T, W = v.shape
    L = min(T, GM_CHUNK)
    N = T // L
    mask = jnp.tril(jnp.ones((L, L), dtype=bool))
    ws = jnp.where(mask[None], w_s[:, :L, :L], 0.0)
    vc = v.reshape(B, N, L, GM_GROUPS, GM_GC)
    mixed = jnp.einsum('gts,bnsgc->bntgc', ws, vc) + b_s[:, :L].T[None, None, :, :, None]
    return u * mixed.reshape(B, T, W)


def token_mix(h, s0, w_in, w_a2, b_a, gla_norm_w, w_pa, gm_ln_w, gm_ln_b, gm_ws, gm_bs, w_pb, w_o):
    B, T, _ = h.shape
    p = h @ w_in
    q, k, v, r, a_lr, u, gv, gates = jnp.split(p, [int(i) for i in np.cumsum(IN_SPLITS)[:-1]], axis=-1)
    log_a = jax.nn.log_sigmoid(a_lr @ w_a2 + b_a) / GLA_TAU
    qh = q.reshape(B, T, GLA_HEADS, GLA_HK) * (GLA_HK ** -0.5)
    kh = k.reshape(B, T, GLA_HEADS, GLA_HK)
    vh = v.reshape(B, T, GLA_HEADS, GLA_HV)
    gh = log_a.reshape(B, T, GLA_HEADS, GLA_HK)
    o, S = gla_chunked(qh, kh, vh, gh, s0)
    o = rmsnorm(o, gla_norm_w).reshape(B, T, GLA_DV) * jax.nn.silu(r)
    y_a = o @ w_pa
    u = jax.nn.gelu(u, approximate=False)
    gv_n = layernorm(jax.nn.gelu(gv, approximate=False), gm_ln_w, gm_ln_b)
    y_b = chunk_spatial_gate(u, gv_n, gm_ws, gm_bs) @ w_pb
    ga, gb = jnp.split(gates, 2, axis=-1)
    merged = jax.nn.sigmoid(ga) * y_a + jax.nn.sigmoid(gb) * y_b
    return merged @ w_o, S, gv_n


def hybrid_layer(x, c, s0, w_ada, b_ada, norm1_w, ffn1_w13, ffn1_w2, norm2_w, w_in, w_a2, b_a,
                 gla_norm_w, w_pa, gm_ln_w, gm_ln_b, gm_ws, gm_bs, w_pb, w_o, norm3_w,
                 ffn2_w13, ffn2_w2):
    mod = jax.nn.silu(c) @ w_ada + b_ada
    sh1, sc1, g1, sh2, sc2, g2, sh3, sc3, g3 = jnp.split(mod, N_MOD, axis=-1)
    h = modulate(rmsnorm(x, norm1_w), sh1, sc1)
    x = x + 0.5 * g1[:, None, :] * swiglu(h, ffn1_w13, ffn1_w2)
    h = modulate(rmsnorm(x, norm2_w), sh2, sc2)
    m, S, gv_n = token_mix(h, s0, w_in, w_a2, b_a, gla_norm_w, w_pa, gm_ln_w, gm_ln_b,
                           gm_ws, gm_bs, w_pb, w_o)
    x = x + g2[:, None, :] * m
    h = modulate(rmsnorm(x, norm3_w), sh3, sc3)
    x = x + 0.5 * g3[:, None, :] * swiglu(h, ffn2_w13, ffn2_w2)
    return x, S, gv_n


def setup_inputs(seed: int = 0) -> dict:
    key = jax.random.key(seed)
    ks = jax.random.split(key, 32)
    f32 = jnp.float32

    def nrm(k, shape, scale):
        return jax.random.normal(k, shape, f32) * scale

    D = D_MODEL
    return {
        "x_prompt": nrm(ks[0], (BATCH, SEQ, D), 1.0),
        "x_sample": nrm(ks[1], (DEC_BATCH, DEC_SEQ, D), 1.0),
        "state_gla": nrm(ks[2], (DEPTH, DEC_BATCH, GLA_HEADS, GLA_HK, GLA_HV), 0.5),
        "c_prompt": nrm(ks[3], (BATCH, D), 1.0),
        "c_sample": nrm(ks[4], (DEC_BATCH, D), 1.0),
        "w_ada": nrm(ks[5], (DEPTH, D, N_MOD * D), 0.3 * D ** -0.5),
        "b_ada": nrm(ks[6], (DEPTH, N_MOD * D), 0.02),
        "norm1_w": 1.0 + nrm(ks[7], (DEPTH, D), 0.05),
        "ffn1_w13": nrm(ks[8], (DEPTH, D, 2 * D_FF), D ** -0.5),
        "ffn1_w2": nrm(ks[9], (DEPTH, D_FF, D), D_FF ** -0.5),
        "norm2_w": 1.0 + nrm(ks[10], (DEPTH, D), 0.05),
        "w_in": nrm(ks[11], (DEPTH, D, D_IN), D ** -0.5),
        "w_a2": nrm(ks[12], (DEPTH, GLA_LOWRANK, GLA_DK), GLA_LOWRANK ** -0.5),
        "b_a": nrm(ks[13], (DEPTH, GLA_DK), 0.1),
        "gla_norm_w": 1.0 + nrm(ks[14], (DEPTH, GLA_HEADS, GLA_HV), 0.05),
        "w_pa": nrm(ks[15], (DEPTH, GLA_DV, D), GLA_DV ** -0.5),
        "gm_ln_w": 1.0 + nrm(ks[16], (DEPTH, GM_WIDTH), 0.05),
        "gm_ln_b": nrm(ks[17], (DEPTH, GM_WIDTH), 0.02),
        "gm_ws": nrm(ks[18], (DEPTH, GM_GROUPS, GM_CHUNK, GM_CHUNK), GM_CHUNK ** -0.5),
        "gm_bs": 1.0 + nrm(ks[19], (DEPTH, GM_GROUPS, GM_CHUNK), 0.1),
        "w_pb": nrm(ks[20], (DEPTH, GM_WIDTH, D), GM_WIDTH ** -0.5),
        "w_o": nrm(ks[21], (DEPTH, D, D), D ** -0.5),
        "norm3_w": 1.0 + nrm(ks[22], (DEPTH, D), 0.05),
        "ffn2_w13": nrm(ks[23], (DEPTH, D, 2 * D_FF), D ** -0.5),
        "ffn2_w2": nrm(ks[24], (DEPTH, D_FF, D), D_FF ** -0.5),
        "normf_w": 1.0 + nrm(ks[25], (D,), 0.05),
    }


def reference(x_prompt, x_sample, state_gla, c_prompt, c_sample, w_ada, b_ada, norm1_w,
              ffn1_w13, ffn1_w2, norm2_w, w_in, w_a2, b_a, gla_norm_w, w_pa, gm_ln_w, gm_ln_b,
              gm_ws, gm_bs, w_pb, w_o, norm3_w, ffn2_w13, ffn2_w2, normf_w):
    xp, xs = x_prompt, x_sample
    sp_list, ss_list, vs_list = [], [], []
    for l in range(DEPTH):
        params = (w_ada[l], b_ada[l], norm1_w[l], ffn1_w13[l], ffn1_w2[l], norm2_w[l], w_in[l],
                  w_a2[l], b_a[l], gla_norm_w[l], w_pa[l], gm_ln_w[l], gm_ln_b[l], gm_ws[l],
                  gm_bs[l], w_pb[l], w_o[l], norm3_w[l], ffn2_w13[l], ffn2_w2[l])
        s0p = jnp.zeros((xp.shape[0], GLA_HEADS, GLA_HK, GLA_HV), dtype=state_gla.dtype)
        xp, sp, _ = hybrid_layer(xp, c_prompt, s0p, *params)
        xs, ss, vs = hybrid_layer(xs, c_sample, state_gla[l], *params)
        sp_list.append(sp)
        ss_list.append(ss)
        vs_list.append(vs)
    y_prompt = rmsnorm(xp, normf_w)
    y_sample = rmsnorm(xs, normf_w)
    state_gla_prompt = jnp.stack(sp_list)
    state_gla_sample = jnp.stack(ss_list)
    gm_v_sample = jnp.stack(vs_list)
    return (y_prompt, y_sample, state_gla_prompt, state_gla_sample, gm_v_sample)
```

```python
import numpy as np
from contextlib import ExitStack
import concourse.bass as bass
import concourse.mybir as mybir
from concourse.bass_utils import run_bass_kernel_spmd

F32 = mybir.dt.float32
BF16 = mybir.dt.bfloat16
AF = mybir.ActivationFunctionType
ALU = mybir.AluOpType

NCORES = 8
D = 1024
KC = 8
T = 2048
NTB = 4
TBS = 512
NS = 16
DFF = 2816
NFC = 22
D_IN = 7184
EPS = 1e-6
ENGS = ("pe", "act", "dve", "pool", "sp")
NSEM_DMA = 12

SB_LO = 16640
SB_HI = 228352


class Op:
    __slots__ = ("eng", "seq", "fn", "waits", "sig", "sigidx", "dma", "dsem", "dval")

    def __init__(self, eng, seq, fn, dma):
        self.eng = eng
        self.seq = seq
        self.fn = fn
        self.waits = []
        self.sig = False
        self.sigidx = 0
        self.dma = dma
        self.dsem = 0
        self.dval = 0


class Buf:
    def __init__(self, name):
        self.name = name
        self.w = {}
        self.r = {}
        self.touch = {}
        self.touch_dma = []
        self.inherit = []
        self.t = None

    def all_ops(self):
        return list(self.touch.values()) + list(self.touch_dma) + list(self.inherit)


def _dedupe(ops):
    best = {}
    dm = {}
    for o in ops:
        if o.dma:
            dm[id(o)] = o
        else:
            b = best.get(o.eng)
            if b is None or b.seq < o.seq:
                best[o.eng] = o
    return list(best.values()) + list(dm.values())


class Prog:
    def __init__(self):
        self.streams = {e: [] for e in ENGS}
        self.known = {e: {f: -1 for f in ENGS} for e in ENGS}
        self.known_dma = {e: {} for e in ENGS}
        self.dma_count = {e: 0 for e in ENGS}
        self.dma_last = {e: [None] * NSEM_DMA for e in ENGS}

    def _add_dep(self, op, dep):
        if dep is None or dep is op:
            return
        e = op.eng
        if dep.dma:
            key = (dep.eng, dep.dsem)
            if self.known_dma[e].get(key, 0) >= dep.dval:
                return
            self.known_dma[e][key] = dep.dval
            op.waits.append(dep)
        else:
            if dep.eng == "pe" and e == "pe":
                return
            if self.known[e][dep.eng] >= dep.seq:
                return
            self.known[e][dep.eng] = dep.seq
            dep.sig = True
            op.waits.append(dep)

    def op(self, eng, fn, reads=(), writes=(), dma=False):
        o = Op(eng, len(self.streams[eng]), fn, dma)
        for (buf, key) in reads:
            for d in buf.inherit:
                self._add_dep(o, d)
            self._add_dep(o, buf.w.get(key))
        for (buf, key) in writes:
            for d in buf.inherit:
                self._add_dep(o, d)
            self._add_dep(o, buf.w.get(key))
            rr = buf.r.get(key)
            if rr:
                for d in rr.values():
                    self._add_dep(o, d)
        for (buf, key) in reads:
            rr = buf.r.setdefault(key, {})
            rr[id(o) if dma else eng] = o
            self._touch(buf, o)
        for (buf, key) in writes:
            buf.w[key] = o
            buf.r[key] = {}
            self._touch(buf, o)
        if dma:
            k = self.dma_count[eng]
            idx = k % NSEM_DMA
            prev = self.dma_last[eng][idx]
            if prev is not None:
                self._add_dep(o, prev)
            o.dsem = idx
            o.dval = 16 * (k // NSEM_DMA + 1)
            self.dma_last[eng][idx] = o
            self.dma_count[eng] = k + 1
        self.streams[eng].append(o)
        return o

    def _touch(self, buf, o):
        if o.dma:
            buf.touch_dma.append(o)
            if len(buf.touch_dma) > 64:
                buf.touch_dma = buf.touch_dma[-64:]
        else:
            buf.touch[o.eng] = o

    def emit(self, nc):
        for e in ENGS:
            c = 0
            for o in self.streams[e]:
                if o.sig and not o.dma:
                    c += 1
                    o.sigidx = c
        with ExitStack() as es:
            sem = {e: es.enter_context(nc.semaphore("s_" + e)) for e in ENGS}
            dsem = {e: [es.enter_context(nc.semaphore(f"d_{e}_{i}")) for i in range(NSEM_DMA)]
                    for e in ("pool", "sp")}
            es.enter_context(nc.allow_low_precision("bf16 matmul operands, fp32 accumulation"))
            block = es.enter_context(nc.Block())
            streams = self.streams

            def run(ename, engine):
                for o in streams[ename]:
                    for d in o.waits:
                        if d.dma:
                            engine.wait_ge(dsem[d.eng][d.dsem], d.dval)
                        else:
                            engine.wait_ge(sem[d.eng], d.sigidx)
                    ins = o.fn(engine)
                    if o.dma:
                        ins.then_inc(dsem[ename][o.dsem], 16)
                    elif o.sig:
                        ins.then_inc(sem[ename], 1)
                if ename in ("pool", "sp"):
                    for i, last in enumerate(self.dma_last[ename]):
                        if last is not None:
                            engine.wait_ge(dsem[ename][i], last.dval)

            @block.tensor
            def _(eng):
                run("pe", eng)

            @block.scalar
            def _(eng):
                run("act", eng)

            @block.vector
            def _(eng):
                run("dve", eng)

            @block.gpsimd
            def _(eng):
                run("pool", eng)

            @block.sync
            def _(eng):
                run("sp", eng)


class Arena:
    def __init__(self, nc, lo, hi):
        self.nc = nc
        self.lo = lo
        self.hi = hi
        self.live = []
        self.dead = []
        self.n = 0

    def alloc(self, name, shape, dtype, off=None, lo=None, hi=None):
        esz = 2 if dtype == BF16 else 4
        nb = esz
        for s in shape[1:]:
            nb *= s
        nb = (nb + 63) // 64 * 64
        lo = self.lo if lo is None else lo
        hi = self.hi if hi is None else hi
        if off is None:
            segs = sorted((s, e) for (s, e, b) in self.live)
            cur = lo
            for (s, e) in segs:
                if e <= cur:
                    continue
                if s - cur >= nb:
                    break
                cur = max(cur, e)
            off = cur
        assert off >= lo and off + nb <= hi, f"SBUF arena overflow for {name}: {off}+{nb} > {hi}"
        for (s, e, b) in self.live:
            assert e <= off or s >= off + nb, f"arena overlap {name} vs {b.name}"
        buf = Buf(name)
        ops = []
        keep = []
        for (s, e, b) in self.dead:
            if e <= off or s >= off + nb:
                keep.append((s, e, b))
                continue
            ops.extend(b.all_ops())
            if not (off <= s and e <= off + nb):
                keep.append((s, e, b))
        self.dead = keep
        buf.inherit = _dedupe(ops)
        self.n += 1
        buf.t = self.nc.alloc_sbuf_tensor_at(f"{name}_{self.n}", list(shape), dtype, offset=off)
        self.live.append((off, off + nb, buf))
        top = max(e for (s_, e, b_) in self.live if e <= self.hi) if any(e <= self.hi for (s_, e, b_) in self.live) else 0
        if top > getattr(self, "peak", 0) and off + nb <= self.hi:
            self.peak = top
            self.peak_live = [(b_.name, s_, e - s_) for (s_, e, b_) in self.live]
        return buf

    def free(self, buf):
        for i, (s, e, b) in enumerate(self.live):
            if b is buf:
                self.dead.append(self.live.pop(i))
                return
        raise KeyError(buf.name)


class Ring:
    def __init__(self, arena, lo, hi):
        self.a = arena
        self.lo = lo
        self.hi = hi
        self.ptr = lo

    def alloc(self, name, shape, dtype):
        esz = 2 if dtype == BF16 else 4
        nb = esz
        for s in shape[1:]:
            nb *= s
        nb = (nb + 63) // 64 * 64
        for _ in range(16):
            if self.ptr + nb > self.hi:
                self.ptr = self.lo
            clash = [e for (s_, e, b_) in self.a.live if s_ < self.ptr + nb and e > self.ptr and s_ >= self.lo]
            if not clash:
                break
            self.ptr = max(clash)
        off = self.ptr
        self.ptr += nb
        return self.a.alloc(name, shape, dtype, off=off, lo=self.lo, hi=self.hi)


FF_SPLITS = [(0, 7), (7, 15), (15, 22)]


def build(stage=99):
    nc = bass.Bass("TRN2", target_bir_lowering=False)
    P = Prog()
    A = Arena(nc, SB_LO, SB_HI)
    RING_BYTES = 32 * 1024
    ring = Ring(A, SB_HI - RING_BYTES, SB_HI)
    A.hi = SB_HI - RING_BYTES

    def din(name, shape):
        return nc.dram_tensor(name, list(shape), F32, kind="ExternalInput").ap()

    def dout(name, shape):
        return nc.dram_tensor(name, list(shape), F32, kind="ExternalOutput").ap()

    x_p = din("x_p", [T, D])
    x_s = din("x_s", [NS, D])
    c17 = din("c17", [17, D])
    w_ada = din("w_ada", [D, 9 * D])
    b_ada = din("b_ada", [1, 9 * D])
    norm_w = [din(f"norm{i}_w", [1, D]) for i in (1, 2, 3)]
    normf_w = din("normf_w", [1, D])
    ffn_w13 = [din("ffn1_w13", [D, 2 * DFF]), din("ffn2_w13", [D, 2 * DFF])]
    ffn_w2 = [din("ffn1_w2", [DFF, D]), din("ffn2_w2", [DFF, D])]
    ident_d = din("ident", [128, 128])
    w_in = din("w_in", [D, D_IN])
    w_a2aug = din("w_a2aug", [17, 512])
    gla_nw = din("gla_norm_w", [1, D])
    gm_lnw = din("gm_ln_w", [1, D])
    gm_lnb = din("gm_ln_b", [1, D])
    wsT_d = din("wsT", [128, 4, 128])
    bs_d = din("gm_bs", [1, 512])
    w_pa = din("w_pa", [D, D])
    w_pb = din("w_pb", [D, D])
    w_o = din("w_o", [D, D])
    tri_d = din("tri", [128, 128])
    tri64_d = din("tri64", [128, 128])
    ucum_d = din("ucum", [128, 128])
    state_s = din("state_s", [NS, 4, 128, 256])
    ws00_d = din("ws00_row", [1, D])
    bs0_d = din("bs0_row", [1, D])
    eye16_d = din("eye16", [1, 256])

    y_p = dout("y_p", [T, D])
    y_s = dout("y_s", [NS, D])
    S_p = dout("S_p", [4, 128, 256])
    S_s = dout("S_s", [NS, 4, 128, 256])
    gv_s = dout("gv_s", [NS, D])

    PS = nc.alloc_psum_tensor("ps", [128, 4096], F32)
    psb = [Buf(f"psb{i}") for i in range(8)]

    def bank(i, n=512, off=0):
        return PS[:, i * 512 + off:i * 512 + off + n]

    def bank_bf(i):
        return PS[:, i * 512:(i + 1) * 512].bitcast(BF16)

    ident = A.alloc("ident", [128, 128], F32)
    identb = A.alloc("identb", [128, 128], BF16)
    onesb = A.alloc("onesb", [128, 128], BF16)
    P.op("sp", lambda e: e.dma_start(out=ident.t[:], in_=ident_d), writes=[(ident, 0)], dma=True)
    P.op("dve", lambda e: e.tensor_copy(out=identb.t[:], in_=ident.t[:]), reads=[(ident, 0)], writes=[(identb, 0)])
    P.op("dve", lambda e: e.memset(onesb.t[:], 1.0), writes=[(onesb, 0)])

    XT = A.alloc("XT", [128, KC, T], F32)
    XS = A.alloc("XS", [NS, D], F32)
    P.op("sp", lambda e: e.dma_start(out=XS.t[:], in_=x_s), writes=[(XS, 0)], dma=True)

    c_sb = A.alloc("c_sb", [17, D], F32)
    cs_b = A.alloc("cs_b", [17, D], BF16)
    scT = A.alloc("scT", [128, KC, 32], BF16)
    P.op("sp", lambda e: e.dma_start(out=c_sb.t[:], in_=c17), writes=[(c_sb, 0)], dma=True)
    P.op("act", lambda e: e.activation(out=cs_b.t[:], in_=c_sb.t[:], func=AF.Silu), reads=[(c_sb, 0)], writes=[(cs_b, 0)])
    for kc in range(KC):
        P.op("pe", lambda e, kc=kc: e.transpose(out=bank_bf(7)[:, kc * 32:kc * 32 + 17],
                                                  in_=cs_b.t[:, kc * 128:(kc + 1) * 128],
                                                  identity=identb.t[0:17, 0:17]),
             reads=[(cs_b, 0), (identb, 0)], writes=[(psb[7], 0)])
    P.op("dve", lambda e: e.tensor_copy(out=scT.t[:, :, 0:17],
                                        in_=bank_bf(7)[:, 0:256].rearrange("p (k c) -> p k c", c=32)[:, :, 0:17]),
         reads=[(psb[7], 0)], writes=[(scT, 0)])
    A.free(c_sb)
    A.free(cs_b)

    def load_x(XHOOK):
      xin = [A.alloc(f"xin{i}", [128, D], F32, off=A.hi - (i + 1) * 4096) for i in range(4)]
      for t in range(T // 128):
        if "fn" in XHOOK:
            XHOOK["fn"](t)
        xb = xin[t % 4]
        pb = 4 + 2 * (t % 2)
        P.op("sp", lambda e, xb=xb, t=t: e.dma_start(out=xb.t[:], in_=x_p[t * 128:(t + 1) * 128, :]),
             writes=[(xb, 0)], dma=True)
        for kc in range(KC):
            P.op("pe", lambda e, xb=xb, kc=kc, pb=pb: e.transpose(
                out=PS[:, pb * 512 + kc * 128: pb * 512 + (kc + 1) * 128],
                in_=xb.t[:, kc * 128:(kc + 1) * 128], identity=ident.t[:]),
                reads=[(xb, 0), (ident, 0)], writes=[(psb[pb + kc // 4], 0)])
        eng = "act" if t % 2 == 0 else "dve"
        src = lambda pb=pb: PS[:, pb * 512:(pb + 2) * 512].rearrange("p (k c) -> p k c", c=128)
        if eng == "act":
            P.op("act", lambda e, t=t, src=src: e.activation(out=XT.t[:, :, t * 128:(t + 1) * 128], in_=src(), func=AF.Copy),
                 reads=[(psb[pb], 0), (psb[pb + 1], 0)], writes=[(XT, (k, t // 4)) for k in range(KC)])
        else:
            P.op("dve", lambda e, t=t, src=src: e.tensor_copy(out=XT.t[:, :, t * 128:(t + 1) * 128], in_=src()),
                 reads=[(psb[pb], 0), (psb[pb + 1], 0)], writes=[(XT, (k, t // 4)) for k in range(KC)])
      for b in xin:
        A.free(b)

    def mod_start(i, extra_rows):
        mod = A.alloc(f"mod{i}", [32, 3 * D], F32)
        modT = A.alloc(f"modT{i}", [128, 24, 32], F32)
        P.op("dve", lambda e: e.memset(mod.t[:], 0.0), writes=[(mod, 0)])
        P.op("sp", lambda e: e.dma_start(out=mod.t[0:17, :],
                                         in_=b_ada[0:1, i * 3 * D:(i + 1) * 3 * D].partition_broadcast(17)),
             reads=[], writes=[(mod, 0)], dma=True)
        for r, (ap, c0) in enumerate(extra_rows):
            row = 17 + r // 3
            col = (r % 3) * D
            P.op("sp", lambda e, ap=ap, row=row, col=col: e.dma_start(out=mod.t[row:row + 1, col:col + D], in_=ap),
                 writes=[(mod, 0)], dma=True)
        return mod, modT

    def mod_block(i, mod, blk, after=()):
        c0 = i * 3 * D + blk * 512
        wt = ring.alloc("wada", [128, KC, 512], BF16)
        P.op("pool", lambda e: e.dma_start(
            out=wt.t[:], in_=w_ada.rearrange("(k p) n -> p k n", p=128)[:, :, c0:c0 + 512]),
            reads=list(after), writes=[(wt, 0)], dma=True)
        for kc in range(KC):
            P.op("pe", lambda e, kc=kc: e.matmul(bank(6)[0:17, :], scT.t[:, kc, 0:17], wt.t[:, kc, :],
                                                 start=(kc == 0), stop=(kc == KC - 1)),
                 reads=[(scT, 0), (wt, 0)], writes=[(psb[6], 0)])
        P.op("dve", lambda e: e.tensor_tensor(out=mod.t[0:17, blk * 512:(blk + 1) * 512], in0=bank(6)[0:17, :],
                                              in1=mod.t[0:17, blk * 512:(blk + 1) * 512], op=ALU.add),
             reads=[(psb[6], 0), (mod, 0)], writes=[(mod, 0)])
        A.free(wt)

    def mod_finish(mod, modT):
        for j in range(24):
            P.op("pe", lambda e, j=j: e.transpose(out=PS[:, 7 * 512 + (j % 12) * 32: 7 * 512 + (j % 12) * 32 + 32],
                                                  in_=mod.t[:, j * 128:(j + 1) * 128], identity=ident.t[0:32, 0:32]),
                 reads=[(mod, 0), (ident, 0)], writes=[(psb[7], 0)])
            if j % 12 == 11:
                h = j // 12
                P.op("dve", lambda e, h=h: e.tensor_copy(
                    out=modT.t[:, h * 12:(h + 1) * 12, :],
                    in_=bank(7)[:, 0:384].rearrange("p (k c) -> p k c", c=32)),
                    reads=[(psb[7], 0)], writes=[(modT, 0)])

    ones17 = A.alloc("ones17", [1, 32], F32)
    P.op("dve", lambda e: e.memset(ones17.t[:], 1.0), writes=[(ones17, 0)])

    def mod_load(i, blk):
        c0 = i * 3 * D + blk * 512
        wt = ring.alloc("wada", [128, KC, 512], BF16)
        P.op("pool", lambda e: e.dma_start(
            out=wt.t[:], in_=w_ada.rearrange("(k p) n -> p k n", p=128)[:, :, c0:c0 + 512]),
            writes=[(wt, 0)], dma=True)
        brow = A.alloc("brow", [1, 512], F32)
        P.op("sp", lambda e: e.dma_start(out=brow.t[:], in_=b_ada[0:1, c0:c0 + 512]), writes=[(brow, 0)], dma=True)
        return wt, brow

    def mod_mm(mod, wb_, blk):
        wt, brow = wb_
        P.op("pe", lambda e: e.matmul(bank(6)[0:17, :], ones17.t[0:1, 0:17], brow.t[0:1, :], start=True, stop=False),
             reads=[(ones17, 0), (brow, 0)], writes=[(psb[6], 0)])
        for kc in range(KC):
            P.op("pe", lambda e, kc=kc: e.matmul(bank(6)[0:17, :], scT.t[:, kc, 0:17], wt.t[:, kc, :],
                                                 start=False, stop=(kc == KC - 1)),
                 reads=[(scT, 0), (wt, 0)], writes=[(psb[6], 0)])
        P.op("dve", lambda e: e.tensor_copy(out=mod.t[0:17, blk * 512:(blk + 1) * 512], in_=bank(6)[0:17, :]),
             reads=[(psb[6], 0), (mod, 0)], writes=[(mod, 0)])
        A.free(wt)
        A.free(brow)

    def mod_start_nob(i, extra_rows):
        mod = A.alloc(f"mod{i}", [32, 3 * D], F32)
        modT = A.alloc(f"modT{i}", [128, 24, 32], F32)
        P.op("dve", lambda e: e.memset(mod.t[:], 0.0), writes=[(mod, 0)])
        for r, (ap, c0) in enumerate(extra_rows):
            row = 17 + r // 3
            col = (r % 3) * D
            P.op("sp", lambda e, ap=ap, row=row, col=col: e.dma_start(out=mod.t[row:row + 1, col:col + D], in_=ap),
                 writes=[(mod, 0)], dma=True)
        return mod, modT

    def mod_pieces(i, extra_rows, out):
        mod, modT = mod_start_nob(i, extra_rows)
        out["mm"] = (mod, modT)
        wt = mod_load(i, 0)
        yield
        for blk in range(6):
            mod_mm(mod, wt, blk)
            wt = mod_load(i, blk + 1) if blk + 1 < 6 else None
            yield
        mod_finish(mod, modT)
        yield

    def compute_mod(i, extra_rows):
        mod, modT = mod_start(i, extra_rows)
        for blk in range(6):
            mod_block(i, mod, blk)
        mod_finish(mod, modT)
        return mod, modT

    def prompt_cols(modT, wrow_blk0):
        cols = A.alloc("pcols", [128, 3, KC], F32)
        P.op("dve", lambda e: e.scalar_tensor_tensor(out=cols.t[:, 0, :], in0=modT.t[:, 8:16, 16], scalar=1.0,
                                                      in1=modT.t[:, wrow_blk0:wrow_blk0 + 8, 17],
                                                      op0=ALU.add, op1=ALU.mult),
             reads=[(modT, 0)], writes=[(cols, 0)])
        P.op("dve", lambda e: e.tensor_copy(out=cols.t[:, 1, :], in_=modT.t[:, 0:8, 16]),
             reads=[(modT, 0), (cols, 0)], writes=[(cols, 0)])
        return cols

    def norm_block(tb, cols, tmp, dst, dc0):
        sq, rstd, nt = tmp
        for kc in range(KC):
            P.op("act", lambda e, kc=kc: e.activation(out=sq.t[:, kc % 4, :], in_=XT.t[:, kc, tb * TBS:(tb + 1) * TBS],
                                                      func=AF.Square),
                 reads=[(XT, (kc, tb))], writes=[(sq, kc % 4)])
            P.op("pe", lambda e, kc=kc: e.matmul(bank(7), onesb.t[:], sq.t[:, kc % 4, :], start=(kc == 0), stop=(kc == KC - 1)),
                 reads=[(onesb, 0), (sq, kc % 4)], writes=[(psb[7], 0)])
        P.op("act", lambda e: e.activation(out=rstd.t[:], in_=bank(7), func=AF.Ln, scale=1.0 / D, bias=EPS),
             reads=[(psb[7], 0)], writes=[(rstd, 0)])
        P.op("act", lambda e: e.activation(out=rstd.t[:], in_=rstd.t[:], func=AF.Exp, scale=-0.5),
             reads=[(rstd, 0)], writes=[(rstd, 0)])
        for kc in range(KC):
            t = nt[kc % 2]
            P.op("dve", lambda e, kc=kc, t=t: e.scalar_tensor_tensor(
                out=t.t[:], in0=XT.t[:, kc, tb * TBS:(tb + 1) * TBS], scalar=cols.t[:, 0, kc:kc + 1],
                in1=rstd.t[:], op0=ALU.mult, op1=ALU.mult),
                reads=[(XT, (kc, tb)), (cols, 0), (rstd, 0)], writes=[(t, 0)])
            if kc % 4 == 3:
                P.op("dve", lambda e, kc=kc, t=t: e.tensor_scalar(out=dst.t[:, kc, dc0:dc0 + TBS], in0=t.t[:],
                                                                  scalar1=cols.t[:, 1, kc:kc + 1], scalar2=None, op0=ALU.add),
                     reads=[(t, 0), (cols, 0)], writes=[(dst, (kc, tb))])
            else:
                P.op("act", lambda e, kc=kc, t=t: e.activation(out=dst.t[:, kc, dc0:dc0 + TBS], in_=t.t[:],
                                                               func=AF.Identity, bias=cols.t[:, 1, kc:kc + 1], scale=1.0),
                     reads=[(t, 0), (cols, 0)], writes=[(dst, (kc, tb))])

    def norm_tmp():
        sq = A.alloc("sq", [128, 4, TBS], BF16)
        rstd = A.alloc("rstd", [128, TBS], F32)
        nt = [A.alloc(f"nt{i}", [128, TBS], F32) for i in range(2)]
        return sq, rstd, nt

    def free_norm_tmp(tmp):
        sq, rstd, nt = tmp
        A.free(sq)
        A.free(rstd)
        for b in nt:
            A.free(b)

    def ffn(idx, gbuf, HT, samp=None, hooks=None):
        hooks = hooks or {}
        first_group = [True]
        w13 = ffn_w13[idx].rearrange("(k p) n -> p k n", p=128)
        w2 = ffn_w2[idx].rearrange("(k p) n -> p k n", p=128)
        sa = [A.alloc(f"sa{i}", [128, TBS], F32) for i in range(2)]
        nev = 0
        for (c_lo, c_hi) in FF_SPLITS:
            nch = c_hi - c_lo
            act = A.alloc("act", [128, nch, T], BF16)
            g0 = c_lo
            while g0 < c_hi:
                g1 = min(g0 + 4, c_hi)
                ncol = (g1 - g0) * 128
                wa = ring.alloc("wa", [128, KC, ncol], BF16)
                wb = ring.alloc("wb", [128, KC, ncol], BF16)
                P.op("pool", lambda e, wa=wa, g0=g0, ncol=ncol: e.dma_start(
                    out=wa.t[:], in_=w13[:, :, g0 * 128:g0 * 128 + ncol]), writes=[(wa, 0)], dma=True)
                P.op("pool", lambda e, wb=wb, g0=g0, ncol=ncol: e.dma_start(
                    out=wb.t[:], in_=w13[:, :, DFF + g0 * 128:DFF + g0 * 128 + ncol]), writes=[(wb, 0)], dma=True)
                if samp is not None:
                    sample_proj(samp["hsT"], wa, ncol, 6)
                    sample_proj(samp["hsT"], wb, ncol, 7)
                    sas = A.alloc("sas", [NS, 512], F32)
                    P.op("act", lambda e, sas=sas, ncol=ncol: e.activation(out=sas.t[:, 0:ncol], in_=bank(6)[0:NS, 0:ncol], func=AF.Silu),
                         reads=[(psb[6], 0)], writes=[(sas, 0)])
                    P.op("dve", lambda e, sas=sas, ncol=ncol, g0=g0: e.tensor_tensor(
                        out=samp["acts"].t[:, g0 * 128:g0 * 128 + ncol], in0=sas.t[:, 0:ncol], in1=bank(7)[0:NS, 0:ncol], op=ALU.mult),
                        reads=[(sas, 0), (psb[7], 0)], writes=[(samp["acts"], 0)])
                    A.free(sas)
                for tb in range(NTB):
                    if first_group[0] and "pre_tb" in hooks:
                        hooks["pre_tb"](tb)
                    for j in range(g0, g1):
                        pa = nev % 2
                        pbk = 2 + nev % 2
                        for kc in range(KC):
                            P.op("pe", lambda e, wa=wa, kc=kc, j=j, g0=g0, tb=tb, pa=pa: e.matmul(
                                bank(pa), wa.t[:, kc, (j - g0) * 128:(j - g0 + 1) * 128],
                                HT.t[:, kc, tb * TBS:(tb + 1) * TBS], start=(kc == 0), stop=(kc == KC - 1)),
                                reads=[(wa, 0), (HT, (kc, tb))], writes=[(psb[pa], 0)])
                        for kc in range(KC):
                            P.op("pe", lambda e, wb=wb, kc=kc, j=j, g0=g0, tb=tb, pbk=pbk: e.matmul(
                                bank(pbk), wb.t[:, kc, (j - g0) * 128:(j - g0 + 1) * 128],
                                HT.t[:, kc, tb * TBS:(tb + 1) * TBS], start=(kc == 0), stop=(kc == KC - 1)),
                                reads=[(wb, 0), (HT, (kc, tb))], writes=[(psb[pbk], 0)])
                        s = sa[nev % 2]
                        P.op("act", lambda e, s=s, pa=pa: e.activation(out=s.t[:], in_=bank(pa), func=AF.Silu),
                             reads=[(psb[pa], 0)], writes=[(s, 0)])
                        P.op("dve", lambda e, s=s, pbk=pbk, j=j, tb=tb, c_lo=c_lo, act=act: e.tensor_tensor(
                            out=act.t[:, j - c_lo, tb * TBS:(tb + 1) * TBS], in0=s.t[:], in1=bank(pbk), op=ALU.mult),
                            reads=[(s, 0), (psb[pbk], 0)], writes=[(act, (j - c_lo, tb))])
                        nev += 1
                A.free(wa)
                A.free(wb)
                first_group[0] = False
                g0 = g1
            last_split = (c_hi == NFC)
            if last_split and "before_last_down" in hooks:
                hooks["before_last_down"]()
            w2t = []
            for half in range(2):
                wt = ring.alloc("w2", [128, nch, 512], BF16)
                P.op("pool", lambda e, wt=wt, half=half, c_lo=c_lo, nch=nch: e.dma_start(
                    out=wt.t[:], in_=w2[:, c_lo:c_lo + nch, half * 512:(half + 1) * 512]), writes=[(wt, 0)], dma=True)
                w2t.append(wt)
            if samp is not None:
                aT_ = transpose_s(samp["acts"], c_lo * 128, nch, "actsT")
                for half in range(2):
                    sample_proj(aT_, w2t[half], 512, 6)
                    sample_residual(6, samp["hgs"], half * 512)
                A.free(aT_)
            nd = 0
            for tb in range(NTB):
                for d in range(KC):
                    pd = 4 + nd % 2
                    wt = w2t[d // 4]
                    for k in range(nch):
                        P.op("pe", lambda e, wt=wt, k=k, d=d, tb=tb, pd=pd, act=act, nch=nch: e.matmul(
                            bank(pd), wt.t[:, k, (d % 4) * 128:(d % 4 + 1) * 128], act.t[:, k, tb * TBS:(tb + 1) * TBS],
                            start=(k == 0), stop=(k == nch - 1)),
                            reads=[(wt, 0), (act, (k, tb))], writes=[(psb[pd], 0)])
                    P.op("dve", lambda e, d=d, tb=tb, pd=pd: e.scalar_tensor_tensor(
                        out=XT.t[:, d, tb * TBS:(tb + 1) * TBS], in0=bank(pd), scalar=gbuf.t[:, d:d + 1],
                        in1=XT.t[:, d, tb * TBS:(tb + 1) * TBS], op0=ALU.mult, op1=ALU.add),
                        reads=[(psb[pd], 0), (XT, (d, tb)), (gbuf, 0)], writes=[(XT, (d, tb))])
                    nd += 1
                    if last_split and "down_unit" in hooks:
                        hooks["down_unit"](nd)
                if last_split and "post_tb" in hooks:
                    hooks["post_tb"](tb)
            for wt in w2t:
                A.free(wt)
            A.free(act)
        for b in sa:
            A.free(b)

    def final_init():
        F = {}
        F["nfb"] = A.alloc("nfb", [128, D], F32)
        P.op("sp", lambda e: e.dma_start(out=F["nfb"].t[:], in_=normf_w[0:1, :].partition_broadcast(128)),
             writes=[(F["nfb"], 0)], dma=True)
        F["yt"] = [A.alloc(f"yt{i}", [128, D], F32) for i in range(2)]
        F["junk"] = A.alloc("junk", [128, D], BF16)
        F["st"] = A.alloc("fstat", [128, 16, 2], F32)
        P.op("dve", lambda e: e.memset(F["st"].t[:], 0.0), writes=[(F["st"], t) for t in range(16)])
        return F

    def final_tile(F, t, pb):
        nfb, junk, st = F["nfb"], F["junk"], F["st"]
        for kc in range(KC):
            P.op("pe", lambda e, kc=kc: e.transpose(
                out=PS[:, pb * 512 + kc * 128: pb * 512 + (kc + 1) * 128],
                in_=XT.t[:, kc, t * 128:(t + 1) * 128], identity=ident.t[:]),
                reads=[(XT, (kc, t // 4)), (ident, 0)], writes=[(psb[pb + kc // 4], 0)])
        src = lambda: PS[:, pb * 512:(pb + 2) * 512]
        P.op("act", lambda e: e.activation(out=junk.t[:], in_=src(), func=AF.Square, accum_out=st.t[:, t, 0:1]),
             reads=[(psb[pb], 0), (psb[pb + 1], 0)], writes=[(junk, 0), (st, t)])
        P.op("act", lambda e: e.activation(out=st.t[:, t, 1:2], in_=st.t[:, t, 0:1], func=AF.Ln, scale=1.0 / D, bias=EPS),
             reads=[(st, t)], writes=[(st, t)])
        P.op("act", lambda e: e.activation(out=st.t[:, t, 1:2], in_=st.t[:, t, 1:2], func=AF.Exp, scale=-0.5),
             reads=[(st, t)], writes=[(st, t)])
        y = F["yt"][t % 2]
        P.op("dve", lambda e: e.scalar_tensor_tensor(
            out=y.t[:], in0=src(), scalar=st.t[:, t, 1:2], in1=nfb.t[:], op0=ALU.mult, op1=ALU.mult),
            reads=[(psb[pb], 0), (psb[pb + 1], 0), (st, t), (nfb, 0)], writes=[(y, 0)])
        P.op("sp", lambda e: e.dma_start(out=y_p[t * 128:(t + 1) * 128, :], in_=y.t[:]), reads=[(y, 0)], dma=True)

    def bcast_row(name, src_row, npart=NS):
        b = A.alloc(name, [npart, D], F32)
        P.op("sp", lambda e: e.dma_start(out=b.t[:], in_=src_row.partition_broadcast(npart)), writes=[(b, 0)], dma=True)
        return b

    def transpose_s(src, c0, nchunk, name, dt=BF16):
        dst = A.alloc(name, [128, nchunk, NS], dt)
        idn = identb if dt == BF16 else ident
        for i in range(nchunk):
            if dt == BF16:
                o = lambda i=i: bank_bf(6)[:, i * 16:(i + 1) * 16]
            else:
                o = lambda i=i: bank(6)[:, i * 16:(i + 1) * 16]
            P.op("pe", lambda e, i=i, o=o: e.transpose(out=o(), in_=src.t[0:NS, c0 + i * 128: c0 + (i + 1) * 128],
                                                       identity=idn.t[0:NS, 0:NS]),
                 reads=[(src, 0), (idn, 0)], writes=[(psb[6], 0)])
        if dt == BF16:
            srcv = lambda: bank_bf(6)[:, 0:nchunk * 16].rearrange("p (k c) -> p k c", c=16)
        else:
            srcv = lambda: bank(6)[:, 0:nchunk * 16].rearrange("p (k c) -> p k c", c=16)
        P.op("dve", lambda e: e.tensor_copy(out=dst.t[:], in_=srcv()), reads=[(psb[6], 0)], writes=[(dst, 0)])
        return dst

    def sample_rstd(src_ap_fn, src_reads, n, scale_inv, name):
        ss = A.alloc(name, [NS, 2 * n], F32)
        junk = A.alloc(name + "j", [NS, D], F32)
        P.op("dve", lambda e: e.memset(ss.t[:], 0.0), writes=[(ss, 0)])
        w = D // n
        for i in range(n):
            P.op("act", lambda e, i=i: e.activation(out=junk.t[:, i * w:(i + 1) * w], in_=src_ap_fn(i * w, (i + 1) * w),
                                                    func=AF.Square, accum_out=ss.t[:, i:i + 1]),
                 reads=src_reads + [(ss, 0)], writes=[(junk, 0), (ss, 0)])
        P.op("act", lambda e: e.activation(out=ss.t[:, n:2 * n], in_=ss.t[:, 0:n], func=AF.Ln, scale=scale_inv, bias=EPS),
             reads=[(ss, 0)], writes=[(ss, 0)])
        P.op("act", lambda e: e.activation(out=ss.t[:, n:2 * n], in_=ss.t[:, n:2 * n], func=AF.Exp, scale=-0.5),
             reads=[(ss, 0)], writes=[(ss, 0)])
        A.free(junk)
        return ss

    def sample_norm(mod, nw_dram):
        nwb = bcast_row("nwb", nw_dram[0:1, :])
        ss = sample_rstd(lambda a, b: XS.t[:, a:b], [(XS, 0)], 1, 1.0 / D, "sss")
        tmpf = A.alloc("snt", [NS, D], F32)
        hs = A.alloc("hs", [NS, D], BF16)
        P.op("dve", lambda e: e.scalar_tensor_tensor(out=nwb.t[:], in0=mod.t[0:NS, D:2 * D], scalar=1.0, in1=nwb.t[:],
                                                      op0=ALU.add, op1=ALU.mult),
             reads=[(mod, 0), (nwb, 0)], writes=[(nwb, 0)])
        P.op("dve", lambda e: e.scalar_tensor_tensor(out=tmpf.t[:], in0=XS.t[:], scalar=ss.t[:, 1:2], in1=nwb.t[:],
                                                      op0=ALU.mult, op1=ALU.mult),
             reads=[(XS, 0), (ss, 0), (nwb, 0)], writes=[(tmpf, 0)])
        P.op("dve", lambda e: e.tensor_tensor(out=hs.t[:], in0=tmpf.t[:], in1=mod.t[0:NS, 0:D], op=ALU.add),
             reads=[(tmpf, 0), (mod, 0)], writes=[(hs, 0)])
        hsT = transpose_s(hs, 0, KC, "hsT")
        A.free(nwb); A.free(ss); A.free(tmpf); A.free(hs)
        return hsT

    def sample_gate(mod, name, scale):
        g = A.alloc(name, [NS, D], F32)
        P.op("dve", lambda e: e.tensor_scalar(out=g.t[:], in0=mod.t[0:NS, 2 * D:3 * D], scalar1=scale, scalar2=None, op0=ALU.mult),
             reads=[(mod, 0)], writes=[(g, 0)])
        return g

    def sample_proj(hsT, wt, ncol, pb):
        nk = hsT.t.shape[1]
        for kc in range(nk):
            P.op("pe", lambda e, kc=kc: e.matmul(bank(pb)[0:NS, 0:ncol], hsT.t[:, kc, :], wt.t[:, kc, :],
                                                 start=(kc == 0), stop=(kc == nk - 1)),
                 reads=[(hsT, 0), (wt, 0)], writes=[(psb[pb], 0)])

    def sample_residual(pb, gs, c0):
        tt = A.alloc("srt", [NS, 512], F32)
        P.op("dve", lambda e: e.tensor_tensor(out=tt.t[:], in0=bank(pb)[0:NS, :], in1=gs.t[:, c0:c0 + 512], op=ALU.mult),
             reads=[(psb[pb], 0), (gs, 0)], writes=[(tt, 0)])
        P.op("dve", lambda e: e.tensor_tensor(out=XS.t[:, c0:c0 + 512], in0=XS.t[:, c0:c0 + 512], in1=tt.t[:], op=ALU.add),
             reads=[(tt, 0), (XS, 0)], writes=[(XS, 0)])
        A.free(tt)

    def sample_token_mix(hsT, g2s, PROJT):
        walr_s = load_w(w_in3, OFF_ALR, 16, "walrs")
        wa2s = A.alloc("wa2s", [17, 512], BF16)
        P.op("pool", lambda e: e.dma_start(out=wa2s.t[:], in_=w_a2aug), writes=[(wa2s, 0)], dma=True)
        eye_m = A.alloc("eye_m", [128, NS, NS], F32)
        P.op("sp", lambda e: e.dma_start(out=eye_m.t[:].rearrange("p a b -> p (a b)"), in_=eye16_d[0:1, :].partition_broadcast(128)),
             writes=[(eye_m, 0)], dma=True)

        def tok_major(chunk0, nchunk, name, dt=F32):
            dst = A.alloc(name, [NS, nchunk * 128], dt)
            for g in range(0, nchunk, 4):
                pb = 2 + (g // 4) % 2
                for c in range(4):
                    P.op("pe", lambda e, g=g, c=c, pb=pb: e.transpose(out=bank(pb)[0:NS, c * 128:(c + 1) * 128],
                                                                    in_=PROJT.t[:, chunk0 + g + c, :], identity=ident.t[:]),
                         reads=[(PROJT, (chunk0 + g) // 4), (ident, 0)], writes=[(psb[pb], 0)])
                P.op("dve", lambda e, g=g, pb=pb: e.tensor_copy(out=dst.t[:, g * 128:(g + 4) * 128], in_=bank(pb)[0:NS, :]),
                     reads=[(psb[pb], 0)], writes=[(dst, 0)])
            return dst

        alrs = A.alloc("alrs", [17, NS], BF16)
        P.op("dve", lambda e: e.memset(alrs.t[:], 1.0), writes=[(alrs, 0)])
        for kc in range(KC):
            P.op("pe", lambda e, kc=kc: e.matmul(bank(6)[0:16, 0:NS], walr_s.t[:, kc, :], hsT.t[:, kc, :],
                                                 start=(kc == 0), stop=(kc == KC - 1)),
                 reads=[(walr_s, 0), (hsT, 0)], writes=[(psb[6], 0)])
        P.op("act", lambda e: e.activation(out=alrs.t[0:16, :], in_=bank(6)[0:16, 0:NS], func=AF.Copy),
             reads=[(psb[6], 0)], writes=[(alrs, 0)])
        dec = A.alloc("dec_s", [NS, 512], F32)
        P.op("pe", lambda e: e.matmul(bank(7)[0:NS, :], alrs.t[0:17, :], wa2s.t[0:17, :], start=True, stop=True),
             reads=[(alrs, 0), (wa2s, 0)], writes=[(psb[7], 0)])
        P.op("act", lambda e: e.activation(out=dec.t[:], in_=bank(7)[0:NS, :], func=AF.Exp, scale=-1.0),
             reads=[(psb[7], 0)], writes=[(dec, 0)])
        P.op("act", lambda e: e.activation(out=dec.t[:], in_=dec.t[:], func=AF.Ln, bias=1.0), reads=[(dec, 0)], writes=[(dec, 0)])
        P.op("act", lambda e: e.activation(out=dec.t[:], in_=dec.t[:], func=AF.Exp, scale=-1.0 / 16.0), reads=[(dec, 0)], writes=[(dec, 0)])
        A.free(walr_s); A.free(wa2s); A.free(alrs)

        ks = tok_major(4, 4, "ks")
        vs = tok_major(8, 8, "vs", BF16)
        aT = transpose_s(dec, 0, 4, "aT_s", F32)
        A.free(dec)
        qTm = A.alloc("qTm", [128, 4, NS, NS], BF16)
        for h in range(4):
            P.op("dve", lambda e, h=h: e.tensor_tensor(out=qTm.t[:, h, :, :],
                                                       in0=PROJT.t[:, h, :].unsqueeze(2).broadcast_to([128, NS, NS]),
                                                       in1=eye_m.t[:], op=ALU.mult),
                 reads=[(PROJT, 0), (eye_m, 0)], writes=[(qTm, h)])

        NB0 = 3
        S0 = [A.alloc(f"S0_{i}", [128, 4, 256], F32) for i in range(NB0)]
        S1 = [A.alloc(f"S1_{i}", [128, 4, 256], F32) for i in range(2)]
        km = [A.alloc(f"km{i}", [NS, 512], BF16) for i in range(2)]
        S1b = [A.alloc(f"S1b_{i}", [128, 4, 256], BF16) for i in range(2)]

        def load_state(b):
            s0 = S0[b % NB0]
            P.op("sp", lambda e: e.dma_start(out=s0.t[:], in_=state_s[b].rearrange("h k v -> k h v")),
                 writes=[(s0, 0)], dma=True)

        load_state(0)
        load_state(1)
        yield
        for b in range(NS):
            s0, s1, kb = S0[b % NB0], S1[b % 2], km[b % 2]
            P.op("dve", lambda e, b=b, kb=kb: e.tensor_scalar(out=kb.t[:], in0=ks.t[:], scalar1=ident.t[0:NS, b:b + 1],
                                                             scalar2=None, op0=ALU.mult),
                 reads=[(ks, 0), (ident, 0)], writes=[(kb, 0)])
            for h in range(4):
                P.op("pe", lambda e, h=h, kb=kb: e.matmul(bank(h // 2)[:, (h % 2) * 256:(h % 2) * 256 + 256], kb.t[:, h * 128:(h + 1) * 128],
                                                          vs.t[:, h * 256:(h + 1) * 256], start=True, stop=True),
                     reads=[(kb, 0), (vs, 0)], writes=[(psb[h // 2], 0)])
            for h in range(4):
                P.op("dve", lambda e, h=h, b=b, s0=s0, s1=s1: e.scalar_tensor_tensor(
                    out=s1.t[:, h, :], in0=s0.t[:, h, :], scalar=aT.t[:, h, b:b + 1], in1=bank(h // 2)[:, (h % 2) * 256:(h % 2) * 256 + 256],
                    op0=ALU.mult, op1=ALU.add),
                    reads=[(s0, 0), (aT, 0), (psb[h // 2], 0)], writes=[(s1, h)])
            s1b = S1b[b % 2]
            for h in range(4):
                P.op("act", lambda e, h=h, s1=s1, s1b=s1b: e.activation(out=s1b.t[:, h, :], in_=s1.t[:, h, :], func=AF.Copy),
                     reads=[(s1, h)], writes=[(s1b, h)])
            for h in range(4):
                ob = 4 + h
                P.op("pe", lambda e, h=h, b=b, s1b=s1b, ob=ob: e.matmul(
                    PS[0:NS, ob * 512: ob * 512 + 256], qTm.t[:, h, b, :], s1b.t[:, h, :],
                    start=(b == 0), stop=(b == NS - 1)),
                    reads=[(qTm, h), (s1b, h)], writes=[(psb[ob], 0)])
            if b + 2 < NS:
                load_state(b + 2)
            P.op("sp", lambda e, b=b, s1=s1: e.dma_start(out=S_s[b].rearrange("h k v -> k h v"), in_=s1.t[:]),
                 reads=[(s1, h) for h in range(4)], dma=True)
        for bf in S0 + S1 + km + S1b:
            A.free(bf)
        A.free(aT); A.free(qTm); A.free(eye_m); A.free(ks); A.free(vs)
        srs = tok_major(16, 8, "srs")
        tga = tok_major(24, 8, "tga")

        o_ap = lambda a, b_: PS[0:NS, (4 + a // 256) * 512: (4 + a // 256) * 512 + 256]
        rso = sample_rstd(o_ap, [(psb[4 + h], 0) for h in range(4)], 4, 1.0 / 256, "rsos")
        gwb = bcast_row("gwb", gla_nw[0:1, :])
        ogf = A.alloc("ogf", [NS, D], F32)
        ogs = A.alloc("ogs", [NS, D], BF16)
        for h in range(4):
            P.op("dve", lambda e, h=h: e.scalar_tensor_tensor(out=ogf.t[:, h * 256:(h + 1) * 256], in0=o_ap(h * 256, (h + 1) * 256),
                                                              scalar=rso.t[:, 4 + h:5 + h], in1=gwb.t[:, h * 256:(h + 1) * 256],
                                                              op0=ALU.mult, op1=ALU.mult),
                 reads=[(psb[4 + h], 0), (rso, 0), (gwb, 0)], writes=[(ogf, 0)])
        P.op("dve", lambda e: e.tensor_tensor(out=ogs.t[:], in0=ogf.t[:], in1=srs.t[:], op=ALU.mult),
             reads=[(ogf, 0), (srs, 0)], writes=[(ogs, 0)])
        ogT = transpose_s(ogs, 0, KC, "ogT")
        A.free(rso); A.free(gwb); A.free(ogf); A.free(ogs); A.free(srs)

        mrg = A.alloc("mrg", [NS, D], F32)
        for c in range(0, D, 512):
            wt = load_w(w_pa3, c, 512, "wsmp")
            sample_proj(ogT, wt, 512, 6)
            P.op("dve", lambda e, c=c: e.scalar_tensor_tensor(out=mrg.t[:, c:c + 512], in0=tga.t[:, c:c + 512], scalar=1.0,
                                                              in1=bank(6)[0:NS, :], op0=ALU.add, op1=ALU.mult),
                 reads=[(tga, 0), (psb[6], 0)], writes=[(mrg, 0)])
            A.free(wt)
        A.free(ogT); A.free(tga)

        tgb = tok_major(32, 8, "tgb")
        us = tok_major(40, 8, "us")
        gg = tok_major(48, 8, "ggs")
        st6 = A.alloc("sbn", [NS, 2, 6], F32)
        mvs = A.alloc("smv", [NS, 4], F32)
        for c in range(2):
            P.op("dve", lambda e, c=c: e.bn_stats(out=st6.t[:, c, :], in_=gg.t[:, c * 512:(c + 1) * 512]),
                 reads=[(gg, 0)], writes=[(st6, c)])
        P.op("dve", lambda e: e.bn_aggr(out=mvs.t[:, 0:2], in_=st6.t[:]), reads=[(st6, 0), (st6, 1)], writes=[(mvs, 0)])
        P.op("act", lambda e: e.activation(out=mvs.t[:, 2:3], in_=mvs.t[:, 1:2], func=AF.Ln, bias=EPS), reads=[(mvs, 0)], writes=[(mvs, 0)])
        P.op("act", lambda e: e.activation(out=mvs.t[:, 2:3], in_=mvs.t[:, 2:3], func=AF.Exp, scale=-0.5), reads=[(mvs, 0)], writes=[(mvs, 0)])
        lwb = bcast_row("lwb", gm_lnw[0:1, :])
        lbb = bcast_row("lbb", gm_lnb[0:1, :])
        wsb = bcast_row("wsb", ws00_d[0:1, :])
        bsb = bcast_row("bsb", bs0_d[0:1, :])
        P.op("dve", lambda e: e.tensor_scalar(out=gg.t[:], in0=gg.t[:], scalar1=mvs.t[:, 0:1], scalar2=mvs.t[:, 2:3],
                                              op0=ALU.subtract, op1=ALU.mult),
             reads=[(gg, 0), (mvs, 0)], writes=[(gg, 0)])
        P.op("dve", lambda e: e.tensor_tensor(out=gg.t[:], in0=gg.t[:], in1=lwb.t[:], op=ALU.mult), reads=[(gg, 0), (lwb, 0)], writes=[(gg, 0)])
        P.op("dve", lambda e: e.tensor_tensor(out=gg.t[:], in0=gg.t[:], in1=lbb.t[:], op=ALU.add), reads=[(gg, 0), (lbb, 0)], writes=[(gg, 0)])
        P.op("sp", lambda e: e.dma_start(out=gv_s, in_=gg.t[:]), reads=[(gg, 0)], dma=True)
        sgf = A.alloc("sgf", [NS, D], F32)
        sgs = A.alloc("sgs", [NS, D], BF16)
        P.op("dve", lambda e: e.tensor_tensor(out=sgf.t[:], in0=gg.t[:], in1=wsb.t[:], op=ALU.mult), reads=[(gg, 0), (wsb, 0)], writes=[(sgf, 0)])
        P.op("dve", lambda e: e.tensor_tensor(out=sgf.t[:], in0=sgf.t[:], in1=bsb.t[:], op=ALU.add), reads=[(sgf, 0), (bsb, 0)], writes=[(sgf, 0)])
        P.op("dve", lambda e: e.tensor_tensor(out=sgs.t[:], in0=sgf.t[:], in1=us.t[:], op=ALU.mult), reads=[(sgf, 0), (us, 0)], writes=[(sgs, 0)])
        sgT = transpose_s(sgs, 0, KC, "sgT")
        for bf in (us, gg, st6, mvs, lwb, lbb, wsb, bsb, sgf, sgs):
            A.free(bf)
        mrb = A.alloc("mrb", [NS, D], BF16)
        tmm = A.alloc("tmm", [NS, 512], F32)
        for c in range(0, D, 512):
            wt = load_w(w_pb3, c, 512, "wsmp")
            sample_proj(sgT, wt, 512, 6)
            P.op("dve", lambda e, c=c: e.scalar_tensor_tensor(out=tmm.t[:], in0=tgb.t[:, c:c + 512], scalar=1.0,
                                                              in1=bank(6)[0:NS, :], op0=ALU.add, op1=ALU.mult),
                 reads=[(tgb, 0), (psb[6], 0)], writes=[(tmm, 0)])
            P.op("dve", lambda e, c=c: e.tensor_tensor(out=mrb.t[:, c:c + 512], in0=mrg.t[:, c:c + 512], in1=tmm.t[:], op=ALU.add),
                 reads=[(mrg, 0), (tmm, 0)], writes=[(mrb, 0)])
            A.free(wt)
        mT = transpose_s(mrb, 0, KC, "mT_s")
        A.free(sgT); A.free(tgb); A.free(mrg); A.free(mrb); A.free(tmm)
        for c in range(0, D, 512):
            wt = load_w(w_o3, c, 512, "wsmp")
            sample_proj(mT, wt, 512, 6)
            sample_residual(6, g2s, c)
            A.free(wt)
        A.free(mT); A.free(g2s); A.free(hsT)

    def final_sample():
        nfs = bcast_row("nfs", normf_w[0:1, :])
        ss = sample_rstd(lambda a, b: XS.t[:, a:b], [(XS, 0)], 1, 1.0 / D, "fss")
        ys = A.alloc("ys", [NS, D], F32)
        P.op("dve", lambda e: e.scalar_tensor_tensor(out=ys.t[:], in0=XS.t[:], scalar=ss.t[:, 1:2], in1=nfs.t[:],
                                                      op0=ALU.mult, op1=ALU.mult),
             reads=[(XS, 0), (ss, 0), (nfs, 0)], writes=[(ys, 0)])
        P.op("sp", lambda e: e.dma_start(out=y_s, in_=ys.t[:]), reads=[(ys, 0)], dma=True)

    w_in3 = w_in.rearrange("(k p) n -> p k n", p=128)
    w_pa3 = w_pa.rearrange("(k p) n -> p k n", p=128)
    w_pb3 = w_pb.rearrange("(k p) n -> p k n", p=128)
    w_o3 = w_o.rearrange("(k p) n -> p k n", p=128)
    OFF_Q, OFF_K, OFF_V, OFF_R, OFF_ALR, OFF_U, OFF_GV, OFF_GA, OFF_GB = 0, 512, 1024, 2048, 3072, 3088, 4112, 5136, 6160
    LN_QSCALE = float(np.log(128.0 ** -0.5))

    NSCR = 24
    wscr = nc.dram_tensor("wscr", [NSCR, 128, KC, 512], BF16).ap()
    SCR = Buf("wscr")
    WCACHE = {}

    def load_w(src3, c0, ncol, name="w", nk=KC):
        wt = ring.alloc(name, [128, nk, ncol], BF16)
        key = (id(src3), c0)
        if ncol == 512 and nk == KC and key in WCACHE:
            tid = WCACHE[key]
            P.op("sp", lambda e: e.dma_start(out=wt.t[:], in_=wscr[tid]), reads=[(SCR, tid)], writes=[(wt, 0)], dma=True)
            return wt
        P.op("pool", lambda e: e.dma_start(out=wt.t[:], in_=src3[:, :, c0:c0 + ncol]), writes=[(wt, 0)], dma=True)
        if ncol == 512 and nk == KC and len(WCACHE) < NSCR:
            tid = len(WCACHE)
            WCACHE[key] = tid
            P.op("sp", lambda e: e.dma_start(out=wscr[tid], in_=wt.t[:]), reads=[(wt, 0)], writes=[(SCR, tid)], dma=True)
        return wt

    SHOOK = {}

    def sample_hook(wt, chunk0, func, fscale, pbk):
        if not SHOOK:
            return
        hsT, PROJT = SHOOK["hsT"], SHOOK["PROJT"]
        for c in range(4):
            for kc in range(KC):
                P.op("pe", lambda e, c=c, kc=kc: e.matmul(bank(pbk)[:, c * 16:(c + 1) * 16], wt.t[:, kc, c * 128:(c + 1) * 128],
                                                          hsT.t[:, kc, :], start=(kc == 0), stop=(kc == KC - 1)),
                     reads=[(wt, 0), (hsT, 0)], writes=[(psb[pbk], 0)])
        P.op("act", lambda e: e.activation(out=PROJT.t[:, chunk0:chunk0 + 4, :],
                                           in_=bank(pbk)[:, 0:64].rearrange("p (c t) -> p c t", t=16), func=func, scale=fscale),
             reads=[(psb[pbk], 0)], writes=[(PROJT, chunk0 // 4)])

    def tm_consts(modT2):
        C = {}
        tri = A.alloc("tri", [128, 128], F32)
        tri64 = A.alloc("tri64", [128, 128], F32)
        ucum = A.alloc("ucum", [128, 128], F32)
        onesf = A.alloc("onesf", [128, 128], F32)
        wsTf = A.alloc("wsTf", [128, 4, 128], F32)
        wsTb = A.alloc("wsTb", [128, 4, 128], BF16)
        BSb = A.alloc("BSb", [128, 512], F32)
        Bblk = A.alloc("Bblk", [128, 8, 128], F32)
        walr = A.alloc("walr", [128, KC, 16], BF16)
        wa2 = A.alloc("wa2", [17, 512], BF16)
        alrT = A.alloc("alrT", [17, TBS], BF16)
        elast = A.alloc("elast", [128, 4, 32], F32)
        S32 = A.alloc("S32", [128, 4, 256], F32)
        Sbf = A.alloc("Sbf", [128, 4, 256], BF16)
        for (b, d_) in ((tri, tri_d), (tri64, tri64_d), (ucum, ucum_d), (wsTf, wsT_d)):
            P.op("sp", lambda e, b=b, d_=d_: e.dma_start(out=b.t[:], in_=d_), writes=[(b, 0)], dma=True)
        P.op("sp", lambda e: e.dma_start(out=BSb.t[:], in_=bs_d[0:1, :].partition_broadcast(128)), writes=[(BSb, 0)], dma=True)
        P.op("pool", lambda e: e.dma_start(out=walr.t[:], in_=w_in3[:, :, OFF_ALR:OFF_ALR + 16]), writes=[(walr, 0)], dma=True)
        P.op("pool", lambda e: e.dma_start(out=wa2.t[:], in_=w_a2aug), writes=[(wa2, 0)], dma=True)
        P.op("dve", lambda e: e.memset(onesf.t[:], 1.0), writes=[(onesf, 0)])
        P.op("dve", lambda e: e.memset(alrT.t[:], 1.0), writes=[(alrT, 0)])
        P.op("dve", lambda e: e.memset(S32.t[:], 0.0), writes=[(S32, h) for h in range(4)])
        P.op("dve", lambda e: e.memset(Sbf.t[:], 0.0), writes=[(Sbf, h) for h in range(4)])
        P.op("dve", lambda e: e.tensor_tensor(out=wsTf.t[:], in0=wsTf.t[:],
                                              in1=tri.t[:].unsqueeze(1).broadcast_to([128, 4, 128]), op=ALU.mult),
             reads=[(wsTf, 0), (tri, 0)], writes=[(wsTf, 0)])
        P.op("dve", lambda e: e.tensor_copy(out=wsTb.t[:], in_=wsTf.t[:]), reads=[(wsTf, 0)], writes=[(wsTb, 0)])
        P.op("pe", lambda e: e.matmul(bank(7), onesf.t[:], wsTf.t[:].rearrange("p g t -> p (g t)"), start=True, stop=True),
             reads=[(onesf, 0), (wsTf, 0)], writes=[(psb[7], 0)])
        for cb in range(8):
            g = cb // 2
            P.op("dve", lambda e, cb=cb, g=g: e.scalar_tensor_tensor(
                out=Bblk.t[:, cb, :], in0=bank(7)[:, g * 128:(g + 1) * 128], scalar=modT2.t[:, cb, 18:19],
                in1=BSb.t[:, g * 128:(g + 1) * 128], op0=ALU.mult, op1=ALU.add),
                reads=[(psb[7], 0), (modT2, 0), (BSb, 0)], writes=[(Bblk, cb)])
        A.free(tri); A.free(onesf); A.free(wsTf); A.free(BSb)
        C.update(tri64=tri64, ucum=ucum, wsTb=wsTb, Bblk=Bblk, walr=walr, wa2=wa2, alrT=alrT, elast=elast, S32=S32, Sbf=Sbf)
        return C

    def proj_fm(wt, col0, rhs, rhs_ap, rkeys, pb):
        for kc in range(KC):
            P.op("pe", lambda e, kc=kc: e.matmul(bank(pb), wt.t[:, kc, col0:col0 + 128], rhs_ap(kc),
                                                 start=(kc == 0), stop=(kc == KC - 1)),
                 reads=[(wt, 0), (rhs, rkeys(kc))], writes=[(psb[pb], 0)])

    def token_mix_block(blk, C, modT2, cols2, g2h, HTB, pre_out_hook=None, unit_hook=None):
        tri64, ucum, wsTb, Bblk = C["tri64"], C["ucum"], C["wsTb"], C["Bblk"]
        walr, wa2, alrT, elast, S32, Sbf = C["walr"], C["wa2"], C["alrT"], C["elast"], C["S32"], C["Sbf"]
        hap = lambda kc: HTB.t[:, kc, :]
        hkey = lambda kc: (kc, blk)

        sp = A.alloc("sp_tok", [128, 4, 512], F32)
        QT = A.alloc("QT", [128, 4, TBS], BF16)
        KT = A.alloc("KT", [128, 4, TBS], BF16)
        Eq = [A.alloc(f"Eq{i}", [128, TBS], F32) for i in range(2)]
        Ek = [A.alloc(f"Ek{i}", [128, TBS], F32) for i in range(2)]
        KTOK = A.alloc("KTOK", [128, 4, 512], BF16)

        def d_gen():
            for kc in range(KC):
                P.op("pe", lambda e, kc=kc: e.matmul(bank(6)[0:16, :], walr.t[:, kc, :], HTB.t[:, kc, :],
                                                     start=(kc == 0), stop=(kc == KC - 1)),
                     reads=[(walr, 0), (HTB, (kc, blk))], writes=[(psb[6], 0)])
            P.op("act", lambda e: e.activation(out=alrT.t[0:16, :], in_=bank(6)[0:16, :], func=AF.Copy),
                 reads=[(psb[6], 0)], writes=[(alrT, 0)])
            yield
            for j in range(4):
                pb = j % 2
                P.op("pe", lambda e, j=j, pb=pb: e.matmul(bank(pb), alrT.t[0:17, j * 128:(j + 1) * 128], wa2.t[0:17, :],
                                                          start=True, stop=True),
                     reads=[(alrT, 0), (wa2, 0)], writes=[(psb[pb], 0)])
                P.op("act", lambda e, j=j, pb=pb: e.activation(out=sp.t[:, j, :], in_=bank(pb), func=AF.Exp, scale=-1.0),
                     reads=[(psb[pb], 0)], writes=[(sp, j)])
                P.op("act", lambda e, j=j: e.activation(out=sp.t[:, j, :], in_=sp.t[:, j, :], func=AF.Ln, bias=1.0),
                     reads=[(sp, j)], writes=[(sp, j)])
                yield

            wq = load_w(w_in3, OFF_Q, 512, "wq")
            wk = load_w(w_in3, OFF_K, 512, "wk")
            if blk == NTB - 1:
                sample_hook(wq, 0, AF.Identity, 128.0 ** -0.5, 7)
                sample_hook(wk, 4, AF.Identity, 1.0, 7)
            for h in range(4):
                pbc = 2 + h % 2
                for j in range(4):
                    P.op("pe", lambda e, j=j, h=h, pbc=pbc: e.matmul(bank(pbc)[:, j * 128:(j + 1) * 128],
                                                                    sp.t[:, j, h * 128:(h + 1) * 128], ucum.t[:],
                                                                    start=True, stop=True),
                         reads=[(sp, j), (ucum, 0)], writes=[(psb[pbc], 0)])
                eq, ek = Eq[h % 2], Ek[h % 2]
                P.op("act", lambda e, eq=eq, pbc=pbc: e.activation(out=eq.t[:], in_=bank(pbc), func=AF.Exp, bias=LN_QSCALE),
                     reads=[(psb[pbc], 0)], writes=[(eq, 0)])
                P.op("act", lambda e, ek=ek, pbc=pbc: e.activation(out=ek.t[:], in_=bank(pbc), func=AF.Exp, scale=-1.0),
                     reads=[(psb[pbc], 0)], writes=[(ek, 0)])
                P.op("act", lambda e, h=h, pbc=pbc: e.activation(
                    out=elast.t[:, h, blk * 8:(blk + 1) * 8],
                    in_=bank(pbc).rearrange("p (c t) -> p c t", t=64)[:, :, 63], func=AF.Exp),
                    reads=[(psb[pbc], 0)], writes=[(elast, (h, blk))])
                yield
                proj_fm(wq, h * 128, HTB, hap, hkey, 0)
                P.op("dve", lambda e, h=h, eq=eq: e.tensor_tensor(out=QT.t[:, h, :], in0=bank(0), in1=eq.t[:], op=ALU.mult),
                     reads=[(psb[0], 0), (eq, 0)], writes=[(QT, h)])
                yield
                proj_fm(wk, h * 128, HTB, hap, hkey, 1)
                P.op("dve", lambda e, h=h, ek=ek: e.tensor_tensor(out=KT.t[:, h, :], in0=bank(1), in1=ek.t[:], op=ALU.mult),
                     reads=[(psb[1], 0), (ek, 0)], writes=[(KT, h)])
                yield
            A.free(wq); A.free(wk); A.free(sp)
            for b in Eq + Ek:
                A.free(b)

            for j in range(4):
                for h in range(4):
                    P.op("pe", lambda e, j=j, h=h: e.transpose(out=bank_bf(3)[:, h * 128:(h + 1) * 128],
                                                               in_=KT.t[:, h, j * 128:(j + 1) * 128], identity=identb.t[:]),
                         reads=[(KT, h), (identb, 0)], writes=[(psb[3], 0)])
                P.op("dve", lambda e, j=j: e.tensor_copy(out=KTOK.t[:, j, :], in_=bank_bf(3)[:, 0:512]),
                     reads=[(psb[3], 0)], writes=[(KTOK, j)])
                yield


        VTOK = A.alloc("VTOK", [128, 4, 1024], BF16)

        def v_gen():
            n = 0
            for cb in range(2):
                wvt = load_w(w_in3, OFF_V + cb * 512, 512, "wv")
                if blk == NTB - 1:
                    sample_hook(wvt, 8 + cb * 4, AF.Identity, 1.0, 7)
                for j in range(4):
                    pb = 4 + n % 2
                    for kc in range(KC):
                        P.op("pe", lambda e, j=j, kc=kc, pb=pb, wvt=wvt: e.matmul(
                            bank(pb), HTB.t[:, kc, j * 128:(j + 1) * 128], wvt.t[:, kc, :],
                            start=(kc == 0), stop=(kc == KC - 1)),
                            reads=[(HTB, (kc, blk)), (wvt, 0)], writes=[(psb[pb], 0)])
                    if n % 2 == 0:
                        P.op("act", lambda e, j=j, cb=cb, pb=pb: e.activation(out=VTOK.t[:, j, cb * 512:(cb + 1) * 512],
                                                                             in_=bank(pb), func=AF.Copy),
                             reads=[(psb[pb], 0)], writes=[(VTOK, (j, cb))])
                    else:
                        P.op("dve", lambda e, j=j, cb=cb, pb=pb: e.tensor_copy(out=VTOK.t[:, j, cb * 512:(cb + 1) * 512],
                                                                              in_=bank(pb)),
                             reads=[(psb[pb], 0)], writes=[(VTOK, (j, cb))])
                    n += 1
                    yield
                A.free(wvt)

        vg = v_gen()
        for _ in d_gen():
            next(vg, None)
        for _ in vg:
            pass

        O32 = A.alloc("O32", [128, 8, TBS], F32)
        sTb = [A.alloc(f"sT{h}", [128, 128], BF16) for h in range(4)]
        tmpS = [A.alloc(f"tmpS{h}", [128, 256], F32) for h in range(4)]
        def scan_gen():
            sc_ps = lambda h: PS[:, h * 512 + 256: h * 512 + 384]
            o_ps = lambda h, vb, c: PS[:, h * 512 + vb * 128 + c * 64: h * 512 + vb * 128 + c * 64 + 64]
            dS = lambda h: PS[:, (4 + h // 2) * 512 + (h % 2) * 256: (4 + h // 2) * 512 + (h % 2) * 256 + 256]
            for j in range(4):
                for h in range(4):
                    P.op("pe", lambda e, j=j, h=h: e.matmul(sc_ps(h), KT.t[:, h, j * 128:(j + 1) * 128],
                                                            QT.t[:, h, j * 128:(j + 1) * 128], start=True, stop=True),
                         reads=[(KT, h), (QT, h)], writes=[(psb[h], 0)])
                for h in range(4):
                    P.op("dve", lambda e, h=h: e.tensor_tensor(out=sTb[h].t[:], in0=sc_ps(h), in1=tri64.t[:], op=ALU.mult),
                         reads=[(psb[h], 0), (tri64, 0)], writes=[(sTb[h], 0)])
                yield
                for c in range(2):
                    chunk = blk * 8 + j * 2 + c
                    for h in range(4):
                        for vb in range(2):
                            P.op("pe", lambda e, j=j, h=h, vb=vb, c=c: e.matmul(
                                o_ps(h, vb, c), VTOK.t[:, j, h * 256 + vb * 128: h * 256 + (vb + 1) * 128],
                                sTb[h].t[:, c * 64:(c + 1) * 64], start=True, stop=False),
                                reads=[(VTOK, (j, h // 2)), (sTb[h], 0)], writes=[(psb[h], 0)])
                            P.op("pe", lambda e, j=j, h=h, vb=vb, c=c: e.matmul(
                                o_ps(h, vb, c), Sbf.t[:, h, vb * 128:(vb + 1) * 128],
                                QT.t[:, h, j * 128 + c * 64: j * 128 + (c + 1) * 64], start=False, stop=True),
                                reads=[(Sbf, h), (QT, h)], writes=[(psb[h], 0)])
                        P.op("pe", lambda e, j=j, h=h, c=c: e.matmul(
                            dS(h), KTOK.t[c * 64:(c + 1) * 64, j, h * 128:(h + 1) * 128],
                            VTOK.t[c * 64:(c + 1) * 64, j, h * 256:(h + 1) * 256], start=True, stop=True),
                            reads=[(KTOK, j), (VTOK, (j, h // 2))], writes=[(psb[4 + h // 2], 0)])
                    for h in range(4):
                        ts = tmpS[h]
                        P.op("dve", lambda e, ts=ts, h=h: e.tensor_tensor(out=ts.t[:], in0=dS(h), in1=S32.t[:, h, :], op=ALU.add),
                             reads=[(psb[4 + h // 2], 0), (S32, h)], writes=[(ts, 0)])
                        P.op("act", lambda e, ts=ts, h=h, chunk=chunk: e.activation(
                            out=Sbf.t[:, h, :], in_=ts.t[:], func=AF.Identity, scale=elast.t[:, h, chunk:chunk + 1]),
                            reads=[(ts, 0), (elast, (h, blk))], writes=[(Sbf, h)])
                        P.op("act", lambda e, ts=ts, h=h, chunk=chunk: e.activation(
                            out=S32.t[:, h, :], in_=ts.t[:], func=AF.Identity, scale=elast.t[:, h, chunk:chunk + 1]),
                            reads=[(ts, 0), (elast, (h, blk))], writes=[(S32, h)])
                    yield
                for h in range(4):
                    P.op("act", lambda e, j=j, h=h: e.activation(
                        out=O32.t[:, h * 2:(h + 1) * 2, j * 128:(j + 1) * 128],
                        in_=PS[:, h * 512: h * 512 + 256].rearrange("p (v t) -> p v t", t=128), func=AF.Copy),
                        reads=[(psb[h], 0)], writes=[(O32, (h, j))])
                yield
        SR = A.alloc("SR", [128, 8, TBS], BF16)
        TGA = A.alloc("TGA", [128, 8, TBS], BF16)
        UT = A.alloc("UT", [128, 8, TBS], BF16)

        def filler_gen():
            for (off, dst, func, fscale) in ((OFF_R, SR, AF.Silu, 1.0), (OFF_GA, TGA, AF.Tanh, 0.5), (OFF_U, UT, AF.Gelu, 1.0)):
                for half in range(2):
                    wt = load_w(w_in3, off + half * 512, 512, "wfill")
                    if blk == NTB - 1:
                        sample_hook(wt, {OFF_R: 16, OFF_GA: 24, OFF_U: 40}[off] + half * 4, func, fscale, 6 + half)
                    for dd in range(4):
                        d = half * 4 + dd
                        pb = 6 + d % 2
                        proj_fm(wt, dd * 128, HTB, hap, hkey, pb)
                        P.op("act", lambda e, d=d, pb=pb, dst=dst, func=func, fscale=fscale: e.activation(
                            out=dst.t[:, d, :], in_=bank(pb), func=func, scale=fscale),
                            reads=[(psb[pb], 0)], writes=[(dst, d)])
                        yield
                    A.free(wt)

        fg = filler_gen()
        nstep = 0
        for _ in scan_gen():
            nstep += 1
            for _k in range(2 if nstep % 2 == 1 else 1):
                next(fg, None)
        for _ in fg:
            pass
        for b in sTb + tmpS:
            A.free(b)
        A.free(QT); A.free(KT); A.free(KTOK); A.free(VTOK)

        sqo = A.alloc("sqo", [128, 2, TBS], BF16)
        rso = [A.alloc(f"rso{h}", [128, TBS], F32) for h in range(4)]
        for h in range(4):
            for vb in range(2):
                P.op("dve", lambda e, h=h, vb=vb: e.tensor_tensor(out=sqo.t[:, vb, :], in0=O32.t[:, h * 2 + vb, :],
                                                                 in1=O32.t[:, h * 2 + vb, :], op=ALU.mult),
                     reads=[(O32, (h, j)) for j in range(4)], writes=[(sqo, vb)])
            for vb in range(2):
                P.op("pe", lambda e, vb=vb: e.matmul(bank(7), onesb.t[:], sqo.t[:, vb, :], start=(vb == 0), stop=(vb == 1)),
                     reads=[(onesb, 0), (sqo, vb)], writes=[(psb[7], 0)])
            P.op("act", lambda e, h=h: e.activation(out=rso[h].t[:], in_=bank(7), func=AF.Ln, scale=1.0 / 256, bias=EPS),
                 reads=[(psb[7], 0)], writes=[(rso[h], 0)])
            P.op("act", lambda e, h=h: e.activation(out=rso[h].t[:], in_=rso[h].t[:], func=AF.Exp, scale=-0.5),
                 reads=[(rso[h], 0)], writes=[(rso[h], 0)])
        A.free(sqo)
        TGB = A.alloc("TGB", [128, 8, TBS], BF16)
        for half in range(2):
            wt = load_w(w_in3, OFF_GB + half * 512, 512, "wgb")
            if blk == NTB - 1:
                sample_hook(wt, 32 + half * 4, AF.Tanh, 0.5, 7)
            for dd in range(4):
                d = half * 4 + dd
                pb = d % 2
                proj_fm(wt, dd * 128, HTB, hap, hkey, pb)
                P.op("act", lambda e, d=d, pb=pb: e.activation(out=TGB.t[:, d, :], in_=bank(pb), func=AF.Tanh, scale=0.5),
                     reads=[(psb[pb], 0)], writes=[(TGB, d)])
            A.free(wt)
        OG = SR
        t1 = [A.alloc(f"t1{i}", [128, TBS], F32) for i in range(2)]
        for d in range(8):
            h = d // 2
            pb = d % 2
            P.op("dve", lambda e, d=d, h=h, pb=pb: e.scalar_tensor_tensor(
                out=t1[pb].t[:], in0=O32.t[:, d, :], scalar=modT2.t[:, 8 + d, 17:18], in1=rso[h].t[:],
                op0=ALU.mult, op1=ALU.mult),
                reads=[(O32, (h, j)) for j in range(4)] + [(modT2, 0), (rso[h], 0)], writes=[(t1[pb], 0)])
            P.op("dve", lambda e, d=d, pb=pb: e.tensor_tensor(out=OG.t[:, d, :], in0=t1[pb].t[:], in1=SR.t[:, d, :], op=ALU.mult),
                 reads=[(t1[pb], 0), (SR, d)], writes=[(OG, d)])
        A.free(O32)
        for b in rso:
            A.free(b)

        MA = A.alloc("MA", [128, 8, TBS], BF16)
        tg = [A.alloc(f"tg{i}", [128, TBS], F32) for i in range(2)]

        def gated_proj(w3, src, skeys, goff, accumulate, pre=None, unit_hook=None):
            for half in range(2):
                wp = load_w(w3, half * 512, 512, "wp")
                wg = load_w(w_in3, goff + half * 512, 512, "wg") if pre is None else None
                for dd in range(4):
                    d = half * 4 + dd
                    pb = d % 2
                    proj_fm(wp, dd * 128, src, lambda kc: src.t[:, kc, :], skeys, pb)
                    if pre is None:
                        proj_fm(wg, dd * 128, HTB, hap, hkey, 2 + pb)
                        P.op("act", lambda e, pb=pb: e.activation(out=tg[pb].t[:], in_=bank(2 + pb), func=AF.Tanh, scale=0.5),
                             reads=[(psb[2 + pb], 0)], writes=[(tg[pb], 0)])
                    if pre is not None and accumulate:
                        P.op("dve", lambda e, d=d, pb=pb: e.scalar_tensor_tensor(
                            out=t1[pb].t[:], in0=pre.t[:, d, :], scalar=1.0, in1=bank(pb), op0=ALU.add, op1=ALU.mult),
                            reads=[(pre, d), (psb[pb], 0)], writes=[(t1[pb], 0)])
                        P.op("dve", lambda e, d=d, pb=pb: e.tensor_tensor(out=MA.t[:, d, :], in0=MA.t[:, d, :], in1=t1[pb].t[:], op=ALU.add),
                             reads=[(t1[pb], 0), (MA, d)], writes=[(MA, d)])
                    elif pre is not None:
                        P.op("dve", lambda e, d=d, pb=pb: e.scalar_tensor_tensor(
                            out=MA.t[:, d, :], in0=pre.t[:, d, :], scalar=1.0, in1=bank(pb), op0=ALU.add, op1=ALU.mult),
                            reads=[(pre, d), (psb[pb], 0)], writes=[(MA, d)])
                    elif not accumulate:
                        P.op("dve", lambda e, d=d, pb=pb: e.scalar_tensor_tensor(
                            out=MA.t[:, d, :], in0=tg[pb].t[:], scalar=1.0, in1=bank(pb), op0=ALU.add, op1=ALU.mult),
                            reads=[(tg[pb], 0), (psb[pb], 0)], writes=[(MA, d)])
                    else:
                        P.op("dve", lambda e, d=d, pb=pb: e.scalar_tensor_tensor(
                            out=t1[pb].t[:], in0=tg[pb].t[:], scalar=1.0, in1=bank(pb), op0=ALU.add, op1=ALU.mult),
                            reads=[(tg[pb], 0), (psb[pb], 0)], writes=[(t1[pb], 0)])
                        P.op("dve", lambda e, d=d, pb=pb: e.tensor_tensor(out=MA.t[:, d, :], in0=MA.t[:, d, :], in1=t1[pb].t[:], op=ALU.add),
                             reads=[(t1[pb], 0), (MA, d)], writes=[(MA, d)])
                    if accumulate and unit_hook is not None:
                        unit_hook()
                A.free(wp)
                if wg is not None:
                    A.free(wg)

        gated_proj(w_pa3, OG, lambda kc: kc, OFF_GA, False, pre=TGA)
        A.free(OG); A.free(TGA)

        GG = A.alloc("GG", [128, 4, 1024], F32)
        ZT = A.alloc("ZT", [128, 4, 1024], BF16)
        stats = A.alloc("bnst", [128, 4, 2, 6], F32)
        mv = A.alloc("bnmv", [128, 4, 2], F32)
        rs = A.alloc("bnrs", [128, 4], F32)
        wgv = [load_w(w_in3, OFF_GV + cb * 512, 512, "wgv") for cb in range(2)]
        if blk == NTB - 1:
            for cb in range(2):
                sample_hook(wgv[cb], 48 + cb * 4, AF.Gelu, 1.0, 7)
        n = 0
        for j in range(4):
            for cb in range(2):
                pb = n % 2
                for kc in range(KC):
                    P.op("pe", lambda e, j=j, cb=cb, kc=kc, pb=pb: e.matmul(
                        bank(pb), HTB.t[:, kc, j * 128:(j + 1) * 128], wgv[cb].t[:, kc, :],
                        start=(kc == 0), stop=(kc == KC - 1)),
                        reads=[(HTB, (kc, blk)), (wgv[cb], 0)], writes=[(psb[pb], 0)])
                P.op("act", lambda e, j=j, cb=cb, pb=pb: e.activation(out=GG.t[:, j, cb * 512:(cb + 1) * 512], in_=bank(pb), func=AF.Gelu),
                     reads=[(psb[pb], 0)], writes=[(GG, (j, cb))])
                P.op("dve", lambda e, j=j, cb=cb: e.bn_stats(out=stats.t[:, j, cb, :], in_=GG.t[:, j, cb * 512:(cb + 1) * 512]),
                     reads=[(GG, (j, cb))], writes=[(stats, (j, cb))])
                n += 1
            P.op("dve", lambda e, j=j: e.bn_aggr(out=mv.t[:, j, :], in_=stats.t[:, j, :, :]),
                 reads=[(stats, (j, 0)), (stats, (j, 1))], writes=[(mv, j)])
        for b in wgv:
            A.free(b)
        P.op("act", lambda e: e.activation(out=rs.t[:], in_=mv.t[:, :, 1], func=AF.Ln, bias=EPS),
             reads=[(mv, j) for j in range(4)], writes=[(rs, 0)])
        P.op("act", lambda e: e.activation(out=rs.t[:], in_=rs.t[:], func=AF.Exp, scale=-0.5),
             reads=[(rs, 0)], writes=[(rs, 0)])
        nb_ = A.alloc("bnnb", [128, 4], F32)
        P.op("dve", lambda e: e.scalar_tensor_tensor(out=nb_.t[:], in0=mv.t[:, :, 0], scalar=-1.0, in1=rs.t[:],
                                                      op0=ALU.mult, op1=ALU.mult),
             reads=[(mv, j) for j in range(4)] + [(rs, 0)], writes=[(nb_, 0)])
        for j in range(4):
            P.op("act", lambda e, j=j: e.activation(out=ZT.t[:, j, :], in_=GG.t[:, j, :], func=AF.Identity,
                                                    scale=rs.t[:, j:j + 1], bias=nb_.t[:, j:j + 1]),
                 reads=[(GG, (j, 0)), (GG, (j, 1)), (nb_, 0), (rs, 0)], writes=[(ZT, j)])
        A.free(nb_)
        A.free(GG); A.free(stats)
        for cb in range(8):
            g = cb // 2
            pb = cb % 2
            for j in range(4):
                P.op("pe", lambda e, j=j, cb=cb, g=g, pb=pb: e.matmul(bank(pb)[:, j * 128:(j + 1) * 128],
                                                                   ZT.t[:, j, cb * 128:(cb + 1) * 128], wsTb.t[:, g, :],
                                                                   start=True, stop=True),
                     reads=[(ZT, j), (wsTb, 0)], writes=[(psb[pb], 0)])
            P.op("dve", lambda e, cb=cb, pb=pb: e.scalar_tensor_tensor(
                out=t1[pb].t[:].rearrange("p (j t) -> p j t", t=128), in0=bank(pb).rearrange("p (j t) -> p j t", t=128),
                scalar=modT2.t[:, 16 + cb, 17:18], in1=Bblk.t[:, cb, :].unsqueeze(1).broadcast_to([128, 4, 128]),
                op0=ALU.mult, op1=ALU.add),
                reads=[(psb[pb], 0), (modT2, 0), (Bblk, cb)], writes=[(t1[pb], 0)])
            P.op("dve", lambda e, cb=cb, pb=pb: e.tensor_tensor(out=UT.t[:, cb, :], in0=t1[pb].t[:], in1=UT.t[:, cb, :], op=ALU.mult),
                 reads=[(t1[pb], 0), (UT, cb)], writes=[(UT, cb)])
        A.free(ZT); A.free(mv); A.free(rs)

        gated_proj(w_pb3, UT, lambda kc: kc, OFF_GB, True, pre=TGB, unit_hook=unit_hook)
        A.free(UT); A.free(TGB)

        if pre_out_hook is not None:
            pre_out_hook()
        HTBn = None
        if blk + 1 < NTB:
            HTBn = A.alloc("HTB", [128, KC, TBS], BF16)
            tmpn = norm_tmp()
            norm_block(blk + 1, cols2, tmpn, HTBn, 0)
            free_norm_tmp(tmpn)
        for half in range(2):
            wo = load_w(w_o3, half * 512, 512, "wo")
            for dd in range(4):
                d = half * 4 + dd
                pb = d % 2
                proj_fm(wo, dd * 128, MA, lambda kc: MA.t[:, kc, :], lambda kc: kc, pb)
                P.op("dve", lambda e, d=d, pb=pb: e.scalar_tensor_tensor(
                    out=XT.t[:, d, blk * TBS:(blk + 1) * TBS], in0=bank(pb), scalar=g2h.t[:, d:d + 1],
                    in1=XT.t[:, d, blk * TBS:(blk + 1) * TBS], op0=ALU.mult, op1=ALU.add),
                    reads=[(psb[pb], 0), (XT, (d, blk)), (g2h, 0)], writes=[(XT, (d, blk))])
                if unit_hook is not None:
                    unit_hook()
            A.free(wo)
        A.free(MA)
        for b in tg + t1:
            A.free(b)
        A.free(HTB)
        return HTBn

    def half_gate(modT, name, scale):
        hg = A.alloc(name, [128, KC], F32)
        P.op("dve", lambda e: e.tensor_scalar(out=hg.t[:], in0=modT.t[:, 16:24, 16], scalar1=scale, scalar2=None, op0=ALU.mult),
             reads=[(modT, 0)], writes=[(hg, 0)])
        return hg

    NEXT = {}

    def ffn_phase(i, idx, mod, modT, next_mod=None, fin=None):
        cols = prompt_cols(modT, 0)
        hg = half_gate(modT, f"hg{i}", 0.5)
        samp = dict(hsT=sample_norm(mod, norm_w[i]), acts=A.alloc("acts", [NS, DFF], BF16), hgs=sample_gate(mod, "hgs", 0.5))
        A.free(mod)
        HT = A.alloc("HT", [128, KC, T], BF16)
        tmp = norm_tmp()
        norm_block(0, cols, tmp, HT, 0)
        hooks = {}

        def pre_tb(tb):
            if tb + 1 < NTB:
                norm_block(tb + 1, cols, tmp, HT, (tb + 1) * TBS)
            else:
                free_norm_tmp(tmp)
        hooks["pre_tb"] = pre_tb
        if next_mod is not None:
            mp = mod_pieces(next_mod[0], next_mod[1], NEXT)

            def down_unit(nd):
                if nd % 4 == 1:
                    next(mp, None)
            hooks["down_unit"] = down_unit
            hooks["drain"] = lambda: [None for _ in mp]
        if fin is not None:
            def post_tb(tb):
                if tb >= 1:
                    if not fin:
                        fin.update(final_init())
                    for t in range((tb - 1) * 4, tb * 4):
                        final_tile(fin, t, 6 if t % 2 == 0 else 2)
            hooks["post_tb"] = post_tb
        ffn(idx, hg, HT, samp, hooks)
        if "drain" in hooks:
            hooks["drain"]()
        for bf in samp.values():
            A.free(bf)
        A.free(HT); A.free(modT); A.free(cols); A.free(hg)

    def mod0_alloc():
        mod = A.alloc("mod0", [32, 3 * D], F32)
        modT = A.alloc("modT0", [128, 24, 32], F32)
        bada = A.alloc("bada", [17, 3 * D], F32)
        P.op("dve", lambda e: e.memset(mod.t[:], 0.0), writes=[(mod, 0), (mod, 1)])
        return mod, modT, bada

    def mod0_loads(mod, bada):
        P.op("sp", lambda e: e.dma_start(out=bada.t[:], in_=b_ada[0:1, 0:3 * D].partition_broadcast(17)),
             writes=[(bada, 0)], dma=True)
        P.op("sp", lambda e: e.dma_start(out=mod.t[17:18, 0:D], in_=norm_w[0][0:1, :]), writes=[(mod, 1)], dma=True)

    def mod0_block(mod, blk):
        c0 = blk * 512
        wt = ring.alloc("wada", [128, KC, 512], BF16)
        P.op("pool", lambda e: e.dma_start(
            out=wt.t[:], in_=w_ada.rearrange("(k p) n -> p k n", p=128)[:, :, c0:c0 + 512]),
            writes=[(wt, 0)], dma=True)
        for kc in range(KC):
            P.op("pe", lambda e, kc=kc: e.matmul(bank(6)[0:17, :], scT.t[:, kc, 0:17], wt.t[:, kc, :],
                                                 start=(kc == 0), stop=(kc == KC - 1)),
                 reads=[(scT, 0), (wt, 0)], writes=[(psb[6], 0)])
        P.op("dve", lambda e: e.tensor_copy(out=mod.t[0:17, blk * 512:(blk + 1) * 512], in_=bank(6)[0:17, :]),
             reads=[(psb[6], 0), (mod, 0)], writes=[(mod, 0)])
        A.free(wt)

    xh = {}
    mod0 = A.alloc("mod0", [32, 3 * D], F32)
    modT0 = A.alloc("modT0", [128, 24, 32], F32)
    P.op("dve", lambda e: e.memset(mod0.t[:], 0.0), writes=[(mod0, 0), (mod0, 1)])

    def xhook(t):
        if 4 <= t < 10:
            mod_mm(mod0, mod_load(0, t - 4), t - 4)
    xh["fn"] = xhook
    load_x(xh)
    P.op("sp", lambda e: e.dma_start(out=mod0.t[17:18, 0:D], in_=norm_w[0][0:1, :]), writes=[(mod0, 0)], dma=True)
    mod_finish(mod0, modT0)

    if stage >= 1:
        ffn_phase(0, 0, mod0, modT0,
                  next_mod=(1, [(norm_w[1][0:1, :], 0), (gla_nw[0:1, :], 0), (gm_lnw[0:1, :], 0), (gm_lnb[0:1, :], 0)]))
    if stage >= 2:
        mod2, modT2 = NEXT["mm"]
        cols2 = prompt_cols(modT2, 0)
        g2h = half_gate(modT2, "g2h", 0.5)
        hsT2 = sample_norm(mod2, norm_w[1])
        g2s = sample_gate(mod2, "g2s", 0.5)
        A.free(mod2)
        PROJT = A.alloc("PROJT", [128, 56, NS], F32)
        SHOOK.update(hsT=hsT2, PROJT=PROJT)
        C = tm_consts(modT2)
        HTBc = A.alloc("HTB", [128, KC, TBS], BF16)
        tmp0 = norm_tmp()
        norm_block(0, cols2, tmp0, HTBc, 0)
        free_norm_tmp(tmp0)
        for blk in range(NTB):
            hook = None
            uh = None
            if blk == NTB - 1:
                def hook():
                    NEXT["stm"] = sample_token_mix(hsT2, g2s, PROJT)
                    next(NEXT["stm"])
                if stage >= 3:
                    mp3 = mod_pieces(2, [(norm_w[2][0:1, :], 0)], NEXT)
                    cnt = [0]

                    def uh():
                        cnt[0] += 1
                        if cnt[0] % 2 == 1:
                            next(mp3, None)
            HTBc = token_mix_block(blk, C, modT2, cols2, g2h, HTBc, hook, uh)
            if blk == NTB - 1 and stage >= 3:
                for _ in mp3:
                    pass
        P.op("sp", lambda e: e.dma_start(out=S_p.rearrange("h k v -> k h v"), in_=C["S32"].t[:]),
             reads=[(C["S32"], h) for h in range(4)], dma=True)
        for b in C.values():
            A.free(b)
        A.free(modT2); A.free(cols2); A.free(g2h)
        for _ in NEXT["stm"]:
            pass
        A.free(PROJT)
    fin = {}
    if stage < 3:
        fin.update(final_init())
    if stage >= 3:
        mod3, modT3 = NEXT["mm"]
        ffn_phase(2, 1, mod3, modT3, fin=fin)
        for t in range(12, 16):
            final_tile(fin, t, 6 if t % 2 == 0 else 2)
    else:
        for t in range(T // 128):
            final_tile(fin, t, 4 + 2 * (t % 2))
    final_sample()

    P.emit(nc)
    return nc


def make_in_maps(inp):
    f = lambda a: np.ascontiguousarray(np.asarray(a, dtype=np.float32))
    shared = {
        "w_ada": f(inp["w_ada"][0]),
        "b_ada": f(inp["b_ada"][0]).reshape(1, -1),
        "norm1_w": f(inp["norm1_w"][0]).reshape(1, -1),
        "norm2_w": f(inp["norm2_w"][0]).reshape(1, -1),
        "norm3_w": f(inp["norm3_w"][0]).reshape(1, -1),
        "normf_w": f(inp["normf_w"]).reshape(1, -1),
        "ffn1_w13": f(inp["ffn1_w13"][0]),
        "ffn1_w2": f(inp["ffn1_w2"][0]),
        "ffn2_w13": f(inp["ffn2_w13"][0]),
        "ffn2_w2": f(inp["ffn2_w2"][0]),
        "ident": np.eye(128, dtype=np.float32),
        "w_in": f(inp["w_in"][0]),
        "w_a2aug": f(np.concatenate([inp["w_a2"][0], inp["b_a"][0][None, :]], axis=0)),
        "gla_norm_w": f(inp["gla_norm_w"][0]).reshape(1, -1),
        "gm_ln_w": f(inp["gm_ln_w"][0]).reshape(1, -1),
        "gm_ln_b": f(inp["gm_ln_b"][0]).reshape(1, -1),
        "wsT": f(np.transpose(inp["gm_ws"][0], (2, 0, 1))),
        "gm_bs": f(inp["gm_bs"][0]).reshape(1, -1),
        "w_pa": f(inp["w_pa"][0]),
        "w_pb": f(inp["w_pb"][0]),
        "w_o": f(inp["w_o"][0]),
        "tri": _TRI,
        "tri64": _TRI64,
        "ucum": _UCUM,
        "ws00_row": f(np.repeat(inp["gm_ws"][0][:, 0, 0], 256)).reshape(1, -1),
        "bs0_row": f(np.repeat(inp["gm_bs"][0][:, 0], 256)).reshape(1, -1),
        "eye16": np.eye(16, dtype=np.float32).reshape(1, 256),
    }
    maps = []
    for c in range(NCORES):
        m = dict(shared)
        m["x_p"] = f(inp["x_prompt"][c])
        m["x_s"] = f(inp["x_sample"][c * NS:(c + 1) * NS, 0, :])
        m["state_s"] = f(inp["state_gla"][0, c * NS:(c + 1) * NS])
        m["c17"] = f(np.concatenate([inp["c_sample"][c * NS:(c + 1) * NS], inp["c_prompt"][c:c + 1]], axis=0))
        maps.append(m)
    return maps


_i = np.arange(128)
_TRI = (_i[:, None] <= _i[None, :]).astype(np.float32)
_TRI64 = (_TRI * ((_i[:, None] // 64) == (_i[None, :] // 64))).astype(np.float32)
_UCUM = (_TRI64 * (-1.0 / 16.0)).astype(np.float32)
_NC_CACHE = {}


def run(inp, stage=99, trace=False):
    if stage not in _NC_CACHE:
        _NC_CACHE[stage] = build(stage)
    nc = _NC_CACHE[stage]
    maps = make_in_maps(inp)
    used = set()
    return run_bass_kernel_spmd(nc, maps, core_ids=list(range(NCORES)), trace=trace)


def kernel(**inputs):
    res = run(inputs)
    r = res.results
    y_prompt = np.stack([r[c]["y_p"] for c in range(NCORES)], axis=0).astype(np.float32)
    y_sample = np.concatenate([r[c]["y_s"] for c in range(NCORES)], axis=0).reshape(NCORES * NS, 1, D).astype(np.float32)
    S_prompt = np.stack([r[c]["S_p"] for c in range(NCORES)], axis=0)[None].astype(np.float32)
    S_sample = np.concatenate([r[c]["S_s"] for c in range(NCORES)], axis=0)[None].astype(np.float32)
    gv_sample = np.concatenate([r[c]["gv_s"] for c in range(NCORES)], axis=0).reshape(1, NCORES * NS, 1, D).astype(np.float32)
    return (y_prompt, y_sample, S_prompt, S_sample, gv_sample)
```

```python
import numpy as np
from contextlib import ExitStack
import concourse.bass as bass
import concourse.mybir as mybir
from concourse.bass_utils import run_bass_kernel_spmd

F32 = mybir.dt.float32
BF16 = mybir.dt.bfloat16
AF = mybir.ActivationFunctionType
ALU = mybir.AluOpType

NCORES = 8
D = 1024
KC = 8
T = 2048
NTB = 4
TBS = 512
NS = 16
DFF = 2816
NFC = 22
D_IN = 7184
EPS = 1e-6
ENGS = ("pe", "act", "dve", "pool", "sp")
NSEM_DMA = 12

SB_LO = 16640
SB_HI = 228352


class Op:
    __slots__ = ("eng", "seq", "fn", "waits", "sig", "sigidx", "dma", "dsem", "dval")

    def __init__(self, eng, seq, fn, dma):
        self.eng = eng
        self.seq = seq
        self.fn = fn
        self.waits = []
        self.sig = False
        self.sigidx = 0
        self.dma = dma
        self.dsem = 0
        self.dval = 0


class Buf:
    def __init__(self, name):
        self.name = name
        self.w = {}
        self.r = {}
        self.touch = {}
        self.touch_dma = []
        self.inherit = []
        self.t = None

    def all_ops(self):
        return list(self.touch.values()) + list(self.touch_dma) + list(self.inherit)


def _dedupe(ops):
    best = {}
    dm = {}
    for o in ops:
        if o.dma:
            dm[id(o)] = o
        else:
            b = best.get(o.eng)
            if b is None or b.seq < o.seq:
                best[o.eng] = o
    return list(best.values()) + list(dm.values())


class Prog:
    def __init__(self):
        self.streams = {e: [] for e in ENGS}
        self.known = {e: {f: -1 for f in ENGS} for e in ENGS}
        self.known_dma = {e: {} for e in ENGS}
        self.dma_count = {e: 0 for e in ENGS}
        self.dma_last = {e: [None] * NSEM_DMA for e in ENGS}

    def _add_dep(self, op, dep):
        if dep is None or dep is op:
            return
        e = op.eng
        if dep.dma:
            key = (dep.eng, dep.dsem)
            if self.known_dma[e].get(key, 0) >= dep.dval:
                return
            self.known_dma[e][key] = dep.dval
            op.waits.append(dep)
        else:
            if dep.eng == "pe" and e == "pe":
                return
            if self.known[e][dep.eng] >= dep.seq:
                return
            self.known[e][dep.eng] = dep.seq
            dep.sig = True
            op.waits.append(dep)

    def op(self, eng, fn, reads=(), writes=(), dma=False):
        o = Op(eng, len(self.streams[eng]), fn, dma)
        for (buf, key) in reads:
            for d in buf.inherit:
                self._add_dep(o, d)
            self._add_dep(o, buf.w.get(key))
        for (buf, key) in writes:
            for d in buf.inherit:
                self._add_dep(o, d)
            self._add_dep(o, buf.w.get(key))
            rr = buf.r.get(key)
            if rr:
                for d in rr.values():
                    self._add_dep(o, d)
        for (buf, key) in reads:
            rr = buf.r.setdefault(key, {})
            rr[id(o) if dma else eng] = o
            self._touch(buf, o)
        for (buf, key) in writes:
            buf.w[key] = o
            buf.r[key] = {}
            self._touch(buf, o)
        if dma:
            k = self.dma_count[eng]
            idx = k % NSEM_DMA
            prev = self.dma_last[eng][idx]
            if prev is not None:
                self._add_dep(o, prev)
            o.dsem = idx
            o.dval = 16 * (k // NSEM_DMA + 1)
            self.dma_last[eng][idx] = o
            self.dma_count[eng] = k + 1
        self.streams[eng].append(o)
        return o

    def _touch(self, buf, o):
        if o.dma:
            buf.touch_dma.append(o)
            if len(buf.touch_dma) > 64:
                buf.touch_dma = buf.touch_dma[-64:]
        else:
            buf.touch[o.eng] = o

    def emit(self, nc):
        for e in ENGS:
            c = 0
            for o in self.streams[e]:
                if o.sig and not o.dma:
                    c += 1
                    o.sigidx = c
        with ExitStack() as es:
            sem = {e: es.enter_context(nc.semaphore("s_" + e)) for e in ENGS}
            dsem = {e: [es.enter_context(nc.semaphore(f"d_{e}_{i}")) for i in range(NSEM_DMA)]
                    for e in ("pool", "sp")}
            es.enter_context(nc.allow_low_precision("bf16 matmul operands, fp32 accumulation"))
            block = es.enter_context(nc.Block())
            streams = self.streams

            def run(ename, engine):
                for o in streams[ename]:
                    for d in o.waits:
                        if d.dma:
                            engine.wait_ge(dsem[d.eng][d.dsem], d.dval)
                        else:
                            engine.wait_ge(sem[d.eng], d.sigidx)
                    ins = o.fn(engine)
                    if o.dma:
                        ins.then_inc(dsem[ename][o.dsem], 16)
                    elif o.sig:
                        ins.then_inc(sem[ename], 1)
                if ename in ("pool", "sp"):
                    for i, last in enumerate(self.dma_last[ename]):
                        if last is not None:
                            engine.wait_ge(dsem[ename][i], last.dval)

            @block.tensor
            def _(eng):
                run("pe", eng)

            @block.scalar
            def _(eng):
                run("act", eng)

            @block.vector
            def _(eng):
                run("dve", eng)

            @block.gpsimd
            def _(eng):
                run("pool", eng)

            @block.sync
            def _(eng):
                run("sp", eng)


class Arena:
    def __init__(self, nc, lo, hi):
        self.nc = nc
        self.lo = lo
        self.hi = hi
        self.live = []
        self.dead = []
        self.n = 0

    def alloc(self, name, shape, dtype, off=None, lo=None, hi=None):
        esz = 2 if dtype == BF16 else 4
        nb = esz
        for s in shape[1:]:
            nb *= s
        nb = (nb + 63) // 64 * 64
        lo = self.lo if lo is None else lo
        hi = self.hi if hi is None else hi
        if off is None:
            segs = sorted((s, e) for (s, e, b) in self.live)
            cur = lo
            for (s, e) in segs:
                if e <= cur:
                    continue
                if s - cur >= nb:
                    break
                cur = max(cur, e)
            off = cur
        assert off >= lo and off + nb <= hi, f"SBUF arena overflow for {name}: {off}+{nb} > {hi}"
        for (s, e, b) in self.live:
            assert e <= off or s >= off + nb, f"arena overlap {name} vs {b.name}"
        buf = Buf(name)
        ops = []
        keep = []
        for (s, e, b) in self.dead:
            if e <= off or s >= off + nb:
                keep.append((s, e, b))
                continue
            ops.extend(b.all_ops())
            if not (off <= s and e <= off + nb):
                keep.append((s, e, b))
        self.dead = keep
        buf.inherit = _dedupe(ops)
        self.n += 1
        buf.t = self.nc.alloc_sbuf_tensor_at(f"{name}_{self.n}", list(shape), dtype, offset=off)
        self.live.append((off, off + nb, buf))
        top = max(e for (s_, e, b_) in self.live if e <= self.hi) if any(e <= self.hi for (s_, e, b_) in self.live) else 0
        if top > getattr(self, "peak", 0) and off + nb <= self.hi:
            self.peak = top
            self.peak_live = [(b_.name, s_, e - s_) for (s_, e, b_) in self.live]
        return buf

    def free(self, buf):
        for i, (s, e, b) in enumerate(self.live):
            if b is buf:
                self.dead.append(self.live.pop(i))
                return
        raise KeyError(buf.name)


class Ring:
    def __init__(self, arena, lo, hi):
        self.a = arena
        self.lo = lo
        self.hi = hi
        self.ptr = lo

    def alloc(self, name, shape, dtype):
        esz = 2 if dtype == BF16 else 4
        nb = esz
        for s in shape[1:]:
            nb *= s
        nb = (nb + 63) // 64 * 64
        for _ in range(16):
            if self.ptr + nb > self.hi:
                self.ptr = self.lo
            clash = [e for (s_, e, b_) in self.a.live if s_ < self.ptr + nb and e > self.ptr and s_ >= self.lo]
            if not clash:
                break
            self.ptr = max(clash)
        off = self.ptr
        self.ptr += nb
        return self.a.alloc(name, shape, dtype, off=off, lo=self.lo, hi=self.hi)


FF_SPLITS = [(0, 7), (7, 15), (15, 22)]


def build(stage=99):
    nc = bass.Bass("TRN2", target_bir_lowering=False)
    P = Prog()
    A = Arena(nc, SB_LO, SB_HI)
    RING_BYTES = 32 * 1024
    ring = Ring(A, SB_HI - RING_BYTES, SB_HI)
    A.hi = SB_HI - RING_BYTES

    def din(name, shape):
        return nc.dram_tensor(name, list(shape), F32, kind="ExternalInput").ap()

    def dout(name, shape):
        return nc.dram_tensor(name, list(shape), F32, kind="ExternalOutput").ap()

    x_p = din("x_p", [T, D])
    x_s = din("x_s", [NS, D])
    c17 = din("c17", [17, D])
    w_ada = din("w_ada", [D, 9 * D])
    b_ada = din("b_ada", [1, 9 * D])
    norm_w = [din(f"norm{i}_w", [1, D]) for i in (1, 2, 3)]
    normf_w = din("normf_w", [1, D])
    ffn_w13 = [din("ffn1_w13", [D, 2 * DFF]), din("ffn2_w13", [D, 2 * DFF])]
    ffn_w2 = [din("ffn1_w2", [DFF, D]), din("ffn2_w2", [DFF, D])]
    ident_d = din("ident", [128, 128])
    w_in = din("w_in", [D, D_IN])
    w_a2aug = din("w_a2aug", [17, 512])
    gla_nw = din("gla_norm_w", [1, D])
    gm_lnw = din("gm_ln_w", [1, D])
    gm_lnb = din("gm_ln_b", [1, D])
    wsT_d = din("wsT", [128, 4, 128])
    bs_d = din("gm_bs", [1, 512])
    w_pa = din("w_pa", [D, D])
    w_pb = din("w_pb", [D, D])
    w_o = din("w_o", [D, D])
    tri_d = din("tri", [128, 128])
    tri64_d = din("tri64", [128, 128])
    ucum_d = din("ucum", [128, 128])
    state_s = din("state_s", [NS, 4, 128, 256])
    ws00_d = din("ws00_row", [1, D])
    bs0_d = din("bs0_row", [1, D])
    eye16_d = din("eye16", [1, 256])

    y_p = dout("y_p", [T, D])
    y_s = dout("y_s", [NS, D])
    S_p = dout("S_p", [4, 128, 256])
    S_s = dout("S_s", [NS, 4, 128, 256])
    gv_s = dout("gv_s", [NS, D])

    PS = nc.alloc_psum_tensor("ps", [128, 4096], F32)
    psb = [Buf(f"psb{i}") for i in range(8)]

    def bank(i, n=512, off=0):
        return PS[:, i * 512 + off:i * 512 + off + n]

    def bank_bf(i):
        return PS[:, i * 512:(i + 1) * 512].bitcast(BF16)

    ident = A.alloc("ident", [128, 128], F32)
    identb = A.alloc("identb", [128, 128], BF16)
    onesb = A.alloc("onesb", [128, 128], BF16)
    P.op("sp", lambda e: e.dma_start(out=ident.t[:], in_=ident_d), writes=[(ident, 0)], dma=True)
    P.op("dve", lambda e: e.tensor_copy(out=identb.t[:], in_=ident.t[:]), reads=[(ident, 0)], writes=[(identb, 0)])
    P.op("dve", lambda e: e.memset(onesb.t[:], 1.0), writes=[(onesb, 0)])

    XT = A.alloc("XT", [128, KC, T], F32)
    XS = A.alloc("XS", [NS, D], F32)
    P.op("sp", lambda e: e.dma_start(out=XS.t[:], in_=x_s), writes=[(XS, 0)], dma=True)

    c_sb = A.alloc("c_sb", [17, D], F32)
    cs_b = A.alloc("cs_b", [17, D], BF16)
    scT = A.alloc("scT", [128, KC, 32], BF16)
    P.op("sp", lambda e: e.dma_start(out=c_sb.t[:], in_=c17), writes=[(c_sb, 0)], dma=True)
    P.op("act", lambda e: e.activation(out=cs_b.t[:], in_=c_sb.t[:], func=AF.Silu), reads=[(c_sb, 0)], writes=[(cs_b, 0)])
    for kc in range(KC):
        P.op("pe", lambda e, kc=kc: e.transpose(out=bank_bf(7)[:, kc * 32:kc * 32 + 17],
                                                  in_=cs_b.t[:, kc * 128:(kc + 1) * 128],
                                                  identity=identb.t[0:17, 0:17]),
             reads=[(cs_b, 0), (identb, 0)], writes=[(psb[7], 0)])
    P.op("dve", lambda e: e.tensor_copy(out=scT.t[:, :, 0:17],
                                        in_=bank_bf(7)[:, 0:256].rearrange("p (k c) -> p k c", c=32)[:, :, 0:17]),
         reads=[(psb[7], 0)], writes=[(scT, 0)])
    A.free(c_sb)
    A.free(cs_b)

    def load_x(XHOOK):
      xin = [A.alloc(f"xin{i}", [128, D], F32, off=A.hi - (i + 1) * 4096) for i in range(4)]
      for t in range(T // 128):
        if "fn" in XHOOK:
            XHOOK["fn"](t)
        xb = xin[t % 4]
        pb = 4 + 2 * (t % 2)
        P.op("sp", lambda e, xb=xb, t=t: e.dma_start(out=xb.t[:], in_=x_p[t * 128:(t + 1) * 128, :]),
             writes=[(xb, 0)], dma=True)
        for kc in range(KC):
            P.op("pe", lambda e, xb=xb, kc=kc, pb=pb: e.transpose(
                out=PS[:, pb * 512 + kc * 128: pb * 512 + (kc + 1) * 128],
                in_=xb.t[:, kc * 128:(kc + 1) * 128], identity=ident.t[:]),
                reads=[(xb, 0), (ident, 0)], writes=[(psb[pb + kc // 4], 0)])
        eng = "act" if t % 2 == 0 else "dve"
        src = lambda pb=pb: PS[:, pb * 512:(pb + 2) * 512].rearrange("p (k c) -> p k c", c=128)
        if eng == "act":
            P.op("act", lambda e, t=t, src=src: e.activation(out=XT.t[:, :, t * 128:(t + 1) * 128], in_=src(), func=AF.Copy),
                 reads=[(psb[pb], 0), (psb[pb + 1], 0)], writes=[(XT, (k, t // 4)) for k in range(KC)])
        else:
            P.op("dve", lambda e, t=t, src=src: e.tensor_copy(out=XT.t[:, :, t * 128:(t + 1) * 128], in_=src()),
                 reads=[(psb[pb], 0), (psb[pb + 1], 0)], writes=[(XT, (k, t // 4)) for k in range(KC)])
      for b in xin:
        A.free(b)

    def mod_start(i, extra_rows):
        mod = A.alloc(f"mod{i}", [32, 3 * D], F32)
        modT = A.alloc(f"modT{i}", [128, 24, 32], F32)
        P.op("dve", lambda e: e.memset(mod.t[:], 0.0), writes=[(mod, 0)])
        P.op("sp", lambda e: e.dma_start(out=mod.t[0:17, :],
                                         in_=b_ada[0:1, i * 3 * D:(i + 1) * 3 * D].partition_broadcast(17)),
             reads=[], writes=[(mod, 0)], dma=True)
        for r, (ap, c0) in enumerate(extra_rows):
            row = 17 + r // 3
            col = (r % 3) * D
            P.op("sp", lambda e, ap=ap, row=row, col=col: e.dma_start(out=mod.t[row:row + 1, col:col + D], in_=ap),
                 writes=[(mod, 0)], dma=True)
        return mod, modT

    def mod_block(i, mod, blk, after=()):
        c0 = i * 3 * D + blk * 512
        wt = ring.alloc("wada", [128, KC, 512], BF16)
        P.op("pool", lambda e: e.dma_start(
            out=wt.t[:], in_=w_ada.rearrange("(k p) n -> p k n", p=128)[:, :, c0:c0 + 512]),
            reads=list(after), writes=[(wt, 0)], dma=True)
        for kc in range(KC):
            P.op("pe", lambda e, kc=kc: e.matmul(bank(6)[0:17, :], scT.t[:, kc, 0:17], wt.t[:, kc, :],
                                                 start=(kc == 0), stop=(kc == KC - 1)),
                 reads=[(scT, 0), (wt, 0)], writes=[(psb[6], 0)])
        P.op("dve", lambda e: e.tensor_tensor(out=mod.t[0:17, blk * 512:(blk + 1) * 512], in0=bank(6)[0:17, :],
                                              in1=mod.t[0:17, blk * 512:(blk + 1) * 512], op=ALU.add),
             reads=[(psb[6], 0), (mod, 0)], writes=[(mod, 0)])
        A.free(wt)

    def mod_finish(mod, modT):
        for j in range(24):
            P.op("pe", lambda e, j=j: e.transpose(out=PS[:, 7 * 512 + (j % 12) * 32: 7 * 512 + (j % 12) * 32 + 32],
                                                  in_=mod.t[:, j * 128:(j + 1) * 128], identity=ident.t[0:32, 0:32]),
                 reads=[(mod, 0), (ident, 0)], writes=[(psb[7], 0)])
            if j % 12 == 11:
                h = j // 12
                P.op("dve", lambda e, h=h: e.tensor_copy(
                    out=modT.t[:, h * 12:(h + 1) * 12, :],
                    in_=bank(7)[:, 0:384].rearrange("p (k c) -> p k c", c=32)),
                    reads=[(psb[7], 0)], writes=[(modT, 0)])

    ones17 = A.alloc("ones17", [1, 32], F32)
    P.op("dve", lambda e: e.memset(ones17.t[:], 1.0), writes=[(ones17, 0)])

    def mod_load(i, blk):
        c0 = i * 3 * D + blk * 512
        wt = ring.alloc("wada", [128, KC, 512], BF16)
        P.op("pool", lambda e: e.dma_start(
            out=wt.t[:], in_=w_ada.rearrange("(k p) n -> p k n", p=128)[:, :, c0:c0 + 512]),
            writes=[(wt, 0)], dma=True)
        brow = A.alloc("brow", [1, 512], F32)
        P.op("sp", lambda e: e.dma_start(out=brow.t[:], in_=b_ada[0:1, c0:c0 + 512]), writes=[(brow, 0)], dma=True)
        return wt, brow

    def mod_mm(mod, wb_, blk):
        wt, brow = wb_
        P.op("pe", lambda e: e.matmul(bank(6)[0:17, :], ones17.t[0:1, 0:17], brow.t[0:1, :], start=True, stop=False),
             reads=[(ones17, 0), (brow, 0)], writes=[(psb[6], 0)])
        for kc in range(KC):
            P.op("pe", lambda e, kc=kc: e.matmul(bank(6)[0:17, :], scT.t[:, kc, 0:17], wt.t[:, kc, :],
                                                 start=False, stop=(kc == KC - 1)),
                 reads=[(scT, 0), (wt, 0)], writes=[(psb[6], 0)])
        P.op("dve", lambda e: e.tensor_copy(out=mod.t[0:17, blk * 512:(blk + 1) * 512], in_=bank(6)[0:17, :]),
             reads=[(psb[6], 0), (mod, 0)], writes=[(mod, 0)])
        A.free(wt)
        A.free(brow)

    def mod_start_nob(i, extra_rows):
        mod = A.alloc(f"mod{i}", [32, 3 * D], F32)
        modT = A.alloc(f"modT{i}", [128, 24, 32], F32)
        P.op("dve", lambda e: e.memset(mod.t[:], 0.0), writes=[(mod, 0)])
        for r, (ap, c0) in enumerate(extra_rows):
            row = 17 + r // 3
            col = (r % 3) * D
            P.op("sp", lambda e, ap=ap, row=row, col=col: e.dma_start(out=mod.t[row:row + 1, col:col + D], in_=ap),
                 writes=[(mod, 0)], dma=True)
        return mod, modT

    def mod_pieces(i, extra_rows, out):
        mod, modT = mod_start_nob(i, extra_rows)
        out["mm"] = (mod, modT)
        wt = mod_load(i, 0)
        yield
        for blk in range(6):
            mod_mm(mod, wt, blk)
            wt = mod_load(i, blk + 1) if blk + 1 < 6 else None
            yield
        mod_finish(mod, modT)
        yield

    def compute_mod(i, extra_rows):
        mod, modT = mod_start(i, extra_rows)
        for blk in range(6):
            mod_block(i, mod, blk)
        mod_finish(mod, modT)
        return mod, modT

    def prompt_cols(modT, wrow_blk0):
        cols = A.alloc("pcols", [128, 3, KC], F32)
        P.op("dve", lambda e: e.scalar_tensor_tensor(out=cols.t[:, 0, :], in0=modT.t[:, 8:16, 16], scalar=1.0,
                                                      in1=modT.t[:, wrow_blk0:wrow_blk0 + 8, 17],
                                                      op0=ALU.add, op1=ALU.mult),
             reads=[(modT, 0)], writes=[(cols, 0)])
        P.op("dve", lambda e: e.tensor_copy(out=cols.t[:, 1, :], in_=modT.t[:, 0:8, 16]),
             reads=[(modT, 0), (cols, 0)], writes=[(cols, 0)])
        return cols

    def norm_block(tb, cols, tmp, dst, dc0):
        sq, rstd, nt = tmp
        for kc in range(KC):
            P.op("act", lambda e, kc=kc: e.activation(out=sq.t[:, kc % 4, :], in_=XT.t[:, kc, tb * TBS:(tb + 1) * TBS],
                                                      func=AF.Square),
                 reads=[(XT, (kc, tb))], writes=[(sq, kc % 4)])
            P.op("pe", lambda e, kc=kc: e.matmul(bank(7), onesb.t[:], sq.t[:, kc % 4, :], start=(kc == 0), stop=(kc == KC - 1)),
                 reads=[(onesb, 0), (sq, kc % 4)], writes=[(psb[7], 0)])
        P.op("act", lambda e: e.activation(out=rstd.t[:], in_=bank(7), func=AF.Ln, scale=1.0 / D, bias=EPS),
             reads=[(psb[7], 0)], writes=[(rstd, 0)])
        P.op("act", lambda e: e.activation(out=rstd.t[:], in_=rstd.t[:], func=AF.Exp, scale=-0.5),
             reads=[(rstd, 0)], writes=[(rstd, 0)])
        for kc in range(KC):
            t = nt[kc % 2]
            P.op("dve", lambda e, kc=kc, t=t: e.scalar_tensor_tensor(
                out=t.t[:], in0=XT.t[:, kc, tb * TBS:(tb + 1) * TBS], scalar=cols.t[:, 0, kc:kc + 1],
                in1=rstd.t[:], op0=ALU.mult, op1=ALU.mult),
                reads=[(XT, (kc, tb)), (cols, 0), (rstd, 0)], writes=[(t, 0)])
            if kc % 4 == 3:
                P.op("dve", lambda e, kc=kc, t=t: e.tensor_scalar(out=dst.t[:, kc, dc0:dc0 + TBS], in0=t.t[:],
                                                                  scalar1=cols.t[:, 1, kc:kc + 1], scalar2=None, op0=ALU.add),
                     reads=[(t, 0), (cols, 0)], writes=[(dst, (kc, tb))])
            else:
                P.op("act", lambda e, kc=kc, t=t: e.activation(out=dst.t[:, kc, dc0:dc0 + TBS], in_=t.t[:],
                                                               func=AF.Identity, bias=cols.t[:, 1, kc:kc + 1], scale=1.0),
                     reads=[(t, 0), (cols, 0)], writes=[(dst, (kc, tb))])

    def norm_tmp():
        sq = A.alloc("sq", [128, 4, TBS], BF16)
        rstd = A.alloc("rstd", [128, TBS], F32)
        nt = [A.alloc(f"nt{i}", [128, TBS], F32) for i in range(2)]
        return sq, rstd, nt

    def free_norm_tmp(tmp):
        sq, rstd, nt = tmp
        A.free(sq)
        A.free(rstd)
        for b in nt:
            A.free(b)

    def ffn(idx, gbuf, HT, samp=None, hooks=None):
        hooks = hooks or {}
        first_group = [True]
        w13 = ffn_w13[idx].rearrange("(k p) n -> p k n", p=128)
        w2 = ffn_w2[idx].rearrange("(k p) n -> p k n", p=128)
        sa = [A.alloc(f"sa{i}", [128, TBS], F32) for i in range(2)]
        nev = 0
        for (c_lo, c_hi) in FF_SPLITS:
            nch = c_hi - c_lo
            act = A.alloc("act", [128, nch, T], BF16)
            g0 = c_lo
            while g0 < c_hi:
                g1 = min(g0 + 4, c_hi)
                ncol = (g1 - g0) * 128
                wa = ring.alloc("wa", [128, KC, ncol], BF16)
                wb = ring.alloc("wb", [128, KC, ncol], BF16)
                P.op("pool", lambda e, wa=wa, g0=g0, ncol=ncol: e.dma_start(
                    out=wa.t[:], in_=w13[:, :, g0 * 128:g0 * 128 + ncol]), writes=[(wa, 0)], dma=True)
                P.op("pool", lambda e, wb=wb, g0=g0, ncol=ncol: e.dma_start(
                    out=wb.t[:], in_=w13[:, :, DFF + g0 * 128:DFF + g0 * 128 + ncol]), writes=[(wb, 0)], dma=True)
                if samp is not None:
                    sample_proj(samp["hsT"], wa, ncol, 6)
                    sample_proj(samp["hsT"], wb, ncol, 7)
                    sas = A.alloc("sas", [NS, 512], F32)
                    P.op("act", lambda e, sas=sas, ncol=ncol: e.activation(out=sas.t[:, 0:ncol], in_=bank(6)[0:NS, 0:ncol], func=AF.Silu),
                         reads=[(psb[6], 0)], writes=[(sas, 0)])
                    P.op("dve", lambda e, sas=sas, ncol=ncol, g0=g0: e.tensor_tensor(
                        out=samp["acts"].t[:, g0 * 128:g0 * 128 + ncol], in0=sas.t[:, 0:ncol], in1=bank(7)[0:NS, 0:ncol], op=ALU.mult),
                        reads=[(sas, 0), (psb[7], 0)], writes=[(samp["acts"], 0)])
                    A.free(sas)
                for tb in range(NTB):
                    if first_group[0] and "pre_tb" in hooks:
                        hooks["pre_tb"](tb)
                    for j in range(g0, g1):
                        pa = nev % 2
                        pbk = 2 + nev % 2
                        for kc in range(KC):
                            P.op("pe", lambda e, wa=wa, kc=kc, j=j, g0=g0, tb=tb, pa=pa: e.matmul(
                                bank(pa), wa.t[:, kc, (j - g0) * 128:(j - g0 + 1) * 128],
                                HT.t[:, kc, tb * TBS:(tb + 1) * TBS], start=(kc == 0), stop=(kc == KC - 1)),
                                reads=[(wa, 0), (HT, (kc, tb))], writes=[(psb[pa], 0)])
                        for kc in range(KC):
                            P.op("pe", lambda e, wb=wb, kc=kc, j=j, g0=g0, tb=tb, pbk=pbk: e.matmul(
                                bank(pbk), wb.t[:, kc, (j - g0) * 128:(j - g0 + 1) * 128],
                                HT.t[:, kc, tb * TBS:(tb + 1) * TBS], start=(kc == 0), stop=(kc == KC - 1)),
                                reads=[(wb, 0), (HT, (kc, tb))], writes=[(psb[pbk], 0)])
                        s = sa[nev % 2]
                        P.op("act", lambda e, s=s, pa=pa: e.activation(out=s.t[:], in_=bank(pa), func=AF.Silu),
                             reads=[(psb[pa], 0)], writes=[(s, 0)])
                        P.op("dve", lambda e, s=s, pbk=pbk, j=j, tb=tb, c_lo=c_lo, act=act: e.tensor_tensor(
                            out=act.t[:, j - c_lo, tb * TBS:(tb + 1) * TBS], in0=s.t[:], in1=bank(pbk), op=ALU.mult),
                            reads=[(s, 0), (psb[pbk], 0)], writes=[(act, (j - c_lo, tb))])
                        nev += 1
                A.free(wa)
                A.free(wb)
                first_group[0] = False
                g0 = g1
            last_split = (c_hi == NFC)
            if last_split and "before_last_down" in hooks:
                hooks["before_last_down"]()
            w2t = []
            for half in range(2):
                wt = ring.alloc("w2", [128, nch, 512], BF16)
                P.op("pool", lambda e, wt=wt, half=half, c_lo=c_lo, nch=nch: e.dma_start(
                    out=wt.t[:], in_=w2[:, c_lo:c_lo + nch, half * 512:(half + 1) * 512]), writes=[(wt, 0)], dma=True)
                w2t.append(wt)
            if samp is not None:
                aT_ = transpose_s(samp["acts"], c_lo * 128, nch, "actsT")
                for half in range(2):
                    sample_proj(aT_, w2t[half], 512, 6)
                    sample_residual(6, samp["hgs"], half * 512)
                A.free(aT_)
            nd = 0
            for tb in range(NTB):
                for d in range(KC):
                    pd = 4 + nd % 2
                    wt = w2t[d // 4]
                    for k in range(nch):
                        P.op("pe", lambda e, wt=wt, k=k, d=d, tb=tb, pd=pd, act=act, nch=nch: e.matmul(
                            bank(pd), wt.t[:, k, (d % 4) * 128:(d % 4 + 1) * 128], act.t[:, k, tb * TBS:(tb + 1) * TBS],
                            start=(k == 0), stop=(k == nch - 1)),
                            reads=[(wt, 0), (act, (k, tb))], writes=[(psb[pd], 0)])
                    P.op("dve", lambda e, d=d, tb=tb, pd=pd: e.scalar_tensor_tensor(
                        out=XT.t[:, d, tb * TBS:(tb + 1) * TBS], in0=bank(pd), scalar=gbuf.t[:, d:d + 1],
                        in1=XT.t[:, d, tb * TBS:(tb + 1) * TBS], op0=ALU.mult, op1=ALU.add),
                        reads=[(psb[pd], 0), (XT, (d, tb)), (gbuf, 0)], writes=[(XT, (d, tb))])
                    nd += 1
                    if last_split and "down_unit" in hooks:
                        hooks["down_unit"](nd)
                if last_split and "post_tb" in hooks:
                    hooks["post_tb"](tb)
            for wt in w2t:
                A.free(wt)
            A.free(act)
        for b in sa:
            A.free(b)

    def final_init():
        F = {}
        F["nfb"] = A.alloc("nfb", [128, D], F32)
        P.op("sp", lambda e: e.dma_start(out=F["nfb"].t[:], in_=normf_w[0:1, :].partition_broadcast(128)),
             writes=[(F["nfb"], 0)], dma=True)
        F["yt"] = [A.alloc(f"yt{i}", [128, D], F32) for i in range(2)]
        F["junk"] = A.alloc("junk", [128, D], BF16)
        F["st"] = A.alloc("fstat", [128, 16, 2], F32)
        P.op("dve", lambda e: e.memset(F["st"].t[:], 0.0), writes=[(F["st"], t) for t in range(16)])
        return F

    def final_tile(F, t, pb):
        nfb, junk, st = F["nfb"], F["junk"], F["st"]
        for kc in range(KC):
            P.op("pe", lambda e, kc=kc: e.transpose(
                out=PS[:, pb * 512 + kc * 128: pb * 512 + (kc + 1) * 128],
                in_=XT.t[:, kc, t * 128:(t + 1) * 128], identity=ident.t[:]),
                reads=[(XT, (kc, t // 4)), (ident, 0)], writes=[(psb[pb + kc // 4], 0)])
        src = lambda: PS[:, pb * 512:(pb + 2) * 512]
        P.op("act", lambda e: e.activation(out=junk.t[:], in_=src(), func=AF.Square, accum_out=st.t[:, t, 0:1]),
             reads=[(psb[pb], 0), (psb[pb + 1], 0)], writes=[(junk, 0), (st, t)])
        P.op("act", lambda e: e.activation(out=st.t[:, t, 1:2], in_=st.t[:, t, 0:1], func=AF.Ln, scale=1.0 / D, bias=EPS),
             reads=[(st, t)], writes=[(st, t)])
        P.op("act", lambda e: e.activation(out=st.t[:, t, 1:2], in_=st.t[:, t, 1:2], func=AF.Exp, scale=-0.5),
             reads=[(st, t)], writes=[(st, t)])
        y = F["yt"][t % 2]
        P.op("dve", lambda e: e.scalar_tensor_tensor(
            out=y.t[:], in0=src(), scalar=st.t[:, t, 1:2], in1=nfb.t[:], op0=ALU.mult, op1=ALU.mult),
            reads=[(psb[pb], 0), (psb[pb + 1], 0), (st, t), (nfb, 0)], writes=[(y, 0)])
        P.op("sp", lambda e: e.dma_start(out=y_p[t * 128:(t + 1) * 128, :], in_=y.t[:]), reads=[(y, 0)], dma=True)

    def bcast_row(name, src_row, npart=NS):
        b = A.alloc(name, [npart, D], F32)
        P.op("sp", lambda e: e.dma_start(out=b.t[:], in_=src_row.partition_broadcast(npart)), writes=[(b, 0)], dma=True)
        return b

    def transpose_s(src, c0, nchunk, name, dt=BF16):
        dst = A.alloc(name, [128, nchunk, NS], dt)
        idn = identb if dt == BF16 else ident
        for i in range(nchunk):
            if dt == BF16:
                o = lambda i=i: bank_bf(6)[:, i * 16:(i + 1) * 16]
            else:
                o = lambda i=i: bank(6)[:, i * 16:(i + 1) * 16]
            P.op("pe", lambda e, i=i, o=o: e.transpose(out=o(), in_=src.t[0:NS, c0 + i * 128: c0 + (i + 1) * 128],
                                                       identity=idn.t[0:NS, 0:NS]),
                 reads=[(src, 0), (idn, 0)], writes=[(psb[6], 0)])
        if dt == BF16:
            srcv = lambda: bank_bf(6)[:, 0:nchunk * 16].rearrange("p (k c) -> p k c", c=16)
        else:
            srcv = lambda: bank(6)[:, 0:nchunk * 16].rearrange("p (k c) -> p k c", c=16)
        P.op("dve", lambda e: e.tensor_copy(out=dst.t[:], in_=srcv()), reads=[(psb[6], 0)], writes=[(dst, 0)])
        return dst

    def sample_rstd(src_ap_fn, src_reads, n, scale_inv, name):
        ss = A.alloc(name, [NS, 2 * n], F32)
        junk = A.alloc(name + "j", [NS, D], F32)
        P.op("dve", lambda e: e.memset(ss.t[:], 0.0), writes=[(ss, 0)])
        w = D // n
        for i in range(n):
            P.op("act", lambda e, i=i: e.activation(out=junk.t[:, i * w:(i + 1) * w], in_=src_ap_fn(i * w, (i + 1) * w),
                                                    func=AF.Square, accum_out=ss.t[:, i:i + 1]),
                 reads=src_reads + [(ss, 0)], writes=[(junk, 0), (ss, 0)])
        P.op("act", lambda e: e.activation(out=ss.t[:, n:2 * n], in_=ss.t[:, 0:n], func=AF.Ln, scale=scale_inv, bias=EPS),
             reads=[(ss, 0)], writes=[(ss, 0)])
        P.op("act", lambda e: e.activation(out=ss.t[:, n:2 * n], in_=ss.t[:, n:2 * n], func=AF.Exp, scale=-0.5),
             reads=[(ss, 0)], writes=[(ss, 0)])
        A.free(junk)
        return ss

    def sample_norm(mod, nw_dram):
        nwb = bcast_row("nwb", nw_dram[0:1, :])
        ss = sample_rstd(lambda a, b: XS.t[:, a:b], [(XS, 0)], 1, 1.0 / D, "sss")
        tmpf = A.alloc("snt", [NS, D], F32)
        hs = A.alloc("hs", [NS, D], BF16)
        P.op("dve", lambda e: e.scalar_tensor_tensor(out=nwb.t[:], in0=mod.t[0:NS, D:2 * D], scalar=1.0, in1=nwb.t[:],
                                                      op0=ALU.add, op1=ALU.mult),
             reads=[(mod, 0), (nwb, 0)], writes=[(nwb, 0)])
        P.op("dve", lambda e: e.scalar_tensor_tensor(out=tmpf.t[:], in0=XS.t[:], scalar=ss.t[:, 1:2], in1=nwb.t[:],
                                                      op0=ALU.mult, op1=ALU.mult),
             reads=[(XS, 0), (ss, 0), (nwb, 0)], writes=[(tmpf, 0)])
        P.op("dve", lambda e: e.tensor_tensor(out=hs.t[:], in0=tmpf.t[:], in1=mod.t[0:NS, 0:D], op=ALU.add),
             reads=[(tmpf, 0), (mod, 0)], writes=[(hs, 0)])
        hsT = transpose_s(hs, 0, KC, "hsT")
        A.free(nwb); A.free(ss); A.free(tmpf); A.free(hs)
        return hsT

    def sample_gate(mod, name, scale):
        g = A.alloc(name, [NS, D], F32)
        P.op("dve", lambda e: e.tensor_scalar(out=g.t[:], in0=mod.t[0:NS, 2 * D:3 * D], scalar1=scale, scalar2=None, op0=ALU.mult),
             reads=[(mod, 0)], writes=[(g, 0)])
        return g

    def sample_proj(hsT, wt, ncol, pb):
        nk = hsT.t.shape[1]
        for kc in range(nk):
            P.op("pe", lambda e, kc=kc: e.matmul(bank(pb)[0:NS, 0:ncol], hsT.t[:, kc, :], wt.t[:, kc, :],
                                                 start=(kc == 0), stop=(kc == nk - 1)),
                 reads=[(hsT, 0), (wt, 0)], writes=[(psb[pb], 0)])

    def sample_residual(pb, gs, c0):
        tt = A.alloc("srt", [NS, 512], F32)
        P.op("dve", lambda e: e.tensor_tensor(out=tt.t[:], in0=bank(pb)[0:NS, :], in1=gs.t[:, c0:c0 + 512], op=ALU.mult),
             reads=[(psb[pb], 0), (gs, 0)], writes=[(tt, 0)])
        P.op("dve", lambda e: e.tensor_tensor(out=XS.t[:, c0:c0 + 512], in0=XS.t[:, c0:c0 + 512], in1=tt.t[:], op=ALU.add),
             reads=[(tt, 0), (XS, 0)], writes=[(XS, 0)])
        A.free(tt)

    def sample_token_mix(hsT, g2s, PROJT):
        walr_s = load_w(w_in3, OFF_ALR, 16, "walrs")
        wa2s = A.alloc("wa2s", [17, 512], BF16)
        P.op("pool", lambda e: e.dma_start(out=wa2s.t[:], in_=w_a2aug), writes=[(wa2s, 0)], dma=True)
        eye_m = A.alloc("eye_m", [128, NS, NS], F32)
        P.op("sp", lambda e: e.dma_start(out=eye_m.t[:].rearrange("p a b -> p (a b)"), in_=eye16_d[0:1, :].partition_broadcast(128)),
             writes=[(eye_m, 0)], dma=True)

        def tok_major(chunk0, nchunk, name, dt=F32):
            dst = A.alloc(name, [NS, nchunk * 128], dt)
            for g in range(0, nchunk, 4):
                pb = 2 + (g // 4) % 2
                for c in range(4):
                    P.op("pe", lambda e, g=g, c=c, pb=pb: e.transpose(out=bank(pb)[0:NS, c * 128:(c + 1) * 128],
                                                                    in_=PROJT.t[:, chunk0 + g + c, :], identity=ident.t[:]),
                         reads=[(PROJT, (chunk0 + g) // 4), (ident, 0)], writes=[(psb[pb], 0)])
                P.op("dve", lambda e, g=g, pb=pb: e.tensor_copy(out=dst.t[:, g * 128:(g + 4) * 128], in_=bank(pb)[0:NS, :]),
                     reads=[(psb[pb], 0)], writes=[(dst, 0)])
            return dst

        alrs = A.alloc("alrs", [17, NS], BF16)
        P.op("dve", lambda e: e.memset(alrs.t[:], 1.0), writes=[(alrs, 0)])
        for kc in range(KC):
            P.op("pe", lambda e, kc=kc: e.matmul(bank(6)[0:16, 0:NS], walr_s.t[:, kc, :], hsT.t[:, kc, :],
                                                 start=(kc == 0), stop=(kc == KC - 1)),
                 reads=[(walr_s, 0), (hsT, 0)], writes=[(psb[6], 0)])
        P.op("act", lambda e: e.activation(out=alrs.t[0:16, :], in_=bank(6)[0:16, 0:NS], func=AF.Copy),
             reads=[(psb[6], 0)], writes=[(alrs, 0)])
        dec = A.alloc("dec_s", [NS, 512], F32)
        P.op("pe", lambda e: e.matmul(bank(7)[0:NS, :], alrs.t[0:17, :], wa2s.t[0:17, :], start=True, stop=True),
             reads=[(alrs, 0), (wa2s, 0)], writes=[(psb[7], 0)])
        P.op("act", lambda e: e.activation(out=dec.t[:], in_=bank(7)[0:NS, :], func=AF.Exp, scale=-1.0),
             reads=[(psb[7], 0)], writes=[(dec, 0)])
        P.op("act", lambda e: e.activation(out=dec.t[:], in_=dec.t[:], func=AF.Ln, bias=1.0), reads=[(dec, 0)], writes=[(dec, 0)])
        P.op("act", lambda e: e.activation(out=dec.t[:], in_=dec.t[:], func=AF.Exp, scale=-1.0 / 16.0), reads=[(dec, 0)], writes=[(dec, 0)])
        A.free(walr_s); A.free(wa2s); A.free(alrs)

        ks = tok_major(4, 4, "ks")
        vs = tok_major(8, 8, "vs", BF16)
        aT = transpose_s(dec, 0, 4, "aT_s", F32)
        A.free(dec)
        qTm = A.alloc("qTm", [128, 4, NS, NS], BF16)
        for h in range(4):
            P.op("dve", lambda e, h=h: e.tensor_tensor(out=qTm.t[:, h, :, :],
                                                       in0=PROJT.t[:, h, :].unsqueeze(2).broadcast_to([128, NS, NS]),
                                                       in1=eye_m.t[:], op=ALU.mult),
                 reads=[(PROJT, 0), (eye_m, 0)], writes=[(qTm, h)])

        NB0 = 3
        S0 = [A.alloc(f"S0_{i}", [128, 4, 256], F32) for i in range(NB0)]
        S1 = [A.alloc(f"S1_{i}", [128, 4, 256], F32) for i in range(2)]
        km = [A.alloc(f"km{i}", [NS, 512], BF16) for i in range(2)]
        S1b = [A.alloc(f"S1b_{i}", [128, 4, 256], BF16) for i in range(2)]

        def load_state(b):
            s0 = S0[b % NB0]
            P.op("sp", lambda e: e.dma_start(out=s0.t[:], in_=state_s[b].rearrange("h k v -> k h v")),
                 writes=[(s0, 0)], dma=True)

        load_state(0)
        load_state(1)
        yield
        for b in range(NS):
            s0, s1, kb = S0[b % NB0], S1[b % 2], km[b % 2]
            P.op("dve", lambda e, b=b, kb=kb: e.tensor_scalar(out=kb.t[:], in0=ks.t[:], scalar1=ident.t[0:NS, b:b + 1],
                                                             scalar2=None, op0=ALU.mult),
                 reads=[(ks, 0), (ident, 0)], writes=[(kb, 0)])
            for h in range(4):
                P.op("pe", lambda e, h=h, kb=kb: e.matmul(bank(h // 2)[:, (h % 2) * 256:(h % 2) * 256 + 256], kb.t[:, h * 128:(h + 1) * 128],
                                                          vs.t[:, h * 256:(h + 1) * 256], start=True, stop=True),
                     reads=[(kb, 0), (vs, 0)], writes=[(psb[h // 2], 0)])
            for h in range(4):
                P.op("dve", lambda e, h=h, b=b, s0=s0, s1=s1: e.scalar_tensor_tensor(
                    out=s1.t[:, h, :], in0=s0.t[:, h, :], scalar=aT.t[:, h, b:b + 1], in1=bank(h // 2)[:, (h % 2) * 256:(h % 2) * 256 + 256],
                    op0=ALU.mult, op1=ALU.add),
                    reads=[(s0, 0), (aT, 0), (psb[h // 2], 0)], writes=[(s1, h)])
            s1b = S1b[b % 2]
            for h in range(4):
                P.op("act", lambda e, h=h, s1=s1, s1b=s1b: e.activation(out=s1b.t[:, h, :], in_=s1.t[:, h, :], func=AF.Copy),
                     reads=[(s1, h)], writes=[(s1b, h)])
            for h in range(4):
                ob = 4 + h
                P.op("pe", lambda e, h=h, b=b, s1b=s1b, ob=ob: e.matmul(
                    PS[0:NS, ob * 512: ob * 512 + 256], qTm.t[:, h, b, :], s1b.t[:, h, :],
                    start=(b == 0), stop=(b == NS - 1)),
                    reads=[(qTm, h), (s1b, h)], writes=[(psb[ob], 0)])
            if b + 2 < NS:
                load_state(b + 2)
            P.op("sp", lambda e, b=b, s1=s1: e.dma_start(out=S_s[b].rearrange("h k v -> k h v"), in_=s1.t[:]),
                 reads=[(s1, h) for h in range(4)], dma=True)
        for bf in S0 + S1 + km + S1b:
            A.free(bf)
        A.free(aT); A.free(qTm); A.free(eye_m); A.free(ks); A.free(vs)
        srs = tok_major(16, 8, "srs")
        tga = tok_major(24, 8, "tga")

        o_ap = lambda a, b_: PS[0:NS, (4 + a // 256) * 512: (4 + a // 256) * 512 + 256]
        rso = sample_rstd(o_ap, [(psb[4 + h], 0) for h in range(4)], 4, 1.0 / 256, "rsos")
        gwb = bcast_row("gwb", gla_nw[0:1, :])
        ogf = A.alloc("ogf", [NS, D], F32)
        ogs = A.alloc("ogs", [NS, D], BF16)
        for h in range(4):
            P.op("dve", lambda e, h=h: e.scalar_tensor_tensor(out=ogf.t[:, h * 256:(h + 1) * 256], in0=o_ap(h * 256, (h + 1) * 256),
                                                              scalar=rso.t[:, 4 + h:5 + h], in1=gwb.t[:, h * 256:(h + 1) * 256],
                                                              op0=ALU.mult, op1=ALU.mult),
                 reads=[(psb[4 + h], 0), (rso, 0), (gwb, 0)], writes=[(ogf, 0)])
        P.op("dve", lambda e: e.tensor_tensor(out=ogs.t[:], in0=ogf.t[:], in1=srs.t[:], op=ALU.mult),
             reads=[(ogf, 0), (srs, 0)], writes=[(ogs, 0)])
        ogT = transpose_s(ogs, 0, KC, "ogT")
        A.free(rso); A.free(gwb); A.free(ogf); A.free(ogs); A.free(srs)

        mrg = A.alloc("mrg", [NS, D], F32)
        for c in range(0, D, 512):
            wt = load_w(w_pa3, c, 512, "wsmp")
            sample_proj(ogT, wt, 512, 6)
            P.op("dve", lambda e, c=c: e.scalar_tensor_tensor(out=mrg.t[:, c:c + 512], in0=tga.t[:, c:c + 512], scalar=1.0,
                                                              in1=bank(6)[0:NS, :], op0=ALU.add, op1=ALU.mult),
                 reads=[(tga, 0), (psb[6], 0)], writes=[(mrg, 0)])
            A.free(wt)
        A.free(ogT); A.free(tga)

        tgb = tok_major(32, 8, "tgb")
        us = tok_major(40, 8, "us")
        gg = tok_major(48, 8, "ggs")
        st6 = A.alloc("sbn", [NS, 2, 6], F32)
        mvs = A.alloc("smv", [NS, 4], F32)
        for c in range(2):
            P.op("dve", lambda e, c=c: e.bn_stats(out=st6.t[:, c, :], in_=gg.t[:, c * 512:(c + 1) * 512]),
                 reads=[(gg, 0)], writes=[(st6, c)])
        P.op("dve", lambda e: e.bn_aggr(out=mvs.t[:, 0:2], in_=st6.t[:]), reads=[(st6, 0), (st6, 1)], writes=[(mvs, 0)])
        P.op("act", lambda e: e.activation(out=mvs.t[:, 2:3], in_=mvs.t[:, 1:2], func=AF.Ln, bias=EPS), reads=[(mvs, 0)], writes=[(mvs, 0)])
        P.op("act", lambda e: e.activation(out=mvs.t[:, 2:3], in_=mvs.t[:, 2:3], func=AF.Exp, scale=-0.5), reads=[(mvs, 0)], writes=[(mvs, 0)])
        lwb = bcast_row("lwb", gm_lnw[0:1, :])
        lbb = bcast_row("lbb", gm_lnb[0:1, :])
        wsb = bcast_row("wsb", ws00_d[0:1, :])
        bsb = bcast_row("bsb", bs0_d[0:1, :])
        P.op("dve", lambda e: e.tensor_scalar(out=gg.t[:], in0=gg.t[:], scalar1=mvs.t[:, 0:1], scalar2=mvs.t[:, 2:3],
                                              op0=ALU.subtract, op1=ALU.mult),
             reads=[(gg, 0), (mvs, 0)], writes=[(gg, 0)])
        P.op("dve", lambda e: e.tensor_tensor(out=gg.t[:], in0=gg.t[:], in1=lwb.t[:], op=ALU.mult), reads=[(gg, 0), (lwb, 0)], writes=[(gg, 0)])
        P.op("dve", lambda e: e.tensor_tensor(out=gg.t[:], in0=gg.t[:], in1=lbb.t[:], op=ALU.add), reads=[(gg, 0), (lbb, 0)], writes=[(gg, 0)])
        P.op("sp", lambda e: e.dma_start(out=gv_s, in_=gg.t[:]), reads=[(gg, 0)], dma=True)
        sgf = A.alloc("sgf", [NS, D], F32)
        sgs = A.alloc("sgs", [NS, D], BF16)
        P.op("dve", lambda e: e.tensor_tensor(out=sgf.t[:], in0=gg.t[:], in1=wsb.t[:], op=ALU.mult), reads=[(gg, 0), (wsb, 0)], writes=[(sgf, 0)])
        P.op("dve", lambda e: e.tensor_tensor(out=sgf.t[:], in0=sgf.t[:], in1=bsb.t[:], op=ALU.add), reads=[(sgf, 0), (bsb, 0)], writes=[(sgf, 0)])
        P.op("dve", lambda e: e.tensor_tensor(out=sgs.t[:], in0=sgf.t[:], in1=us.t[:], op=ALU.mult), reads=[(sgf, 0), (us, 0)], writes=[(sgs, 0)])
        sgT = transpose_s(sgs, 0, KC, "sgT")
        for bf in (us, gg, st6, mvs, lwb, lbb, wsb, bsb, sgf, sgs):
            A.free(bf)
        mrb = A.alloc("mrb", [NS, D], BF16)
        tmm = A.alloc("tmm", [NS, 512], F32)
        for c in range(0, D, 512):
            wt = load_w(w_pb3, c, 512, "wsmp")
            sample_proj(sgT, wt, 512, 6)
            P.op("dve", lambda e, c=c: e.scalar_tensor_tensor(out=tmm.t[:], in0=tgb.t[:, c:c + 512], scalar=1.0,
                                                              in1=bank(6)[0:NS, :], op0=ALU.add, op1=ALU.mult),
                 reads=[(tgb, 0), (psb[6], 0)], writes=[(tmm, 0)])
            P.op("dve", lambda e, c=c: e.tensor_tensor(out=mrb.t[:, c:c + 512], in0=mrg.t[:, c:c + 512], in1=tmm.t[:], op=ALU.add),
                 reads=[(mrg, 0), (tmm, 0)], writes=[(mrb, 0)])
            A.free(wt)
        mT = transpose_s(mrb, 0, KC, "mT_s")
        A.free(sgT); A.free(tgb); A.free(mrg); A.free(mrb); A.free(tmm)
        for c in range(0, D, 512):
            wt = load_w(w_o3, c, 512, "wsmp")
            sample_proj(mT, wt, 512, 6)
            sample_residual(6, g2s, c)
            A.free(wt)
        A.free(mT); A.free(g2s); A.free(hsT)

    def final_sample():
        nfs = bcast_row("nfs", normf_w[0:1, :])
        ss = sample_rstd(lambda a, b: XS.t[:, a:b], [(XS, 0)], 1, 1.0 / D, "fss")
        ys = A.alloc("ys", [NS, D], F32)
        P.op("dve", lambda e: e.scalar_tensor_tensor(out=ys.t[:], in0=XS.t[:], scalar=ss.t[:, 1:2], in1=nfs.t[:],
                                                      op0=ALU.mult, op1=ALU.mult),
             reads=[(XS, 0), (ss, 0), (nfs, 0)], writes=[(ys, 0)])
        P.op("sp", lambda e: e.dma_start(out=y_s, in_=ys.t[:]), reads=[(ys, 0)], dma=True)

    w_in3 = w_in.rearrange("(k p) n -> p k n", p=128)
    w_pa3 = w_pa.rearrange("(k p) n -> p k n", p=128)
    w_pb3 = w_pb.rearrange("(k p) n -> p k n", p=128)
    w_o3 = w_o.rearrange("(k p) n -> p k n", p=128)
    OFF_Q, OFF_K, OFF_V, OFF_R, OFF_ALR, OFF_U, OFF_GV, OFF_GA, OFF_GB = 0, 512, 1024, 2048, 3072, 3088, 4112, 5136, 6160
    LN_QSCALE = float(np.log(128.0 ** -0.5))

    NSCR = 24
    wscr = nc.dram_tensor("wscr", [NSCR, 128, KC, 512], BF16).ap()
    SCR = Buf("wscr")
    WCACHE = {}

    def load_w(src3, c0, ncol, name="w", nk=KC):
        wt = ring.alloc(name, [128, nk, ncol], BF16)
        key = (id(src3), c0)
        if ncol == 512 and nk == KC and key in WCACHE:
            tid = WCACHE[key]
            P.op("sp", lambda e: e.dma_start(out=wt.t[:], in_=wscr[tid]), reads=[(SCR, tid)], writes=[(wt, 0)], dma=True)
            return wt
        P.op("pool", lambda e: e.dma_start(out=wt.t[:], in_=src3[:, :, c0:c0 + ncol]), writes=[(wt, 0)], dma=True)
        if ncol == 512 and nk == KC and len(WCACHE) < NSCR:
            tid = len(WCACHE)
            WCACHE[key] = tid
            P.op("sp", lambda e: e.dma_start(out=wscr[tid], in_=wt.t[:]), reads=[(wt, 0)], writes=[(SCR, tid)], dma=True)
        return wt

    SHOOK = {}

    def sample_hook(wt, chunk0, func, fscale, pbk):
        if not SHOOK:
            return
        hsT, PROJT = SHOOK["hsT"], SHOOK["PROJT"]
        for c in range(4):
            for kc in range(KC):
                P.op("pe", lambda e, c=c, kc=kc: e.matmul(bank(pbk)[:, c * 16:(c + 1) * 16], wt.t[:, kc, c * 128:(c + 1) * 128],
                                                          hsT.t[:, kc, :], start=(kc == 0), stop=(kc == KC - 1)),
                     reads=[(wt, 0), (hsT, 0)], writes=[(psb[pbk], 0)])
        P.op("act", lambda e: e.activation(out=PROJT.t[:, chunk0:chunk0 + 4, :],
                                           in_=bank(pbk)[:, 0:64].rearrange("p (c t) -> p c t", t=16), func=func, scale=fscale),
             reads=[(psb[pbk], 0)], writes=[(PROJT, chunk0 // 4)])

    def tm_consts(modT2):
        C = {}
        tri = A.alloc("tri", [128, 128], F32)
        tri64 = A.alloc("tri64", [128, 128], F32)
        ucum = A.alloc("ucum", [128, 128], F32)
        onesf = A.alloc("onesf", [128, 128], F32)
        wsTf = A.alloc("wsTf", [128, 4, 128], F32)
        wsTb = A.alloc("wsTb", [128, 4, 128], BF16)
        BSb = A.alloc("BSb", [128, 512], F32)
        Bblk = A.alloc("Bblk", [128, 8, 128], F32)
        walr = A.alloc("walr", [128, KC, 16], BF16)
        wa2 = A.alloc("wa2", [17, 512], BF16)
        alrT = A.alloc("alrT", [17, TBS], BF16)
        elast = A.alloc("elast", [128, 4, 32], F32)
        S32 = A.alloc("S32", [128, 4, 256], F32)
        Sbf = A.alloc("Sbf", [128, 4, 256], BF16)
        for (b, d_) in ((tri, tri_d), (tri64, tri64_d), (ucum, ucum_d), (wsTf, wsT_d)):
            P.op("sp", lambda e, b=b, d_=d_: e.dma_start(out=b.t[:], in_=d_), writes=[(b, 0)], dma=True)
        P.op("sp", lambda e: e.dma_start(out=BSb.t[:], in_=bs_d[0:1, :].partition_broadcast(128)), writes=[(BSb, 0)], dma=True)
        P.op("pool", lambda e: e.dma_start(out=walr.t[:], in_=w_in3[:, :, OFF_ALR:OFF_ALR + 16]), writes=[(walr, 0)], dma=True)
        P.op("pool", lambda e: e.dma_start(out=wa2.t[:], in_=w_a2aug), writes=[(wa2, 0)], dma=True)
        P.op("dve", lambda e: e.memset(onesf.t[:], 1.0), writes=[(onesf, 0)])
        P.op("dve", lambda e: e.memset(alrT.t[:], 1.0), writes=[(alrT, 0)])
        P.op("dve", lambda e: e.memset(S32.t[:], 0.0), writes=[(S32, h) for h in range(4)])
        P.op("dve", lambda e: e.memset(Sbf.t[:], 0.0), writes=[(Sbf, h) for h in range(4)])
        P.op("dve", lambda e: e.tensor_tensor(out=wsTf.t[:], in0=wsTf.t[:],
                                              in1=tri.t[:].unsqueeze(1).broadcast_to([128, 4, 128]), op=ALU.mult),
             reads=[(wsTf, 0), (tri, 0)], writes=[(wsTf, 0)])
        P.op("dve", lambda e: e.tensor_copy(out=wsTb.t[:], in_=wsTf.t[:]), reads=[(wsTf, 0)], writes=[(wsTb, 0)])
        P.op("pe", lambda e: e.matmul(bank(7), onesf.t[:], wsTf.t[:].rearrange("p g t -> p (g t)"), start=True, stop=True),
             reads=[(onesf, 0), (wsTf, 0)], writes=[(psb[7], 0)])
        for cb in range(8):
            g = cb // 2
            P.op("dve", lambda e, cb=cb, g=g: e.scalar_tensor_tensor(
                out=Bblk.t[:, cb, :], in0=bank(7)[:, g * 128:(g + 1) * 128], scalar=modT2.t[:, cb, 18:19],
                in1=BSb.t[:, g * 128:(g + 1) * 128], op0=ALU.mult, op1=ALU.add),
                reads=[(psb[7], 0), (modT2, 0), (BSb, 0)], writes=[(Bblk, cb)])
        A.free(tri); A.free(onesf); A.free(wsTf); A.free(BSb)
        C.update(tri64=tri64, ucum=ucum, wsTb=wsTb, Bblk=Bblk, walr=walr, wa2=wa2, alrT=alrT, elast=elast, S32=S32, Sbf=Sbf)
        return C

    def proj_fm(wt, col0, rhs, rhs_ap, rkeys, pb):
        for kc in range(KC):
            P.op("pe", lambda e, kc=kc: e.matmul(bank(pb), wt.t[:, kc, col0:col0 + 128], rhs_ap(kc),
                                                 start=(kc == 0), stop=(kc == KC - 1)),
                 reads=[(wt, 0), (rhs, rkeys(kc))], writes=[(psb[pb], 0)])

    def token_mix_block(blk, C, modT2, cols2, g2h, HTB, pre_out_hook=None, unit_hook=None):
        tri64, ucum, wsTb, Bblk = C["tri64"], C["ucum"], C["wsTb"], C["Bblk"]
        walr, wa2, alrT, elast, S32, Sbf = C["walr"], C["wa2"], C["alrT"], C["elast"], C["S32"], C["Sbf"]
        hap = lambda kc: HTB.t[:, kc, :]
        hkey = lambda kc: (kc, blk)

        sp = A.alloc("sp_tok", [128, 4, 512], F32)
        QT = A.alloc("QT", [128, 4, TBS], BF16)
        KT = A.alloc("KT", [128, 4, TBS], BF16)
        Eq = [A.alloc(f"Eq{i}", [128, TBS], F32) for i in range(2)]
        Ek = [A.alloc(f"Ek{i}", [128, TBS], F32) for i in range(2)]
        KTOK = A.alloc("KTOK", [128, 4, 512], BF16)

        def d_gen():
            for kc in range(KC):
                P.op("pe", lambda e, kc=kc: e.matmul(bank(6)[0:16, :], walr.t[:, kc, :], HTB.t[:, kc, :],
                                                     start=(kc == 0), stop=(kc == KC - 1)),
                     reads=[(walr, 0), (HTB, (kc, blk))], writes=[(psb[6], 0)])
            P.op("act", lambda e: e.activation(out=alrT.t[0:16, :], in_=bank(6)[0:16, :], func=AF.Copy),
                 reads=[(psb[6], 0)], writes=[(alrT, 0)])
            yield
            for j in range(4):
                pb = j % 2
                P.op("pe", lambda e, j=j, pb=pb: e.matmul(bank(pb), alrT.t[0:17, j * 128:(j + 1) * 128], wa2.t[0:17, :],
                                                          start=True, stop=True),
                     reads=[(alrT, 0), (wa2, 0)], writes=[(psb[pb], 0)])
                P.op("act", lambda e, j=j, pb=pb: e.activation(out=sp.t[:, j, :], in_=bank(pb), func=AF.Exp, scale=-1.0),
                     reads=[(psb[pb], 0)], writes=[(sp, j)])
                P.op("act", lambda e, j=j: e.activation(out=sp.t[:, j, :], in_=sp.t[:, j, :], func=AF.Ln, bias=1.0),
                     reads=[(sp, j)], writes=[(sp, j)])
                yield

            wq = load_w(w_in3, OFF_Q, 512, "wq")
            wk = load_w(w_in3, OFF_K, 512, "wk")
            if blk == NTB - 1:
                sample_hook(wq, 0, AF.Identity, 128.0 ** -0.5, 7)
                sample_hook(wk, 4, AF.Identity, 1.0, 7)
            for h in range(4):
                pbc = 2 + h % 2
                for j in range(4):
                    P.op("pe", lambda e, j=j, h=h, pbc=pbc: e.matmul(bank(pbc)[:, j * 128:(j + 1) * 128],
                                                                    sp.t[:, j, h * 128:(h + 1) * 128], ucum.t[:],
                                                                    start=True, stop=True),
                         reads=[(sp, j), (ucum, 0)], writes=[(psb[pbc], 0)])
                eq, ek = Eq[h % 2], Ek[h % 2]
                P.op("act", lambda e, eq=eq, pbc=pbc: e.activation(out=eq.t[:], in_=bank(pbc), func=AF.Exp, bias=LN_QSCALE),
                     reads=[(psb[pbc], 0)], writes=[(eq, 0)])
                P.op("act", lambda e, ek=ek, pbc=pbc: e.activation(out=ek.t[:], in_=bank(pbc), func=AF.Exp, scale=-1.0),
                     reads=[(psb[pbc], 0)], writes=[(ek, 0)])
                P.op("act", lambda e, h=h, pbc=pbc: e.activation(
                    out=elast.t[:, h, blk * 8:(blk + 1) * 8],
                    in_=bank(pbc).rearrange("p (c t) -> p c t", t=64)[:, :, 63], func=AF.Exp),
                    reads=[(psb[pbc], 0)], writes=[(elast, (h, blk))])
                yield
                proj_fm(wq, h * 128, HTB, hap, hkey, 0)
                P.op("dve", lambda e, h=h, eq=eq: e.tensor_tensor(out=QT.t[:, h, :], in0=bank(0), in1=eq.t[:], op=ALU.mult),
                     reads=[(psb[0], 0), (eq, 0)], writes=[(QT, h)])
                yield
                proj_fm(wk, h * 128, HTB, hap, hkey, 1)
                P.op("dve", lambda e, h=h, ek=ek: e.tensor_tensor(out=KT.t[:, h, :], in0=bank(1), in1=ek.t[:], op=ALU.mult),
                     reads=[(psb[1], 0), (ek, 0)], writes=[(KT, h)])
                yield
            A.free(wq); A.free(wk); A.free(sp)
            for b in Eq + Ek:
                A.free(b)

            for j in range(4):
                for h in range(4):
                    P.op("pe", lambda e, j=j, h=h: e.transpose(out=bank_bf(3)[:, h * 128:(h + 1) * 128],
                                                               in_=KT.t[:, h, j * 128:(j + 1) * 128], identity=identb.t[:]),
                         reads=[(KT, h), (identb, 0)], writes=[(psb[3], 0)])
                P.op("dve", lambda e, j=j: e.tensor_copy(out=KTOK.t[:, j, :], in_=bank_bf(3)[:, 0:512]),
                     reads=[(psb[3], 0)], writes=[(KTOK, j)])
                yield


        VTOK = A.alloc("VTOK", [128, 4, 1024], BF16)

        def v_gen():
            n = 0
            for cb in range(2):
                wvt = load_w(w_in3, OFF_V + cb * 512, 512, "wv")
                if blk == NTB - 1:
                    sample_hook(wvt, 8 + cb * 4, AF.Identity, 1.0, 7)
                for j in range(4):
                    pb = 4 + n % 2
                    for kc in range(KC):
                        P.op("pe", lambda e, j=j, kc=kc, pb=pb, wvt=wvt: e.matmul(
                            bank(pb), HTB.t[:, kc, j * 128:(j + 1) * 128], wvt.t[:, kc, :],
                            start=(kc == 0), stop=(kc == KC - 1)),
                            reads=[(HTB, (kc, blk)), (wvt, 0)], writes=[(psb[pb], 0)])
                    if n % 2 == 0:
                        P.op("act", lambda e, j=j, cb=cb, pb=pb: e.activation(out=VTOK.t[:, j, cb * 512:(cb + 1) * 512],
                                                                             in_=bank(pb), func=AF.Copy),
                             reads=[(psb[pb], 0)], writes=[(VTOK, (j, cb))])
                    else:
                        P.op("dve", lambda e, j=j, cb=cb, pb=pb: e.tensor_copy(out=VTOK.t[:, j, cb * 512:(cb + 1) * 512],
                                                                              in_=bank(pb)),
                             reads=[(psb[pb], 0)], writes=[(VTOK, (j, cb))])
                    n += 1
                    yield
                A.free(wvt)

        vg = v_gen()
        for _ in d_gen():
            next(vg, None)
        for _ in vg:
            pass

        O32 = A.alloc("O32", [128, 8, TBS], F32)
        sTb = [A.alloc(f"sT{h}", [128, 128], BF16) for h in range(4)]
        tmpS = [A.alloc(f"tmpS{h}", [128, 256], F32) for h in range(4)]
        def scan_gen():
            sc_ps = lambda h: PS[:, h * 512 + 256: h * 512 + 384]
            o_ps = lambda h, vb, c: PS[:, h * 512 + vb * 128 + c * 64: h * 512 + vb * 128 + c * 64 + 64]
            dS = lambda h: PS[:, (4 + h // 2) * 512 + (h % 2) * 256: (4 + h // 2) * 512 + (h % 2) * 256 + 256]
            for j in range(4):
                for h in range(4):
                    P.op("pe", lambda e, j=j, h=h: e.matmul(sc_ps(h), KT.t[:, h, j * 128:(j + 1) * 128],
                                                            QT.t[:, h, j * 128:(j + 1) * 128], start=True, stop=True),
                         reads=[(KT, h), (QT, h)], writes=[(psb[h], 0)])
                for h in range(4):
                    P.op("dve", lambda e, h=h: e.tensor_tensor(out=sTb[h].t[:], in0=sc_ps(h), in1=tri64.t[:], op=ALU.mult),
                         reads=[(psb[h], 0), (tri64, 0)], writes=[(sTb[h], 0)])
                yield
                for c in range(2):
                    chunk = blk * 8 + j * 2 + c
                    for h in range(4):
                        for vb in range(2):
                            P.op("pe", lambda e, j=j, h=h, vb=vb, c=c: e.matmul(
                                o_ps(h, vb, c), VTOK.t[:, j, h * 256 + vb * 128: h * 256 + (vb + 1) * 128],
                                sTb[h].t[:, c * 64:(c + 1) * 64], start=True, stop=False),
                                reads=[(VTOK, (j, h // 2)), (sTb[h], 0)], writes=[(psb[h], 0)])
                            P.op("pe", lambda e, j=j, h=h, vb=vb, c=c: e.matmul(
                                o_ps(h, vb, c), Sbf.t[:, h, vb * 128:(vb + 1) * 128],
                                QT.t[:, h, j * 128 + c * 64: j * 128 + (c + 1) * 64], start=False, stop=True),
                                reads=[(Sbf, h), (QT, h)], writes=[(psb[h], 0)])
                        P.op("pe", lambda e, j=j, h=h, c=c: e.matmul(
                            dS(h), KTOK.t[c * 64:(c + 1) * 64, j, h * 128:(h + 1) * 128],
                            VTOK.t[c * 64:(c + 1) * 64, j, h * 256:(h + 1) * 256], start=True, stop=True),
                            reads=[(KTOK, j), (VTOK, (j, h // 2))], writes=[(psb[4 + h // 2], 0)])
                    for h in range(4):
                        ts = tmpS[h]
                        P.op("dve", lambda e, ts=ts, h=h: e.tensor_tensor(out=ts.t[:], in0=dS(h), in1=S32.t[:, h, :], op=ALU.add),
                             reads=[(psb[4 + h // 2], 0), (S32, h)], writes=[(ts, 0)])
                        P.op("act", lambda e, ts=ts, h=h, chunk=chunk: e.activation(
                            out=Sbf.t[:, h, :], in_=ts.t[:], func=AF.Identity, scale=elast.t[:, h, chunk:chunk + 1]),
                            reads=[(ts, 0), (elast, (h, blk))], writes=[(Sbf, h)])
                        P.op("act", lambda e, ts=ts, h=h, chunk=chunk: e.activation(
                            out=S32.t[:, h, :], in_=ts.t[:], func=AF.Identity, scale=elast.t[:, h, chunk:chunk + 1]),
                            reads=[(ts, 0), (elast, (h, blk))], writes=[(S32, h)])
                    yield
                for h in range(4):
                    P.op("act", lambda e, j=j, h=h: e.activation(
                        out=O32.t[:, h * 2:(h + 1) * 2, j * 128:(j + 1) * 128],
                        in_=PS[:, h * 512: h * 512 + 256].rearrange("p (v t) -> p v t", t=128), func=AF.Copy),
                        reads=[(psb[h], 0)], writes=[(O32, (h, j))])
                yield
        SR = A.alloc("SR", [128, 8, TBS], BF16)
        TGA = A.alloc("TGA", [128, 8, TBS], BF16)
        UT = A.alloc("UT", [128, 8, TBS], BF16)

        def filler_gen():
            for (off, dst, func, fscale) in ((OFF_R, SR, AF.Silu, 1.0), (OFF_GA, TGA, AF.Tanh, 0.5), (OFF_U, UT, AF.Gelu, 1.0)):
                for half in range(2):
                    wt = load_w(w_in3, off + half * 512, 512, "wfill")
                    if blk == NTB - 1:
                        sample_hook(wt, {OFF_R: 16, OFF_GA: 24, OFF_U: 40}[off] + half * 4, func, fscale, 6 + half)
                    for dd in range(4):
                        d = half * 4 + dd
                        pb = 6 + d % 2
                        proj_fm(wt, dd * 128, HTB, hap, hkey, pb)
                        P.op("act", lambda e, d=d, pb=pb, dst=dst, func=func, fscale=fscale: e.activation(
                            out=dst.t[:, d, :], in_=bank(pb), func=func, scale=fscale),
                            reads=[(psb[pb], 0)], writes=[(dst, d)])
                        yield
                    A.free(wt)

        fg = filler_gen()
        nstep = 0
        for _ in scan_gen():
            nstep += 1
            for _k in range(2 if nstep % 2 == 1 else 1):
                next(fg, None)
        for _ in fg:
            pass
        for b in sTb + tmpS:
            A.free(b)
        A.free(QT); A.free(KT); A.free(KTOK); A.free(VTOK)

        sqo = A.alloc("sqo", [128, 2, TBS], BF16)
        rso = [A.alloc(f"rso{h}", [128, TBS], F32) for h in range(4)]
        for h in range(4):
            for vb in range(2):
                P.op("dve", lambda e, h=h, vb=vb: e.tensor_tensor(out=sqo.t[:, vb, :], in0=O32.t[:, h * 2 + vb, :],
                                                                 in1=O32.t[:, h * 2 + vb, :], op=ALU.mult),
                     reads=[(O32, (h, j)) for j in range(4)], writes=[(sqo, vb)])
            for vb in range(2):
                P.op("pe", lambda e, vb=vb: e.matmul(bank(7), onesb.t[:], sqo.t[:, vb, :], start=(vb == 0), stop=(vb == 1)),
                     reads=[(onesb, 0), (sqo, vb)], writes=[(psb[7], 0)])
            P.op("act", lambda e, h=h: e.activation(out=rso[h].t[:], in_=bank(7), func=AF.Ln, scale=1.0 / 256, bias=EPS),
                 reads=[(psb[7], 0)], writes=[(rso[h], 0)])
            P.op("act", lambda e, h=h: e.activation(out=rso[h].t[:], in_=rso[h].t[:], func=AF.Exp, scale=-0.5),
                 reads=[(rso[h], 0)], writes=[(rso[h], 0)])
        A.free(sqo)
        TGB = A.alloc("TGB", [128, 8, TBS], BF16)
        for half in range(2):
            wt = load_w(w_in3, OFF_GB + half * 512, 512, "wgb")
            if blk == NTB - 1:
                sample_hook(wt, 32 + half * 4, AF.Tanh, 0.5, 7)
            for dd in range(4):
                d = half * 4 + dd
                pb = d % 2
                proj_fm(wt, dd * 128, HTB, hap, hkey, pb)
                P.op("act", lambda e, d=d, pb=pb: e.activation(out=TGB.t[:, d, :], in_=bank(pb), func=AF.Tanh, scale=0.5),
                     reads=[(psb[pb], 0)], writes=[(TGB, d)])
            A.free(wt)
        OG = SR
        t1 = [A.alloc(f"t1{i}", [128, TBS], F32) for i in range(2)]
        for d in range(8):
            h = d // 2
            pb = d % 2
            P.op("dve", lambda e, d=d, h=h, pb=pb: e.scalar_tensor_tensor(
                out=t1[pb].t[:], in0=O32.t[:, d, :], scalar=modT2.t[:, 8 + d, 17:18], in1=rso[h].t[:],
                op0=ALU.mult, op1=ALU.mult),
                reads=[(O32, (h, j)) for j in range(4)] + [(modT2, 0), (rso[h], 0)], writes=[(t1[pb], 0)])
            P.op("dve", lambda e, d=d, pb=pb: e.tensor_tensor(out=OG.t[:, d, :], in0=t1[pb].t[:], in1=SR.t[:, d, :], op=ALU.mult),
                 reads=[(t1[pb], 0), (SR, d)], writes=[(OG, d)])
        A.free(O32)
        for b in rso:
            A.free(b)

        MA = A.alloc("MA", [128, 8, TBS], BF16)
        tg = []

        def gated_proj(w3, src, skeys, goff, accumulate, pre=None, unit_hook=None):
            for half in range(2):
                wp = load_w(w3, half * 512, 512, "wp")
                wg = load_w(w_in3, goff + half * 512, 512, "wg") if pre is None else None
                for dd in range(4):
                    d = half * 4 + dd
                    pb = d % 2
                    proj_fm(wp, dd * 128, src, lambda kc: src.t[:, kc, :], skeys, pb)
                    if pre is None:
                        proj_fm(wg, dd * 128, HTB, hap, hkey, 2 + pb)
                        P.op("act", lambda e, pb=pb: e.activation(out=tg[pb].t[:], in_=bank(2 + pb), func=AF.Tanh, scale=0.5),
                             reads=[(psb[2 + pb], 0)], writes=[(tg[pb], 0)])
                    if pre is not None and accumulate:
                        P.op("dve", lambda e, d=d, pb=pb: e.scalar_tensor_tensor(
                            out=t1[pb].t[:], in0=pre.t[:, d, :], scalar=1.0, in1=bank(pb), op0=ALU.add, op1=ALU.mult),
                            reads=[(pre, d), (psb[pb], 0)], writes=[(t1[pb], 0)])
                        P.op("dve", lambda e, d=d, pb=pb: e.tensor_tensor(out=MA.t[:, d, :], in0=MA.t[:, d, :], in1=t1[pb].t[:], op=ALU.add),
                             reads=[(t1[pb], 0), (MA, d)], writes=[(MA, d)])
                    elif pre is not None:
                        P.op("dve", lambda e, d=d, pb=pb: e.scalar_tensor_tensor(
                            out=MA.t[:, d, :], in0=pre.t[:, d, :], scalar=1.0, in1=bank(pb), op0=ALU.add, op1=ALU.mult),
                            reads=[(pre, d), (psb[pb], 0)], writes=[(MA, d)])
                    elif not accumulate:
                        P.op("dve", lambda e, d=d, pb=pb: e.scalar_tensor_tensor(
                            out=MA.t[:, d, :], in0=tg[pb].t[:], scalar=1.0, in1=bank(pb), op0=ALU.add, op1=ALU.mult),
                            reads=[(tg[pb], 0), (psb[pb], 0)], writes=[(MA, d)])
                    else:
                        P.op("dve", lambda e, d=d, pb=pb: e.scalar_tensor_tensor(
                            out=t1[pb].t[:], in0=tg[pb].t[:], scalar=1.0, in1=bank(pb), op0=ALU.add, op1=ALU.mult),
                            reads=[(tg[pb], 0), (psb[pb], 0)], writes=[(t1[pb], 0)])
                        P.op("dve", lambda e, d=d, pb=pb: e.tensor_tensor(out=MA.t[:, d, :], in0=MA.t[:, d, :], in1=t1[pb].t[:], op=ALU.add),
                             reads=[(t1[pb], 0), (MA, d)], writes=[(MA, d)])
                    if accumulate and unit_hook is not None:
                        unit_hook()
                A.free(wp)
                if wg is not None:
                    A.free(wg)

        GG = A.alloc("GG", [128, 4, 1024], F32)
        stats = A.alloc("bnst", [128, 4, 2, 6], F32)
        mv = A.alloc("bnmv", [128, 4, 2], F32)
        rs = A.alloc("bnrs", [128, 4], F32)
        wgv = [load_w(w_in3, OFF_GV + cb * 512, 512, "wgv") for cb in range(2)]
        if blk == NTB - 1:
            for cb in range(2):
                sample_hook(wgv[cb], 48 + cb * 4, AF.Gelu, 1.0, 7)
        n = 0
        for j in range(4):
            for cb in range(2):
                pb = n % 2
                for kc in range(KC):
                    P.op("pe", lambda e, j=j, cb=cb, kc=kc, pb=pb: e.matmul(
                        bank(pb), HTB.t[:, kc, j * 128:(j + 1) * 128], wgv[cb].t[:, kc, :],
                        start=(kc == 0), stop=(kc == KC - 1)),
                        reads=[(HTB, (kc, blk)), (wgv[cb], 0)], writes=[(psb[pb], 0)])
                P.op("act", lambda e, j=j, cb=cb, pb=pb: e.activation(out=GG.t[:, j, cb * 512:(cb + 1) * 512], in_=bank(pb), func=AF.Gelu),
                     reads=[(psb[pb], 0)], writes=[(GG, (j, cb))])
                P.op("dve", lambda e, j=j, cb=cb: e.bn_stats(out=stats.t[:, j, cb, :], in_=GG.t[:, j, cb * 512:(cb + 1) * 512]),
                     reads=[(GG, (j, cb))], writes=[(stats, (j, cb))])
                n += 1
            P.op("dve", lambda e, j=j: e.bn_aggr(out=mv.t[:, j, :], in_=stats.t[:, j, :, :]),
                 reads=[(stats, (j, 0)), (stats, (j, 1))], writes=[(mv, j)])
        for b in wgv:
            A.free(b)
        gated_proj(w_pa3, OG, lambda kc: kc, OFF_GA, False, pre=TGA)
        A.free(OG); A.free(TGA)

        ZT = A.alloc("ZT", [128, 4, 1024], BF16)
        P.op("act", lambda e: e.activation(out=rs.t[:], in_=mv.t[:, :, 1], func=AF.Ln, bias=EPS),
             reads=[(mv, j) for j in range(4)], writes=[(rs, 0)])
        P.op("act", lambda e: e.activation(out=rs.t[:], in_=rs.t[:], func=AF.Exp, scale=-0.5),
             reads=[(rs, 0)], writes=[(rs, 0)])
        nb_ = A.alloc("bnnb", [128, 4], F32)
        P.op("dve", lambda e: e.scalar_tensor_tensor(out=nb_.t[:], in0=mv.t[:, :, 0], scalar=-1.0, in1=rs.t[:],
                                                      op0=ALU.mult, op1=ALU.mult),
             reads=[(mv, j) for j in range(4)] + [(rs, 0)], writes=[(nb_, 0)])
        for j in range(4):
            P.op("act", lambda e, j=j: e.activation(out=ZT.t[:, j, :], in_=GG.t[:, j, :], func=AF.Identity,
                                                    scale=rs.t[:, j:j + 1], bias=nb_.t[:, j:j + 1]),
                 reads=[(GG, (j, 0)), (GG, (j, 1)), (nb_, 0), (rs, 0)], writes=[(ZT, j)])
        A.free(nb_)
        A.free(GG); A.free(stats)
        for cb in range(8):
            g = cb // 2
            pb = cb % 2
            for j in range(4):
                P.op("pe", lambda e, j=j, cb=cb, g=g, pb=pb: e.matmul(bank(pb)[:, j * 128:(j + 1) * 128],
                                                                   ZT.t[:, j, cb * 128:(cb + 1) * 128], wsTb.t[:, g, :],
                                                                   start=True, stop=True),
                     reads=[(ZT, j), (wsTb, 0)], writes=[(psb[pb], 0)])
            P.op("dve", lambda e, cb=cb, pb=pb: e.scalar_tensor_tensor(
                out=t1[pb].t[:].rearrange("p (j t) -> p j t", t=128), in0=bank(pb).rearrange("p (j t) -> p j t", t=128),
                scalar=modT2.t[:, 16 + cb, 17:18], in1=Bblk.t[:, cb, :].unsqueeze(1).broadcast_to([128, 4, 128]),
                op0=ALU.mult, op1=ALU.add),
                reads=[(psb[pb], 0), (modT2, 0), (Bblk, cb)], writes=[(t1[pb], 0)])
            P.op("dve", lambda e, cb=cb, pb=pb: e.tensor_tensor(out=UT.t[:, cb, :], in0=t1[pb].t[:], in1=UT.t[:, cb, :], op=ALU.mult),
                 reads=[(t1[pb], 0), (UT, cb)], writes=[(UT, cb)])
        A.free(ZT); A.free(mv); A.free(rs)

        gated_proj(w_pb3, UT, lambda kc: kc, OFF_GB, True, pre=TGB, unit_hook=unit_hook)
        A.free(UT); A.free(TGB)

        if pre_out_hook is not None:
            pre_out_hook()
        HTBn = None
        if blk + 1 < NTB:
            HTBn = A.alloc("HTB", [128, KC, TBS], BF16)
            tmpn = norm_tmp()
            norm_block(blk + 1, cols2, tmpn, HTBn, 0)
            free_norm_tmp(tmpn)
        for half in range(2):
            wo = load_w(w_o3, half * 512, 512, "wo")
            for dd in range(4):
                d = half * 4 + dd
                pb = d % 2
                proj_fm(wo, dd * 128, MA, lambda kc: MA.t[:, kc, :], lambda kc: kc, pb)
                P.op("dve", lambda e, d=d, pb=pb: e.scalar_tensor_tensor(
                    out=XT.t[:, d, blk * TBS:(blk + 1) * TBS], in0=bank(pb), scalar=g2h.t[:, d:d + 1],
                    in1=XT.t[:, d, blk * TBS:(blk + 1) * TBS], op0=ALU.mult, op1=ALU.add),
                    reads=[(psb[pb], 0), (XT, (d, blk)), (g2h, 0)], writes=[(XT, (d, blk))])
                if unit_hook is not None:
                    unit_hook()
            A.free(wo)
        A.free(MA)
        for b in tg + t1:
            A.free(b)
        A.free(HTB)
        return HTBn

    def half_gate(modT, name, scale):
        hg = A.alloc(name, [128, KC], F32)
        P.op("dve", lambda e: e.tensor_scalar(out=hg.t[:], in0=modT.t[:, 16:24, 16], scalar1=scale, scalar2=None, op0=ALU.mult),
             reads=[(modT, 0)], writes=[(hg, 0)])
        return hg

    NEXT = {}

    def ffn_phase(i, idx, mod, modT, next_mod=None, fin=None):
        cols = prompt_cols(modT, 0)
        hg = half_gate(modT, f"hg{i}", 0.5)
        samp = dict(hsT=sample_norm(mod, norm_w[i]), acts=A.alloc("acts", [NS, DFF], BF16), hgs=sample_gate(mod, "hgs", 0.5))
        A.free(mod)
        HT = A.alloc("HT", [128, KC, T], BF16)
        tmp = norm_tmp()
        norm_block(0, cols, tmp, HT, 0)
        hooks = {}

        def pre_tb(tb):
            if tb + 1 < NTB:
                norm_block(tb + 1, cols, tmp, HT, (tb + 1) * TBS)
            else:
                free_norm_tmp(tmp)
        hooks["pre_tb"] = pre_tb
        if next_mod is not None:
            mp = mod_pieces(next_mod[0], next_mod[1], NEXT)

            def down_unit(nd):
                if nd % 4 == 1:
                    next(mp, None)
            hooks["down_unit"] = down_unit
            hooks["drain"] = lambda: [None for _ in mp]
        if fin is not None:
            def post_tb(tb):
                if tb >= 1:
                    if not fin:
                        fin.update(final_init())
                    for t in range((tb - 1) * 4, tb * 4):
                        final_tile(fin, t, 6 if t % 2 == 0 else 2)
            hooks["post_tb"] = post_tb
        ffn(idx, hg, HT, samp, hooks)
        if "drain" in hooks:
            hooks["drain"]()
        for bf in samp.values():
            A.free(bf)
        A.free(HT); A.free(modT); A.free(cols); A.free(hg)

    def mod0_alloc():
        mod = A.alloc("mod0", [32, 3 * D], F32)
        modT = A.alloc("modT0", [128, 24, 32], F32)
        bada = A.alloc("bada", [17, 3 * D], F32)
        P.op("dve", lambda e: e.memset(mod.t[:], 0.0), writes=[(mod, 0), (mod, 1)])
        return mod, modT, bada

    def mod0_loads(mod, bada):
        P.op("sp", lambda e: e.dma_start(out=bada.t[:], in_=b_ada[0:1, 0:3 * D].partition_broadcast(17)),
             writes=[(bada, 0)], dma=True)
        P.op("sp", lambda e: e.dma_start(out=mod.t[17:18, 0:D], in_=norm_w[0][0:1, :]), writes=[(mod, 1)], dma=True)

    def mod0_block(mod, blk):
        c0 = blk * 512
        wt = ring.alloc("wada", [128, KC, 512], BF16)
        P.op("pool", lambda e: e.dma_start(
            out=wt.t[:], in_=w_ada.rearrange("(k p) n -> p k n", p=128)[:, :, c0:c0 + 512]),
            writes=[(wt, 0)], dma=True)
        for kc in range(KC):
            P.op("pe", lambda e, kc=kc: e.matmul(bank(6)[0:17, :], scT.t[:, kc, 0:17], wt.t[:, kc, :],
                                                 start=(kc == 0), stop=(kc == KC - 1)),
                 reads=[(scT, 0), (wt, 0)], writes=[(psb[6], 0)])
        P.op("dve", lambda e: e.tensor_copy(out=mod.t[0:17, blk * 512:(blk + 1) * 512], in_=bank(6)[0:17, :]),
             reads=[(psb[6], 0), (mod, 0)], writes=[(mod, 0)])
        A.free(wt)

    xh = {}
    mod0 = A.alloc("mod0", [32, 3 * D], F32)
    modT0 = A.alloc("modT0", [128, 24, 32], F32)
    P.op("dve", lambda e: e.memset(mod0.t[:], 0.0), writes=[(mod0, 0), (mod0, 1)])

    def xhook(t):
        if 4 <= t < 10:
            mod_mm(mod0, mod_load(0, t - 4), t - 4)
    xh["fn"] = xhook
    load_x(xh)
    P.op("sp", lambda e: e.dma_start(out=mod0.t[17:18, 0:D], in_=norm_w[0][0:1, :]), writes=[(mod0, 0)], dma=True)
    mod_finish(mod0, modT0)

    if stage >= 1:
        ffn_phase(0, 0, mod0, modT0,
                  next_mod=(1, [(norm_w[1][0:1, :], 0), (gla_nw[0:1, :], 0), (gm_lnw[0:1, :], 0), (gm_lnb[0:1, :], 0)]))
    if stage >= 2:
        mod2, modT2 = NEXT["mm"]
        cols2 = prompt_cols(modT2, 0)
        g2h = half_gate(modT2, "g2h", 0.5)
        hsT2 = sample_norm(mod2, norm_w[1])
        g2s = sample_gate(mod2, "g2s", 0.5)
        A.free(mod2)
        PROJT = A.alloc("PROJT", [128, 56, NS], F32)
        SHOOK.update(hsT=hsT2, PROJT=PROJT)
        C = tm_consts(modT2)
        HTBc = A.alloc("HTB", [128, KC, TBS], BF16)
        tmp0 = norm_tmp()
        norm_block(0, cols2, tmp0, HTBc, 0)
        free_norm_tmp(tmp0)
        for blk in range(NTB):
            hook = None
            uh = None
            if blk == NTB - 1:
                def hook():
                    NEXT["stm"] = sample_token_mix(hsT2, g2s, PROJT)
                    next(NEXT["stm"])
                if stage >= 3:
                    mp3 = mod_pieces(2, [(norm_w[2][0:1, :], 0)], NEXT)
                    cnt = [0]

                    def uh():
                        cnt[0] += 1
                        if cnt[0] % 2 == 1:
                            next(mp3, None)
            HTBc = token_mix_block(blk, C, modT2, cols2, g2h, HTBc, hook, uh)
            if blk == NTB - 1 and stage >= 3:
                for _ in mp3:
                    pass
        P.op("sp", lambda e: e.dma_start(out=S_p.rearrange("h k v -> k h v"), in_=C["S32"].t[:]),
             reads=[(C["S32"], h) for h in range(4)], dma=True)
        for b in C.values():
            A.free(b)
        A.free(modT2); A.free(cols2); A.free(g2h)
        for _ in NEXT["stm"]:
            pass
        A.free(PROJT)
    fin = {}
    if stage < 3:
        fin.update(final_init())
    if stage >= 3:
        mod3, modT3 = NEXT["mm"]
        ffn_phase(2, 1, mod3, modT3, fin=fin)
        for t in range(12, 16):
            final_tile(fin, t, 6 if t % 2 == 0 else 2)
    else:
        for t in range(T // 128):
            final_tile(fin, t, 4 + 2 * (t % 2))
    final_sample()

    P.emit(nc)
    return nc


def make_in_maps(inp):
    f = lambda a: np.ascontiguousarray(np.asarray(a, dtype=np.float32))
    shared = {
        "w_ada": f(inp["w_ada"][0]),
        "b_ada": f(inp["b_ada"][0]).reshape(1, -1),
        "norm1_w": f(inp["norm1_w"][0]).reshape(1, -1),
        "norm2_w": f(inp["norm2_w"][0]).reshape(1, -1),
        "norm3_w": f(inp["norm3_w"][0]).reshape(1, -1),
        "normf_w": f(inp["normf_w"]).reshape(1, -1),
        "ffn1_w13": f(inp["ffn1_w13"][0]),
        "ffn1_w2": f(inp["ffn1_w2"][0]),
        "ffn2_w13": f(inp["ffn2_w13"][0]),
        "ffn2_w2": f(inp["ffn2_w2"][0]),
        "ident": np.eye(128, dtype=np.float32),
        "w_in": f(inp["w_in"][0]),
        "w_a2aug": f(np.concatenate([inp["w_a2"][0], inp["b_a"][0][None, :]], axis=0)),
        "gla_norm_w": f(inp["gla_norm_w"][0]).reshape(1, -1),
        "gm_ln_w": f(inp["gm_ln_w"][0]).reshape(1, -1),
        "gm_ln_b": f(inp["gm_ln_b"][0]).reshape(1, -1),
        "wsT": f(np.transpose(inp["gm_ws"][0], (2, 0, 1))),
        "gm_bs": f(inp["gm_bs"][0]).reshape(1, -1),
        "w_pa": f(inp["w_pa"][0]),
        "w_pb": f(inp["w_pb"][0]),
        "w_o": f(inp["w_o"][0]),
        "tri": _TRI,
        "tri64": _TRI64,
        "ucum": _UCUM,
        "ws00_row": f(np.repeat(inp["gm_ws"][0][:, 0, 0], 256)).reshape(1, -1),
        "bs0_row": f(np.repeat(inp["gm_bs"][0][:, 0], 256)).reshape(1, -1),
        "eye16": np.eye(16, dtype=np.float32).reshape(1, 256),
    }
    maps = []
    for c in range(NCORES):
        m = dict(shared)
        m["x_p"] = f(inp["x_prompt"][c])
        m["x_s"] = f(inp["x_sample"][c * NS:(c + 1) * NS, 0, :])
        m["state_s"] = f(inp["state_gla"][0, c * NS:(c + 1) * NS])
        m["c17"] = f(np.concatenate([inp["c_sample"][c * NS:(c + 1) * NS], inp["c_prompt"][c:c + 1]], axis=0))
        maps.append(m)
    return maps


_i = np.arange(128)
_TRI = (_i[:, None] <= _i[None, :]).astype(np.float32)
_TRI64 = (_TRI * ((_i[:, None] // 64) == (_i[None, :] // 64))).astype(np.float32)
_UCUM = (_TRI64 * (-1.0 / 16.0)).astype(np.float32)
_NC_CACHE = {}


def run(inp, stage=99, trace=False):
    if stage not in _NC_CACHE:
        _NC_CACHE[stage] = build(stage)
    nc = _NC_CACHE[stage]
    maps = make_in_maps(inp)
    used = set()
    return run_bass_kernel_spmd(nc, maps, core_ids=list(range(NCORES)), trace=trace)


def kernel(**inputs):
    res = run(inputs)
    r = res.results
    y_prompt = np.stack([r[c]["y_p"] for c in range(NCORES)], axis=0).astype(np.float32)
    y_sample = np.concatenate([r[c]["y_s"] for c in range(NCORES)], axis=0).reshape(NCORES * NS, 1, D).astype(np.float32)
    S_prompt = np.stack([r[c]["S_p"] for c in range(NCORES)], axis=0)[None].astype(np.float32)
    S_sample = np.concatenate([r[c]["S_s"] for c in range(NCORES)], axis=0)[None].astype(np.float32)
    gv_sample = np.concatenate([r[c]["gv_s"] for c in range(NCORES)], axis=0).reshape(1, NCORES * NS, 1, D).astype(np.float32)
    return (y_prompt, y_sample, S_prompt, S_sample, gv_sample)
```

```python
import numpy as np
from contextlib import ExitStack
import concourse.bass as bass
import concourse.mybir as mybir
from concourse.bass_utils import run_bass_kernel_spmd

F32 = mybir.dt.float32
BF16 = mybir.dt.bfloat16
AF = mybir.ActivationFunctionType
ALU = mybir.AluOpType

NCORES = 8
D = 1024
KC = 8
T = 2048
NTB = 4
TBS = 512
NS = 16
DFF = 2816
NFC = 22
D_IN = 7184
EPS = 1e-6
ENGS = ("pe", "act", "dve", "pool", "sp")
NSEM_DMA = 12

SB_LO = 16640
SB_HI = 228352


class Op:
    __slots__ = ("eng", "seq", "fn", "waits", "sig", "sigidx", "dma", "dsem", "dval")

    def __init__(self, eng, seq, fn, dma):
        self.eng = eng
        self.seq = seq
        self.fn = fn
        self.waits = []
        self.sig = False
        self.sigidx = 0
        self.dma = dma
        self.dsem = 0
        self.dval = 0


class Buf:
    def __init__(self, name):
        self.name = name
        self.w = {}
        self.r = {}
        self.touch = {}
        self.touch_dma = []
        self.inherit = []
        self.t = None

    def all_ops(self):
        return list(self.touch.values()) + list(self.touch_dma) + list(self.inherit)


def _dedupe(ops):
    best = {}
    dm = {}
    for o in ops:
        if o.dma:
            dm[id(o)] = o
        else:
            b = best.get(o.eng)
            if b is None or b.seq < o.seq:
                best[o.eng] = o
    return list(best.values()) + list(dm.values())


class Prog:
    def __init__(self):
        self.streams = {e: [] for e in ENGS}
        self.known = {e: {f: -1 for f in ENGS} for e in ENGS}
        self.known_dma = {e: {} for e in ENGS}
        self.dma_count = {e: 0 for e in ENGS}
        self.dma_last = {e: [None] * NSEM_DMA for e in ENGS}

    def _add_dep(self, op, dep):
        if dep is None or dep is op:
            return
        e = op.eng
        if dep.dma:
            key = (dep.eng, dep.dsem)
            if self.known_dma[e].get(key, 0) >= dep.dval:
                return
            self.known_dma[e][key] = dep.dval
            op.waits.append(dep)
        else:
            if dep.eng == "pe" and e == "pe":
                return
            if self.known[e][dep.eng] >= dep.seq:
                return
            self.known[e][dep.eng] = dep.seq
            dep.sig = True
            op.waits.append(dep)

    def op(self, eng, fn, reads=(), writes=(), dma=False):
        o = Op(eng, len(self.streams[eng]), fn, dma)
        for (buf, key) in reads:
            for d in buf.inherit:
                self._add_dep(o, d)
            self._add_dep(o, buf.w.get(key))
        for (buf, key) in writes:
            for d in buf.inherit:
                self._add_dep(o, d)
            self._add_dep(o, buf.w.get(key))
            rr = buf.r.get(key)
            if rr:
                for d in rr.values():
                    self._add_dep(o, d)
        for (buf, key) in reads:
            rr = buf.r.setdefault(key, {})
            rr[id(o) if dma else eng] = o
            self._touch(buf, o)
        for (buf, key) in writes:
            buf.w[key] = o
            buf.r[key] = {}
            self._touch(buf, o)
        if dma:
            k = self.dma_count[eng]
            idx = k % NSEM_DMA
            prev = self.dma_last[eng][idx]
            if prev is not None:
                self._add_dep(o, prev)
            o.dsem = idx
            o.dval = 16 * (k // NSEM_DMA + 1)
            self.dma_last[eng][idx] = o
            self.dma_count[eng] = k + 1
        self.streams[eng].append(o)
        return o

    def _touch(self, buf, o):
        if o.dma:
            buf.touch_dma.append(o)
            if len(buf.touch_dma) > 64:
                buf.touch_dma = buf.touch_dma[-64:]
        else:
            buf.touch[o.eng] = o

    def emit(self, nc):
        for e in ENGS:
            c = 0
            for o in self.streams[e]:
                if o.sig and not o.dma:
                    c += 1
                    o.sigidx = c
        with ExitStack() as es:
            sem = {e: es.enter_context(nc.semaphore("s_" + e)) for e in ENGS}
            dsem = {e: [es.enter_context(nc.semaphore(f"d_{e}_{i}")) for i in range(NSEM_DMA)]
                    for e in ("pool", "sp")}
            es.enter_context(nc.allow_low_precision("bf16 matmul operands, fp32 accumulation"))
            block = es.enter_context(nc.Block())
            streams = self.streams

            def run(ename, engine):
                for o in streams[ename]:
                    for d in o.waits:
                        if d.dma:
                            engine.wait_ge(dsem[d.eng][d.dsem], d.dval)
                        else:
                            engine.wait_ge(sem[d.eng], d.sigidx)
                    ins = o.fn(engine)
                    if o.dma:
                        ins.then_inc(dsem[ename][o.dsem], 16)
                    elif o.sig:
                        ins.then_inc(sem[ename], 1)
                if ename in ("pool", "sp"):
                    for i, last in enumerate(self.dma_last[ename]):
                        if last is not None:
                            engine.wait_ge(dsem[ename][i], last.dval)

            @block.tensor
            def _(eng):
                run("pe", eng)

            @block.scalar
            def _(eng):
                run("act", eng)

            @block.vector
            def _(eng):
                run("dve", eng)

            @block.gpsimd
            def _(eng):
                run("pool", eng)

            @block.sync
            def _(eng):
                run("sp", eng)


class Arena:
    def __init__(self, nc, lo, hi):
        self.nc = nc
        self.lo = lo
        self.hi = hi
        self.live = []
        self.dead = []
        self.n = 0

    def alloc(self, name, shape, dtype, off=None, lo=None, hi=None):
        esz = 2 if dtype == BF16 else 4
        nb = esz
        for s in shape[1:]:
            nb *= s
        nb = (nb + 63) // 64 * 64
        lo = self.lo if lo is None else lo
        hi = self.hi if hi is None else hi
        if off is None:
            segs = sorted((s, e) for (s, e, b) in self.live)
            cur = lo
            for (s, e) in segs:
                if e <= cur:
                    continue
                if s - cur >= nb:
                    break
                cur = max(cur, e)
            off = cur
        assert off >= lo and off + nb <= hi, f"SBUF arena overflow for {name}: {off}+{nb} > {hi}"
        for (s, e, b) in self.live:
            assert e <= off or s >= off + nb, f"arena overlap {name} vs {b.name}"
        buf = Buf(name)
        ops = []
        keep = []
        for (s, e, b) in self.dead:
            if e <= off or s >= off + nb:
                keep.append((s, e, b))
                continue
            ops.extend(b.all_ops())
            if not (off <= s and e <= off + nb):
                keep.append((s, e, b))
        self.dead = keep
        buf.inherit = _dedupe(ops)
        self.n += 1
        buf.t = self.nc.alloc_sbuf_tensor_at(f"{name}_{self.n}", list(shape), dtype, offset=off)
        self.live.append((off, off + nb, buf))
        top = max(e for (s_, e, b_) in self.live if e <= self.hi) if any(e <= self.hi for (s_, e, b_) in self.live) else 0
        if top > getattr(self, "peak", 0) and off + nb <= self.hi:
            self.peak = top
            self.peak_live = [(b_.name, s_, e - s_) for (s_, e, b_) in self.live]
        return buf

    def free(self, buf):
        for i, (s, e, b) in enumerate(self.live):
            if b is buf:
                self.dead.append(self.live.pop(i))
                return
        raise KeyError(buf.name)


class Ring:
    def __init__(self, arena, lo, hi):
        self.a = arena
        self.lo = lo
        self.hi = hi
        self.ptr = lo

    def alloc(self, name, shape, dtype):
        esz = 2 if dtype == BF16 else 4
        nb = esz
        for s in shape[1:]:
            nb *= s
        nb = (nb + 63) // 64 * 64
        for _ in range(16):
            if self.ptr + nb > self.hi:
                self.ptr = self.lo
            clash = [e for (s_, e, b_) in self.a.live if s_ < self.ptr + nb and e > self.ptr and s_ >= self.lo]
            if not clash:
                break
            self.ptr = max(clash)
        off = self.ptr
        self.ptr += nb
        return self.a.alloc(name, shape, dtype, off=off, lo=self.lo, hi=self.hi)


FF_SPLITS = [(0, 7), (7, 15), (15, 22)]


def build(stage=99):
    nc = bass.Bass("TRN2", target_bir_lowering=False)
    P = Prog()
    A = Arena(nc, SB_LO, SB_HI)
    RING_BYTES = 32 * 1024
    ring = Ring(A, SB_HI - RING_BYTES, SB_HI)
    A.hi = SB_HI - RING_BYTES

    def din(name, shape):
        return nc.dram_tensor(name, list(shape), F32, kind="ExternalInput").ap()

    def dout(name, shape):
        return nc.dram_tensor(name, list(shape), F32, kind="ExternalOutput").ap()

    x_p = din("x_p", [T, D])
    x_s = din("x_s", [NS, D])
    c17 = din("c17", [17, D])
    w_ada = din("w_ada", [D, 9 * D])
    b_ada = din("b_ada", [1, 9 * D])
    norm_w = [din(f"norm{i}_w", [1, D]) for i in (1, 2, 3)]
    normf_w = din("normf_w", [1, D])
    ffn_w13 = [din("ffn1_w13", [D, 2 * DFF]), din("ffn2_w13", [D, 2 * DFF])]
    ffn_w2 = [din("ffn1_w2", [DFF, D]), din("ffn2_w2", [DFF, D])]
    ident_d = din("ident", [128, 128])
    w_in = din("w_in", [D, D_IN])
    w_a2aug = din("w_a2aug", [17, 512])
    gla_nw = din("gla_norm_w", [1, D])
    gm_lnw = din("gm_ln_w", [1, D])
    gm_lnb = din("gm_ln_b", [1, D])
    wsT_d = din("wsT", [128, 4, 128])
    bs_d = din("gm_bs", [1, 512])
    w_pa = din("w_pa", [D, D])
    w_pb = din("w_pb", [D, D])
    w_o = din("w_o", [D, D])
    tri_d = din("tri", [128, 128])
    tri64_d = din("tri64", [128, 128])
    ucum_d = din("ucum", [128, 128])
    state_s = din("state_s", [NS, 4, 128, 256])
    ws00_d = din("ws00_row", [1, D])
    bs0_d = din("bs0_row", [1, D])
    eye16_d = din("eye16", [1, 256])

    y_p = dout("y_p", [T, D])
    y_s = dout("y_s", [NS, D])
    S_p = dout("S_p", [4, 128, 256])
    S_s = dout("S_s", [NS, 4, 128, 256])
    gv_s = dout("gv_s", [NS, D])

    PS = nc.alloc_psum_tensor("ps", [128, 4096], F32)
    psb = [Buf(f"psb{i}") for i in range(8)]

    def bank(i, n=512, off=0):
        return PS[:, i * 512 + off:i * 512 + off + n]

    def bank_bf(i):
        return PS[:, i * 512:(i + 1) * 512].bitcast(BF16)

    ident = A.alloc("ident", [128, 128], F32)
    identb = A.alloc("identb", [128, 128], BF16)
    onesb = A.alloc("onesb", [128, 128], BF16)
    P.op("sp", lambda e: e.dma_start(out=ident.t[:], in_=ident_d), writes=[(ident, 0)], dma=True)
    P.op("dve", lambda e: e.tensor_copy(out=identb.t[:], in_=ident.t[:]), reads=[(ident, 0)], writes=[(identb, 0)])
    P.op("dve", lambda e: e.memset(onesb.t[:], 1.0), writes=[(onesb, 0)])

    XT = A.alloc("XT", [128, KC, T], F32)
    XS = A.alloc("XS", [NS, D], F32)
    P.op("sp", lambda e: e.dma_start(out=XS.t[:], in_=x_s), writes=[(XS, 0)], dma=True)

    c_sb = A.alloc("c_sb", [17, D], F32)
    cs_b = A.alloc("cs_b", [17, D], BF16)
    scT = A.alloc("scT", [128, KC, 32], BF16)
    P.op("sp", lambda e: e.dma_start(out=c_sb.t[:], in_=c17), writes=[(c_sb, 0)], dma=True)
    P.op("act", lambda e: e.activation(out=cs_b.t[:], in_=c_sb.t[:], func=AF.Silu), reads=[(c_sb, 0)], writes=[(cs_b, 0)])
    for kc in range(KC):
        P.op("pe", lambda e, kc=kc: e.transpose(out=bank_bf(7)[:, kc * 32:kc * 32 + 17],
                                                  in_=cs_b.t[:, kc * 128:(kc + 1) * 128],
                                                  identity=identb.t[0:17, 0:17]),
             reads=[(cs_b, 0), (identb, 0)], writes=[(psb[7], 0)])
    P.op("dve", lambda e: e.tensor_copy(out=scT.t[:, :, 0:17],
                                        in_=bank_bf(7)[:, 0:256].rearrange("p (k c) -> p k c", c=32)[:, :, 0:17]),
         reads=[(psb[7], 0)], writes=[(scT, 0)])
    A.free(c_sb)
    A.free(cs_b)

    def load_x(XHOOK):
      xin = [A.alloc(f"xin{i}", [128, D], F32, off=A.hi - (i + 1) * 4096) for i in range(4)]
      for t in range(T // 128):
        if "fn" in XHOOK:
            XHOOK["fn"](t)
        xb = xin[t % 4]
        pb = 4 + 2 * (t % 2)
        P.op("sp", lambda e, xb=xb, t=t: e.dma_start(out=xb.t[:], in_=x_p[t * 128:(t + 1) * 128, :]),
             writes=[(xb, 0)], dma=True)
        for kc in range(KC):
            P.op("pe", lambda e, xb=xb, kc=kc, pb=pb: e.transpose(
                out=PS[:, pb * 512 + kc * 128: pb * 512 + (kc + 1) * 128],
                in_=xb.t[:, kc * 128:(kc + 1) * 128], identity=ident.t[:]),
                reads=[(xb, 0), (ident, 0)], writes=[(psb[pb + kc // 4], 0)])
        eng = "act" if t % 2 == 0 else "dve"
        src = lambda pb=pb: PS[:, pb * 512:(pb + 2) * 512].rearrange("p (k c) -> p k c", c=128)
        if eng == "act":
            P.op("act", lambda e, t=t, src=src: e.activation(out=XT.t[:, :, t * 128:(t + 1) * 128], in_=src(), func=AF.Copy),
                 reads=[(psb[pb], 0), (psb[pb + 1], 0)], writes=[(XT, (k, t // 4)) for k in range(KC)])
        else:
            P.op("dve", lambda e, t=t, src=src: e.tensor_copy(out=XT.t[:, :, t * 128:(t + 1) * 128], in_=src()),
                 reads=[(psb[pb], 0), (psb[pb + 1], 0)], writes=[(XT, (k, t // 4)) for k in range(KC)])
      for b in xin:
        A.free(b)

    def mod_start(i, extra_rows):
        mod = A.alloc(f"mod{i}", [32, 3 * D], F32)
        modT = A.alloc(f"modT{i}", [128, 24, 32], F32)
        P.op("dve", lambda e: e.memset(mod.t[:], 0.0), writes=[(mod, 0)])
        P.op("sp", lambda e: e.dma_start(out=mod.t[0:17, :],
                                         in_=b_ada[0:1, i * 3 * D:(i + 1) * 3 * D].partition_broadcast(17)),
             reads=[], writes=[(mod, 0)], dma=True)
        for r, (ap, c0) in enumerate(extra_rows):
            row = 17 + r // 3
            col = (r % 3) * D
            P.op("sp", lambda e, ap=ap, row=row, col=col: e.dma_start(out=mod.t[row:row + 1, col:col + D], in_=ap),
                 writes=[(mod, 0)], dma=True)
        return mod, modT

    def mod_block(i, mod, blk, after=()):
        c0 = i * 3 * D + blk * 512
        wt = ring.alloc("wada", [128, KC, 512], BF16)
        P.op("pool", lambda e: e.dma_start(
            out=wt.t[:], in_=w_ada.rearrange("(k p) n -> p k n", p=128)[:, :, c0:c0 + 512]),
            reads=list(after), writes=[(wt, 0)], dma=True)
        for kc in range(KC):
            P.op("pe", lambda e, kc=kc: e.matmul(bank(6)[0:17, :], scT.t[:, kc, 0:17], wt.t[:, kc, :],
                                                 start=(kc == 0), stop=(kc == KC - 1)),
                 reads=[(scT, 0), (wt, 0)], writes=[(psb[6], 0)])
        P.op("dve", lambda e: e.tensor_tensor(out=mod.t[0:17, blk * 512:(blk + 1) * 512], in0=bank(6)[0:17, :],
                                              in1=mod.t[0:17, blk * 512:(blk + 1) * 512], op=ALU.add),
             reads=[(psb[6], 0), (mod, 0)], writes=[(mod, 0)])
        A.free(wt)

    def mod_finish(mod, modT):
        for j in range(24):
            P.op("pe", lambda e, j=j: e.transpose(out=PS[:, 7 * 512 + (j % 12) * 32: 7 * 512 + (j % 12) * 32 + 32],
                                                  in_=mod.t[:, j * 128:(j + 1) * 128], identity=ident.t[0:32, 0:32]),
                 reads=[(mod, 0), (ident, 0)], writes=[(psb[7], 0)])
            if j % 12 == 11:
                h = j // 12
                P.op("dve", lambda e, h=h: e.tensor_copy(
                    out=modT.t[:, h * 12:(h + 1) * 12, :],
                    in_=bank(7)[:, 0:384].rearrange("p (k c) -> p k c", c=32)),
                    reads=[(psb[7], 0)], writes=[(modT, 0)])

    ones17 = A.alloc("ones17", [1, 32], F32)
    P.op("dve", lambda e: e.memset(ones17.t[:], 1.0), writes=[(ones17, 0)])

    def mod_load(i, blk):
        c0 = i * 3 * D + blk * 512
        wt = ring.alloc("wada", [128, KC, 512], BF16)
        P.op("pool", lambda e: e.dma_start(
            out=wt.t[:], in_=w_ada.rearrange("(k p) n -> p k n", p=128)[:, :, c0:c0 + 512]),
            writes=[(wt, 0)], dma=True)
        brow = A.alloc("brow", [1, 512], F32)
        P.op("sp", lambda e: e.dma_start(out=brow.t[:], in_=b_ada[0:1, c0:c0 + 512]), writes=[(brow, 0)], dma=True)
        return wt, brow

    def mod_mm(mod, wb_, blk):
        wt, brow = wb_
        P.op("pe", lambda e: e.matmul(bank(6)[0:17, :], ones17.t[0:1, 0:17], brow.t[0:1, :], start=True, stop=False),
             reads=[(ones17, 0), (brow, 0)], writes=[(psb[6], 0)])
        for kc in range(KC):
            P.op("pe", lambda e, kc=kc: e.matmul(bank(6)[0:17, :], scT.t[:, kc, 0:17], wt.t[:, kc, :],
                                                 start=False, stop=(kc == KC - 1)),
                 reads=[(scT, 0), (wt, 0)], writes=[(psb[6], 0)])
        P.op("dve", lambda e: e.tensor_copy(out=mod.t[0:17, blk * 512:(blk + 1) * 512], in_=bank(6)[0:17, :]),
             reads=[(psb[6], 0), (mod, 0)], writes=[(mod, 0)])
        A.free(wt)
        A.free(brow)

    def mod_start_nob(i, extra_rows):
        mod = A.alloc(f"mod{i}", [32, 3 * D], F32)
        modT = A.alloc(f"modT{i}", [128, 24, 32], F32)
        P.op("dve", lambda e: e.memset(mod.t[:], 0.0), writes=[(mod, 0)])
        for r, (ap, c0) in enumerate(extra_rows):
            row = 17 + r // 3
            col = (r % 3) * D
            P.op("sp", lambda e, ap=ap, row=row, col=col: e.dma_start(out=mod.t[row:row + 1, col:col + D], in_=ap),
                 writes=[(mod, 0)], dma=True)
        return mod, modT

    def mod_pieces(i, extra_rows, out):
        mod, modT = mod_start_nob(i, extra_rows)
        out["mm"] = (mod, modT)
        wt = mod_load(i, 0)
        yield
        for blk in range(6):
            mod_mm(mod, wt, blk)
            wt = mod_load(i, blk + 1) if blk + 1 < 6 else None
            yield
        mod_finish(mod, modT)
        yield

    def compute_mod(i, extra_rows):
        mod, modT = mod_start(i, extra_rows)
        for blk in range(6):
            mod_block(i, mod, blk)
        mod_finish(mod, modT)
        return mod, modT

    def prompt_cols(modT, wrow_blk0):
        cols = A.alloc("pcols", [128, 3, KC], F32)
        P.op("dve", lambda e: e.scalar_tensor_tensor(out=cols.t[:, 0, :], in0=modT.t[:, 8:16, 16], scalar=1.0,
                                                      in1=modT.t[:, wrow_blk0:wrow_blk0 + 8, 17],
                                                      op0=ALU.add, op1=ALU.mult),
             reads=[(modT, 0)], writes=[(cols, 0)])
        P.op("dve", lambda e: e.tensor_copy(out=cols.t[:, 1, :], in_=modT.t[:, 0:8, 16]),
             reads=[(modT, 0), (cols, 0)], writes=[(cols, 0)])
        return cols

    def norm_block(tb, cols, tmp, dst, dc0):
        sq, rstd, nt = tmp
        for kc in range(KC):
            P.op("act", lambda e, kc=kc: e.activation(out=sq.t[:, kc % 4, :], in_=XT.t[:, kc, tb * TBS:(tb + 1) * TBS],
                                                      func=AF.Square),
                 reads=[(XT, (kc, tb))], writes=[(sq, kc % 4)])
            P.op("pe", lambda e, kc=kc: e.matmul(bank(7), onesb.t[:], sq.t[:, kc % 4, :], start=(kc == 0), stop=(kc == KC - 1)),
                 reads=[(onesb, 0), (sq, kc % 4)], writes=[(psb[7], 0)])
        P.op("act", lambda e: e.activation(out=rstd.t[:], in_=bank(7), func=AF.Ln, scale=1.0 / D, bias=EPS),
             reads=[(psb[7], 0)], writes=[(rstd, 0)])
        P.op("act", lambda e: e.activation(out=rstd.t[:], in_=rstd.t[:], func=AF.Exp, scale=-0.5),
             reads=[(rstd, 0)], writes=[(rstd, 0)])
        for kc in range(KC):
            t = nt[kc % 2]
            P.op("dve", lambda e, kc=kc, t=t: e.scalar_tensor_tensor(
                out=t.t[:], in0=XT.t[:, kc, tb * TBS:(tb + 1) * TBS], scalar=cols.t[:, 0, kc:kc + 1],
                in1=rstd.t[:], op0=ALU.mult, op1=ALU.mult),
                reads=[(XT, (kc, tb)), (cols, 0), (rstd, 0)], writes=[(t, 0)])
            if kc % 4 == 3:
                P.op("dve", lambda e, kc=kc, t=t: e.tensor_scalar(out=dst.t[:, kc, dc0:dc0 + TBS], in0=t.t[:],
                                                                  scalar1=cols.t[:, 1, kc:kc + 1], scalar2=None, op0=ALU.add),
                     reads=[(t, 0), (cols, 0)], writes=[(dst, (kc, tb))])
            else:
                P.op("act", lambda e, kc=kc, t=t: e.activation(out=dst.t[:, kc, dc0:dc0 + TBS], in_=t.t[:],
                                                               func=AF.Identity, bias=cols.t[:, 1, kc:kc + 1], scale=1.0),
                     reads=[(t, 0), (cols, 0)], writes=[(dst, (kc, tb))])

    def norm_tmp():
        sq = A.alloc("sq", [128, 4, TBS], BF16)
        rstd = A.alloc("rstd", [128, TBS], F32)
        nt = [A.alloc(f"nt{i}", [128, TBS], F32) for i in range(2)]
        return sq, rstd, nt

    def free_norm_tmp(tmp):
        sq, rstd, nt = tmp
        A.free(sq)
        A.free(rstd)
        for b in nt:
            A.free(b)

    def ffn(idx, gbuf, HT, samp=None, hooks=None):
        hooks = hooks or {}
        first_group = [True]
        w13 = ffn_w13[idx].rearrange("(k p) n -> p k n", p=128)
        w2 = ffn_w2[idx].rearrange("(k p) n -> p k n", p=128)
        sa = [A.alloc(f"sa{i}", [128, TBS], F32) for i in range(2)]
        nev = 0
        for (c_lo, c_hi) in FF_SPLITS:
            nch = c_hi - c_lo
            act = A.alloc("act", [128, nch, T], BF16)
            g0 = c_lo
            while g0 < c_hi:
                g1 = min(g0 + 4, c_hi)
                ncol = (g1 - g0) * 128
                wa = ring.alloc("wa", [128, KC, ncol], BF16)
                wb = ring.alloc("wb", [128, KC, ncol], BF16)
                P.op("pool", lambda e, wa=wa, g0=g0, ncol=ncol: e.dma_start(
                    out=wa.t[:], in_=w13[:, :, g0 * 128:g0 * 128 + ncol]), writes=[(wa, 0)], dma=True)
                P.op("pool", lambda e, wb=wb, g0=g0, ncol=ncol: e.dma_start(
                    out=wb.t[:], in_=w13[:, :, DFF + g0 * 128:DFF + g0 * 128 + ncol]), writes=[(wb, 0)], dma=True)
                if samp is not None:
                    sample_proj(samp["hsT"], wa, ncol, 6)
                    sample_proj(samp["hsT"], wb, ncol, 7)
                    sas = A.alloc("sas", [NS, 512], F32)
                    P.op("act", lambda e, sas=sas, ncol=ncol: e.activation(out=sas.t[:, 0:ncol], in_=bank(6)[0:NS, 0:ncol], func=AF.Silu),
                         reads=[(psb[6], 0)], writes=[(sas, 0)])
                    P.op("dve", lambda e, sas=sas, ncol=ncol, g0=g0: e.tensor_tensor(
                        out=samp["acts"].t[:, g0 * 128:g0 * 128 + ncol], in0=sas.t[:, 0:ncol], in1=bank(7)[0:NS, 0:ncol], op=ALU.mult),
                        reads=[(sas, 0), (psb[7], 0)], writes=[(samp["acts"], 0)])
                    A.free(sas)
                for tb in range(NTB):
                    if first_group[0] and "pre_tb" in hooks:
                        hooks["pre_tb"](tb)
                    for j in range(g0, g1):
                        pa = nev % 2
                        pbk = 2 + nev % 2
                        for kc in range(KC):
                            P.op("pe", lambda e, wa=wa, kc=kc, j=j, g0=g0, tb=tb, pa=pa: e.matmul(
                                bank(pa), wa.t[:, kc, (j - g0) * 128:(j - g0 + 1) * 128],
                                HT.t[:, kc, tb * TBS:(tb + 1) * TBS], start=(kc == 0), stop=(kc == KC - 1)),
                                reads=[(wa, 0), (HT, (kc, tb))], writes=[(psb[pa], 0)])
                        for kc in range(KC):
                            P.op("pe", lambda e, wb=wb, kc=kc, j=j, g0=g0, tb=tb, pbk=pbk: e.matmul(
                                bank(pbk), wb.t[:, kc, (j - g0) * 128:(j - g0 + 1) * 128],
                                HT.t[:, kc, tb * TBS:(tb + 1) * TBS], start=(kc == 0), stop=(kc == KC - 1)),
                                reads=[(wb, 0), (HT, (kc, tb))], writes=[(psb[pbk], 0)])
                        s = sa[nev % 2]
                        P.op("act", lambda e, s=s, pa=pa: e.activation(out=s.t[:], in_=bank(pa), func=AF.Silu),
                             reads=[(psb[pa], 0)], writes=[(s, 0)])
                        P.op("dve", lambda e, s=s, pbk=pbk, j=j, tb=tb, c_lo=c_lo, act=act: e.tensor_tensor(
                            out=act.t[:, j - c_lo, tb * TBS:(tb + 1) * TBS], in0=s.t[:], in1=bank(pbk), op=ALU.mult),
                            reads=[(s, 0), (psb[pbk], 0)], writes=[(act, (j - c_lo, tb))])
                        nev += 1
                A.free(wa)
                A.free(wb)
                first_group[0] = False
                g0 = g1
            last_split = (c_hi == NFC)
            if last_split and "before_last_down" in hooks:
                hooks["before_last_down"]()
            w2t = []
            for half in range(2):
                wt = ring.alloc("w2", [128, nch, 512], BF16)
                P.op("pool", lambda e, wt=wt, half=half, c_lo=c_lo, nch=nch: e.dma_start(
                    out=wt.t[:], in_=w2[:, c_lo:c_lo + nch, half * 512:(half + 1) * 512]), writes=[(wt, 0)], dma=True)
                w2t.append(wt)
            if samp is not None:
                aT_ = transpose_s(samp["acts"], c_lo * 128, nch, "actsT")
                for half in range(2):
                    sample_proj(aT_, w2t[half], 512, 6)
                    sample_residual(6, samp["hgs"], half * 512)
                A.free(aT_)
            nd = 0
            for tb in range(NTB):
                for d in range(KC):
                    pd = 4 + nd % 2
                    wt = w2t[d // 4]
                    for k in range(nch):
                        P.op("pe", lambda e, wt=wt, k=k, d=d, tb=tb, pd=pd, act=act, nch=nch: e.matmul(
                            bank(pd), wt.t[:, k, (d % 4) * 128:(d % 4 + 1) * 128], act.t[:, k, tb * TBS:(tb + 1) * TBS],
                            start=(k == 0), stop=(k == nch - 1)),
                            reads=[(wt, 0), (act, (k, tb))], writes=[(psb[pd], 0)])
                    P.op("dve", lambda e, d=d, tb=tb, pd=pd: e.scalar_tensor_tensor(
                        out=XT.t[:, d, tb * TBS:(tb + 1) * TBS], in0=bank(pd), scalar=gbuf.t[:, d:d + 1],
                        in1=XT.t[:, d, tb * TBS:(tb + 1) * TBS], op0=ALU.mult, op1=ALU.add),
                        reads=[(psb[pd], 0), (XT, (d, tb)), (gbuf, 0)], writes=[(XT, (d, tb))])
                    nd += 1
                    if last_split and "down_unit" in hooks:
                        hooks["down_unit"](nd)
                if last_split and "post_tb" in hooks:
                    hooks["post_tb"](tb)
            for wt in w2t:
                A.free(wt)
            A.free(act)
        for b in sa:
            A.free(b)

    def final_init():
        F = {}
        F["nfb"] = A.alloc("nfb", [128, D], F32)
        P.op("sp", lambda e: e.dma_start(out=F["nfb"].t[:], in_=normf_w[0:1, :].partition_broadcast(128)),
             writes=[(F["nfb"], 0)], dma=True)
        F["yt"] = [A.alloc(f"yt{i}", [128, D], F32) for i in range(2)]
        F["junk"] = A.alloc("junk", [128, D], BF16)
        F["st"] = A.alloc("fstat", [128, 16, 2], F32)
        P.op("dve", lambda e: e.memset(F["st"].t[:], 0.0), writes=[(F["st"], t) for t in range(16)])
        return F

    def final_tile(F, t, pb):
        nfb, junk, st = F["nfb"], F["junk"], F["st"]
        for kc in range(KC):
            P.op("pe", lambda e, kc=kc: e.transpose(
                out=PS[:, pb * 512 + kc * 128: pb * 512 + (kc + 1) * 128],
                in_=XT.t[:, kc, t * 128:(t + 1) * 128], identity=ident.t[:]),
                reads=[(XT, (kc, t // 4)), (ident, 0)], writes=[(psb[pb + kc // 4], 0)])
        src = lambda: PS[:, pb * 512:(pb + 2) * 512]
        P.op("act", lambda e: e.activation(out=junk.t[:], in_=src(), func=AF.Square, accum_out=st.t[:, t, 0:1]),
             reads=[(psb[pb], 0), (psb[pb + 1], 0)], writes=[(junk, 0), (st, t)])
        P.op("act", lambda e: e.activation(out=st.t[:, t, 1:2], in_=st.t[:, t, 0:1], func=AF.Ln, scale=1.0 / D, bias=EPS),
             reads=[(st, t)], writes=[(st, t)])
        P.op("act", lambda e: e.activation(out=st.t[:, t, 1:2], in_=st.t[:, t, 1:2], func=AF.Exp, scale=-0.5),
             reads=[(st, t)], writes=[(st, t)])
        y = F["yt"][t % 2]
        P.op("dve", lambda e: e.scalar_tensor_tensor(
            out=y.t[:], in0=src(), scalar=st.t[:, t, 1:2], in1=nfb.t[:], op0=ALU.mult, op1=ALU.mult),
            reads=[(psb[pb], 0), (psb[pb + 1], 0), (st, t), (nfb, 0)], writes=[(y, 0)])
        P.op("sp", lambda e: e.dma_start(out=y_p[t * 128:(t + 1) * 128, :], in_=y.t[:]), reads=[(y, 0)], dma=True)

    def bcast_row(name, src_row, npart=NS):
        b = A.alloc(name, [npart, D], F32)
        P.op("sp", lambda e: e.dma_start(out=b.t[:], in_=src_row.partition_broadcast(npart)), writes=[(b, 0)], dma=True)
        return b

    def transpose_s(src, c0, nchunk, name, dt=BF16):
        dst = A.alloc(name, [128, nchunk, NS], dt)
        idn = identb if dt == BF16 else ident
        for i in range(nchunk):
            if dt == BF16:
                o = lambda i=i: bank_bf(6)[:, i * 16:(i + 1) * 16]
            else:
                o = lambda i=i: bank(6)[:, i * 16:(i + 1) * 16]
            P.op("pe", lambda e, i=i, o=o: e.transpose(out=o(), in_=src.t[0:NS, c0 + i * 128: c0 + (i + 1) * 128],
                                                       identity=idn.t[0:NS, 0:NS]),
                 reads=[(src, 0), (idn, 0)], writes=[(psb[6], 0)])
        if dt == BF16:
            srcv = lambda: bank_bf(6)[:, 0:nchunk * 16].rearrange("p (k c) -> p k c", c=16)
        else:
            srcv = lambda: bank(6)[:, 0:nchunk * 16].rearrange("p (k c) -> p k c", c=16)
        P.op("dve", lambda e: e.tensor_copy(out=dst.t[:], in_=srcv()), reads=[(psb[6], 0)], writes=[(dst, 0)])
        return dst

    def sample_rstd(src_ap_fn, src_reads, n, scale_inv, name):
        ss = A.alloc(name, [NS, 2 * n], F32)
        junk = A.alloc(name + "j", [NS, D], F32)
        P.op("dve", lambda e: e.memset(ss.t[:], 0.0), writes=[(ss, 0)])
        w = D // n
        for i in range(n):
            P.op("act", lambda e, i=i: e.activation(out=junk.t[:, i * w:(i + 1) * w], in_=src_ap_fn(i * w, (i + 1) * w),
                                                    func=AF.Square, accum_out=ss.t[:, i:i + 1]),
                 reads=src_reads + [(ss, 0)], writes=[(junk, 0), (ss, 0)])
        P.op("act", lambda e: e.activation(out=ss.t[:, n:2 * n], in_=ss.t[:, 0:n], func=AF.Ln, scale=scale_inv, bias=EPS),
             reads=[(ss, 0)], writes=[(ss, 0)])
        P.op("act", lambda e: e.activation(out=ss.t[:, n:2 * n], in_=ss.t[:, n:2 * n], func=AF.Exp, scale=-0.5),
             reads=[(ss, 0)], writes=[(ss, 0)])
        A.free(junk)
        return ss

    def sample_norm(mod, nw_dram):
        nwb = bcast_row("nwb", nw_dram[0:1, :])
        ss = sample_rstd(lambda a, b: XS.t[:, a:b], [(XS, 0)], 1, 1.0 / D, "sss")
        tmpf = A.alloc("snt", [NS, D], F32)
        hs = A.alloc("hs", [NS, D], BF16)
        P.op("dve", lambda e: e.scalar_tensor_tensor(out=nwb.t[:], in0=mod.t[0:NS, D:2 * D], scalar=1.0, in1=nwb.t[:],
                                                      op0=ALU.add, op1=ALU.mult),
             reads=[(mod, 0), (nwb, 0)], writes=[(nwb, 0)])
        P.op("dve", lambda e: e.scalar_tensor_tensor(out=tmpf.t[:], in0=XS.t[:], scalar=ss.t[:, 1:2], in1=nwb.t[:],
                                                      op0=ALU.mult, op1=ALU.mult),
             reads=[(XS, 0), (ss, 0), (nwb, 0)], writes=[(tmpf, 0)])
        P.op("dve", lambda e: e.tensor_tensor(out=hs.t[:], in0=tmpf.t[:], in1=mod.t[0:NS, 0:D], op=ALU.add),
             reads=[(tmpf, 0), (mod, 0)], writes=[(hs, 0)])
        hsT = transpose_s(hs, 0, KC, "hsT")
        A.free(nwb); A.free(ss); A.free(tmpf); A.free(hs)
        return hsT

    def sample_gate(mod, name, scale):
        g = A.alloc(name, [NS, D], F32)
        P.op("dve", lambda e: e.tensor_scalar(out=g.t[:], in0=mod.t[0:NS, 2 * D:3 * D], scalar1=scale, scalar2=None, op0=ALU.mult),
             reads=[(mod, 0)], writes=[(g, 0)])
        return g

    def sample_proj(hsT, wt, ncol, pb):
        nk = hsT.t.shape[1]
        for kc in range(nk):
            P.op("pe", lambda e, kc=kc: e.matmul(bank(pb)[0:NS, 0:ncol], hsT.t[:, kc, :], wt.t[:, kc, :],
                                                 start=(kc == 0), stop=(kc == nk - 1)),
                 reads=[(hsT, 0), (wt, 0)], writes=[(psb[pb], 0)])

    def sample_residual(pb, gs, c0):
        tt = A.alloc("srt", [NS, 512], F32)
        P.op("dve", lambda e: e.tensor_tensor(out=tt.t[:], in0=bank(pb)[0:NS, :], in1=gs.t[:, c0:c0 + 512], op=ALU.mult),
             reads=[(psb[pb], 0), (gs, 0)], writes=[(tt, 0)])
        P.op("dve", lambda e: e.tensor_tensor(out=XS.t[:, c0:c0 + 512], in0=XS.t[:, c0:c0 + 512], in1=tt.t[:], op=ALU.add),
             reads=[(tt, 0), (XS, 0)], writes=[(XS, 0)])
        A.free(tt)

    def sample_token_mix(hsT, g2s, PROJT):
        walr_s = load_w(w_in3, OFF_ALR, 16, "walrs")
        wa2s = A.alloc("wa2s", [17, 512], BF16)
        P.op("pool", lambda e: e.dma_start(out=wa2s.t[:], in_=w_a2aug), writes=[(wa2s, 0)], dma=True)
        eye_m = A.alloc("eye_m", [128, NS, NS], F32)
        P.op("sp", lambda e: e.dma_start(out=eye_m.t[:].rearrange("p a b -> p (a b)"), in_=eye16_d[0:1, :].partition_broadcast(128)),
             writes=[(eye_m, 0)], dma=True)

        def tok_major(chunk0, nchunk, name, dt=F32):
            dst = A.alloc(name, [NS, nchunk * 128], dt)
            for g in range(0, nchunk, 4):
                pb = 2 + (g // 4) % 2
                for c in range(4):
                    P.op("pe", lambda e, g=g, c=c, pb=pb: e.transpose(out=bank(pb)[0:NS, c * 128:(c + 1) * 128],
                                                                    in_=PROJT.t[:, chunk0 + g + c, :], identity=ident.t[:]),
                         reads=[(PROJT, (chunk0 + g) // 4), (ident, 0)], writes=[(psb[pb], 0)])
                P.op("dve", lambda e, g=g, pb=pb: e.tensor_copy(out=dst.t[:, g * 128:(g + 4) * 128], in_=bank(pb)[0:NS, :]),
                     reads=[(psb[pb], 0)], writes=[(dst, 0)])
            return dst

        alrs = A.alloc("alrs", [17, NS], BF16)
        P.op("dve", lambda e: e.memset(alrs.t[:], 1.0), writes=[(alrs, 0)])
        for kc in range(KC):
            P.op("pe", lambda e, kc=kc: e.matmul(bank(6)[0:16, 0:NS], walr_s.t[:, kc, :], hsT.t[:, kc, :],
                                                 start=(kc == 0), stop=(kc == KC - 1)),
                 reads=[(walr_s, 0), (hsT, 0)], writes=[(psb[6], 0)])
        P.op("act", lambda e: e.activation(out=alrs.t[0:16, :], in_=bank(6)[0:16, 0:NS], func=AF.Copy),
             reads=[(psb[6], 0)], writes=[(alrs, 0)])
        dec = A.alloc("dec_s", [NS, 512], F32)
        P.op("pe", lambda e: e.matmul(bank(7)[0:NS, :], alrs.t[0:17, :], wa2s.t[0:17, :], start=True, stop=True),
             reads=[(alrs, 0), (wa2s, 0)], writes=[(psb[7], 0)])
        P.op("act", lambda e: e.activation(out=dec.t[:], in_=bank(7)[0:NS, :], func=AF.Exp, scale=-1.0),
             reads=[(psb[7], 0)], writes=[(dec, 0)])
        P.op("act", lambda e: e.activation(out=dec.t[:], in_=dec.t[:], func=AF.Ln, bias=1.0), reads=[(dec, 0)], writes=[(dec, 0)])
        P.op("act", lambda e: e.activation(out=dec.t[:], in_=dec.t[:], func=AF.Exp, scale=-1.0 / 16.0), reads=[(dec, 0)], writes=[(dec, 0)])
        A.free(walr_s); A.free(wa2s); A.free(alrs)

        ks = tok_major(4, 4, "ks")
        vs = tok_major(8, 8, "vs", BF16)
        aT = transpose_s(dec, 0, 4, "aT_s", F32)
        A.free(dec)
        qTm = A.alloc("qTm", [128, 4, NS, NS], BF16)
        for h in range(4):
            P.op("dve", lambda e, h=h: e.tensor_tensor(out=qTm.t[:, h, :, :],
                                                       in0=PROJT.t[:, h, :].unsqueeze(2).broadcast_to([128, NS, NS]),
                                                       in1=eye_m.t[:], op=ALU.mult),
                 reads=[(PROJT, 0), (eye_m, 0)], writes=[(qTm, h)])

        NB0 = 3
        S0 = [A.alloc(f"S0_{i}", [128, 4, 256], F32) for i in range(NB0)]
        S1 = [A.alloc(f"S1_{i}", [128, 4, 256], F32) for i in range(2)]
        km = [A.alloc(f"km{i}", [NS, 512], BF16) for i in range(2)]
        S1b = [A.alloc(f"S1b_{i}", [128, 4, 256], BF16) for i in range(2)]

        def load_state(b):
            s0 = S0[b % NB0]
            P.op("sp", lambda e: e.dma_start(out=s0.t[:], in_=state_s[b].rearrange("h k v -> k h v")),
                 writes=[(s0, 0)], dma=True)

        load_state(0)
        load_state(1)
        yield
        for b in range(NS):
            s0, s1, kb = S0[b % NB0], S1[b % 2], km[b % 2]
            P.op("dve", lambda e, b=b, kb=kb: e.tensor_scalar(out=kb.t[:], in0=ks.t[:], scalar1=ident.t[0:NS, b:b + 1],
                                                             scalar2=None, op0=ALU.mult),
                 reads=[(ks, 0), (ident, 0)], writes=[(kb, 0)])
            for h in range(4):
                P.op("pe", lambda e, h=h, kb=kb: e.matmul(bank(h // 2)[:, (h % 2) * 256:(h % 2) * 256 + 256], kb.t[:, h * 128:(h + 1) * 128],
                                                          vs.t[:, h * 256:(h + 1) * 256], start=True, stop=True),
                     reads=[(kb, 0), (vs, 0)], writes=[(psb[h // 2], 0)])
            for h in range(4):
                P.op("dve", lambda e, h=h, b=b, s0=s0, s1=s1: e.scalar_tensor_tensor(
                    out=s1.t[:, h, :], in0=s0.t[:, h, :], scalar=aT.t[:, h, b:b + 1], in1=bank(h // 2)[:, (h % 2) * 256:(h % 2) * 256 + 256],
                    op0=ALU.mult, op1=ALU.add),
                    reads=[(s0, 0), (aT, 0), (psb[h // 2], 0)], writes=[(s1, h)])
            s1b = S1b[b % 2]
            for h in range(4):
                P.op("act", lambda e, h=h, s1=s1, s1b=s1b: e.activation(out=s1b.t[:, h, :], in_=s1.t[:, h, :], func=AF.Copy),
                     reads=[(s1, h)], writes=[(s1b, h)])
            for h in range(4):
                ob = 4 + h
                P.op("pe", lambda e, h=h, b=b, s1b=s1b, ob=ob: e.matmul(
                    PS[0:NS, ob * 512: ob * 512 + 256], qTm.t[:, h, b, :], s1b.t[:, h, :],
                    start=(b == 0), stop=(b == NS - 1)),
                    reads=[(qTm, h), (s1b, h)], writes=[(psb[ob], 0)])
            if b + 2 < NS:
                load_state(b + 2)
            P.op("sp", lambda e, b=b, s1=s1: e.dma_start(out=S_s[b].rearrange("h k v -> k h v"), in_=s1.t[:]),
                 reads=[(s1, h) for h in range(4)], dma=True)
        for bf in S0 + S1 + km + S1b:
            A.free(bf)
        A.free(aT); A.free(qTm); A.free(eye_m); A.free(ks); A.free(vs)
        srs = tok_major(16, 8, "srs")
        tga = tok_major(24, 8, "tga")

        o_ap = lambda a, b_: PS[0:NS, (4 + a // 256) * 512: (4 + a // 256) * 512 + 256]
        rso = sample_rstd(o_ap, [(psb[4 + h], 0) for h in range(4)], 4, 1.0 / 256, "rsos")
        gwb = bcast_row("gwb", gla_nw[0:1, :])
        ogf = A.alloc("ogf", [NS, D], F32)
        ogs = A.alloc("ogs", [NS, D], BF16)
        for h in range(4):
            P.op("dve", lambda e, h=h: e.scalar_tensor_tensor(out=ogf.t[:, h * 256:(h + 1) * 256], in0=o_ap(h * 256, (h + 1) * 256),
                                                              scalar=rso.t[:, 4 + h:5 + h], in1=gwb.t[:, h * 256:(h + 1) * 256],
                                                              op0=ALU.mult, op1=ALU.mult),
                 reads=[(psb[4 + h], 0), (rso, 0), (gwb, 0)], writes=[(ogf, 0)])
        P.op("dve", lambda e: e.tensor_tensor(out=ogs.t[:], in0=ogf.t[:], in1=srs.t[:], op=ALU.mult),
             reads=[(ogf, 0), (srs, 0)], writes=[(ogs, 0)])
        ogT = transpose_s(ogs, 0, KC, "ogT")
        A.free(rso); A.free(gwb); A.free(ogf); A.free(ogs); A.free(srs)

        mrg = A.alloc("mrg", [NS, D], F32)
        for c in range(0, D, 512):
            wt = load_w(w_pa3, c, 512, "wsmp")
            sample_proj(ogT, wt, 512, 6)
            P.op("dve", lambda e, c=c: e.scalar_tensor_tensor(out=mrg.t[:, c:c + 512], in0=tga.t[:, c:c + 512], scalar=1.0,
                                                              in1=bank(6)[0:NS, :], op0=ALU.add, op1=ALU.mult),
                 reads=[(tga, 0), (psb[6], 0)], writes=[(mrg, 0)])
            A.free(wt)
        A.free(ogT); A.free(tga)

        tgb = tok_major(32, 8, "tgb")
        us = tok_major(40, 8, "us")
        gg = tok_major(48, 8, "ggs")
        st6 = A.alloc("sbn", [NS, 2, 6], F32)
        mvs = A.alloc("smv", [NS, 4], F32)
        for c in range(2):
            P.op("dve", lambda e, c=c: e.bn_stats(out=st6.t[:, c, :], in_=gg.t[:, c * 512:(c + 1) * 512]),
                 reads=[(gg, 0)], writes=[(st6, c)])
        P.op("dve", lambda e: e.bn_aggr(out=mvs.t[:, 0:2], in_=st6.t[:]), reads=[(st6, 0), (st6, 1)], writes=[(mvs, 0)])
        P.op("act", lambda e: e.activation(out=mvs.t[:, 2:3], in_=mvs.t[:, 1:2], func=AF.Ln, bias=EPS), reads=[(mvs, 0)], writes=[(mvs, 0)])
        P.op("act", lambda e: e.activation(out=mvs.t[:, 2:3], in_=mvs.t[:, 2:3], func=AF.Exp, scale=-0.5), reads=[(mvs, 0)], writes=[(mvs, 0)])
        lwb = bcast_row("lwb", gm_lnw[0:1, :])
        lbb = bcast_row("lbb", gm_lnb[0:1, :])
        wsb = bcast_row("wsb", ws00_d[0:1, :])
        bsb = bcast_row("bsb", bs0_d[0:1, :])
        P.op("dve", lambda e: e.tensor_scalar(out=gg.t[:], in0=gg.t[:], scalar1=mvs.t[:, 0:1], scalar2=mvs.t[:, 2:3],
                                              op0=ALU.subtract, op1=ALU.mult),
             reads=[(gg, 0), (mvs, 0)], writes=[(gg, 0)])
        P.op("dve", lambda e: e.tensor_tensor(out=gg.t[:], in0=gg.t[:], in1=lwb.t[:], op=ALU.mult), reads=[(gg, 0), (lwb, 0)], writes=[(gg, 0)])
        P.op("dve", lambda e: e.tensor_tensor(out=gg.t[:], in0=gg.t[:], in1=lbb.t[:], op=ALU.add), reads=[(gg, 0), (lbb, 0)], writes=[(gg, 0)])
        P.op("sp", lambda e: e.dma_start(out=gv_s, in_=gg.t[:]), reads=[(gg, 0)], dma=True)
        sgf = A.alloc("sgf", [NS, D], F32)
        sgs = A.alloc("sgs", [NS, D], BF16)
        P.op("dve", lambda e: e.tensor_tensor(out=sgf.t[:], in0=gg.t[:], in1=wsb.t[:], op=ALU.mult), reads=[(gg, 0), (wsb, 0)], writes=[(sgf, 0)])
        P.op("dve", lambda e: e.tensor_tensor(out=sgf.t[:], in0=sgf.t[:], in1=bsb.t[:], op=ALU.add), reads=[(sgf, 0), (bsb, 0)], writes=[(sgf, 0)])
        P.op("dve", lambda e: e.tensor_tensor(out=sgs.t[:], in0=sgf.t[:], in1=us.t[:], op=ALU.mult), reads=[(sgf, 0), (us, 0)], writes=[(sgs, 0)])
        sgT = transpose_s(sgs, 0, KC, "sgT")
        for bf in (us, gg, st6, mvs, lwb, lbb, wsb, bsb, sgf, sgs):
            A.free(bf)
        mrb = A.alloc("mrb", [NS, D], BF16)
        tmm = A.alloc("tmm", [NS, 512], F32)
        for c in range(0, D, 512):
            wt = load_w(w_pb3, c, 512, "wsmp")
            sample_proj(sgT, wt, 512, 6)
            P.op("dve", lambda e, c=c: e.scalar_tensor_tensor(out=tmm.t[:], in0=tgb.t[:, c:c + 512], scalar=1.0,
                                                              in1=bank(6)[0:NS, :], op0=ALU.add, op1=ALU.mult),
                 reads=[(tgb, 0), (psb[6], 0)], writes=[(tmm, 0)])
            P.op("dve", lambda e, c=c: e.tensor_tensor(out=mrb.t[:, c:c + 512], in0=mrg.t[:, c:c + 512], in1=tmm.t[:], op=ALU.add),
                 reads=[(mrg, 0), (tmm, 0)], writes=[(mrb, 0)])
            A.free(wt)
        mT = transpose_s(mrb, 0, KC, "mT_s")
        A.free(sgT); A.free(tgb); A.free(mrg); A.free(mrb); A.free(tmm)
        for c in range(0, D, 512):
            wt = load_w(w_o3, c, 512, "wsmp")
            sample_proj(mT, wt, 512, 6)
            sample_residual(6, g2s, c)
            A.free(wt)
        A.free(mT); A.free(g2s); A.free(hsT)

    def final_sample():
        nfs = bcast_row("nfs", normf_w[0:1, :])
        ss = sample_rstd(lambda a, b: XS.t[:, a:b], [(XS, 0)], 1, 1.0 / D, "fss")
        ys = A.alloc("ys", [NS, D], F32)
        P.op("dve", lambda e: e.scalar_tensor_tensor(out=ys.t[:], in0=XS.t[:], scalar=ss.t[:, 1:2], in1=nfs.t[:],
                                                      op0=ALU.mult, op1=ALU.mult),
             reads=[(XS, 0), (ss, 0), (nfs, 0)], writes=[(ys, 0)])
        P.op("sp", lambda e: e.dma_start(out=y_s, in_=ys.t[:]), reads=[(ys, 0)], dma=True)

    w_in3 = w_in.rearrange("(k p) n -> p k n", p=128)
    w_pa3 = w_pa.rearrange("(k p) n -> p k n", p=128)
    w_pb3 = w_pb.rearrange("(k p) n -> p k n", p=128)
    w_o3 = w_o.rearrange("(k p) n -> p k n", p=128)
    OFF_Q, OFF_K, OFF_V, OFF_R, OFF_ALR, OFF_U, OFF_GV, OFF_GA, OFF_GB = 0, 512, 1024, 2048, 3072, 3088, 4112, 5136, 6160
    LN_QSCALE = float(np.log(128.0 ** -0.5))

    NSCR = 24
    wscr = nc.dram_tensor("wscr", [NSCR, 128, KC, 512], BF16).ap()
    SCR = Buf("wscr")
    WCACHE = {}

    def load_w(src3, c0, ncol, name="w", nk=KC):
        wt = ring.alloc(name, [128, nk, ncol], BF16)
        key = (id(src3), c0)
        if ncol == 512 and nk == KC and key in WCACHE:
            tid = WCACHE[key]
            P.op("sp", lambda e: e.dma_start(out=wt.t[:], in_=wscr[tid]), reads=[(SCR, tid)], writes=[(wt, 0)], dma=True)
            return wt
        P.op("pool", lambda e: e.dma_start(out=wt.t[:], in_=src3[:, :, c0:c0 + ncol]), writes=[(wt, 0)], dma=True)
        if ncol == 512 and nk == KC and len(WCACHE) < NSCR:
            tid = len(WCACHE)
            WCACHE[key] = tid
            P.op("sp", lambda e: e.dma_start(out=wscr[tid], in_=wt.t[:]), reads=[(wt, 0)], writes=[(SCR, tid)], dma=True)
        return wt

    SHOOK = {}

    def sample_hook(wt, chunk0, func, fscale, pbk):
        if not SHOOK:
            return
        hsT, PROJT = SHOOK["hsT"], SHOOK["PROJT"]
        for c in range(4):
            for kc in range(KC):
                P.op("pe", lambda e, c=c, kc=kc: e.matmul(bank(pbk)[:, c * 16:(c + 1) * 16], wt.t[:, kc, c * 128:(c + 1) * 128],
                                                          hsT.t[:, kc, :], start=(kc == 0), stop=(kc == KC - 1)),
                     reads=[(wt, 0), (hsT, 0)], writes=[(psb[pbk], 0)])
        P.op("act", lambda e: e.activation(out=PROJT.t[:, chunk0:chunk0 + 4, :],
                                           in_=bank(pbk)[:, 0:64].rearrange("p (c t) -> p c t", t=16), func=func, scale=fscale),
             reads=[(psb[pbk], 0)], writes=[(PROJT, chunk0 // 4)])

    def tm_consts(modT2):
        C = {}
        tri = A.alloc("tri", [128, 128], F32)
        tri64 = A.alloc("tri64", [128, 128], F32)
        ucum = A.alloc("ucum", [128, 128], F32)
        onesf = A.alloc("onesf", [128, 128], F32)
        wsTf = A.alloc("wsTf", [128, 4, 128], F32)
        wsTb = A.alloc("wsTb", [128, 4, 128], BF16)
        BSb = A.alloc("BSb", [128, 512], F32)
        Bblk = A.alloc("Bblk", [128, 8, 128], F32)
        walr = A.alloc("walr", [128, KC, 16], BF16)
        wa2 = A.alloc("wa2", [17, 512], BF16)
        alrT = A.alloc("alrT", [17, TBS], BF16)
        elast = A.alloc("elast", [128, 4, 32], F32)
        S32 = A.alloc("S32", [128, 4, 256], F32)
        Sbf = A.alloc("Sbf", [128, 4, 256], BF16)
        for (b, d_) in ((tri, tri_d), (tri64, tri64_d), (ucum, ucum_d), (wsTf, wsT_d)):
            P.op("sp", lambda e, b=b, d_=d_: e.dma_start(out=b.t[:], in_=d_), writes=[(b, 0)], dma=True)
        P.op("sp", lambda e: e.dma_start(out=BSb.t[:], in_=bs_d[0:1, :].partition_broadcast(128)), writes=[(BSb, 0)], dma=True)
        P.op("pool", lambda e: e.dma_start(out=walr.t[:], in_=w_in3[:, :, OFF_ALR:OFF_ALR + 16]), writes=[(walr, 0)], dma=True)
        P.op("pool", lambda e: e.dma_start(out=wa2.t[:], in_=w_a2aug), writes=[(wa2, 0)], dma=True)
        P.op("dve", lambda e: e.memset(onesf.t[:], 1.0), writes=[(onesf, 0)])
        P.op("dve", lambda e: e.memset(alrT.t[:], 1.0), writes=[(alrT, 0)])
        P.op("dve", lambda e: e.memset(S32.t[:], 0.0), writes=[(S32, h) for h in range(4)])
        P.op("dve", lambda e: e.memset(Sbf.t[:], 0.0), writes=[(Sbf, h) for h in range(4)])
        P.op("dve", lambda e: e.tensor_tensor(out=wsTf.t[:], in0=wsTf.t[:],
                                              in1=tri.t[:].unsqueeze(1).broadcast_to([128, 4, 128]), op=ALU.mult),
             reads=[(wsTf, 0), (tri, 0)], writes=[(wsTf, 0)])
        P.op("dve", lambda e: e.tensor_copy(out=wsTb.t[:], in_=wsTf.t[:]), reads=[(wsTf, 0)], writes=[(wsTb, 0)])
        P.op("pe", lambda e: e.matmul(bank(7), onesf.t[:], wsTf.t[:].rearrange("p g t -> p (g t)"), start=True, stop=True),
             reads=[(onesf, 0), (wsTf, 0)], writes=[(psb[7], 0)])
        for cb in range(8):
            g = cb // 2
            P.op("dve", lambda e, cb=cb, g=g: e.scalar_tensor_tensor(
                out=Bblk.t[:, cb, :], in0=bank(7)[:, g * 128:(g + 1) * 128], scalar=modT2.t[:, cb, 18:19],
                in1=BSb.t[:, g * 128:(g + 1) * 128], op0=ALU.mult, op1=ALU.add),
                reads=[(psb[7], 0), (modT2, 0), (BSb, 0)], writes=[(Bblk, cb)])
        A.free(tri); A.free(onesf); A.free(wsTf); A.free(BSb)
        C.update(tri64=tri64, ucum=ucum, wsTb=wsTb, Bblk=Bblk, walr=walr, wa2=wa2, alrT=alrT, elast=elast, S32=S32, Sbf=Sbf)
        return C

    def proj_fm(wt, col0, rhs, rhs_ap, rkeys, pb):
        for kc in range(KC):
            P.op("pe", lambda e, kc=kc: e.matmul(bank(pb), wt.t[:, kc, col0:col0 + 128], rhs_ap(kc),
                                                 start=(kc == 0), stop=(kc == KC - 1)),
                 reads=[(wt, 0), (rhs, rkeys(kc))], writes=[(psb[pb], 0)])

    def token_mix_block(blk, C, modT2, cols2, g2h, HTB, pre_out_hook=None, unit_hook=None):
        tri64, ucum, wsTb, Bblk = C["tri64"], C["ucum"], C["wsTb"], C["Bblk"]
        walr, wa2, alrT, elast, S32, Sbf = C["walr"], C["wa2"], C["alrT"], C["elast"], C["S32"], C["Sbf"]
        hap = lambda kc: HTB.t[:, kc, :]
        hkey = lambda kc: (kc, blk)

        sp = A.alloc("sp_tok", [128, 4, 512], F32)
        QT = A.alloc("QT", [128, 4, TBS], BF16)
        KT = A.alloc("KT", [128, 4, TBS], BF16)
        Eq = [A.alloc(f"Eq{i}", [128, TBS], F32) for i in range(2)]
        Ek = [A.alloc(f"Ek{i}", [128, TBS], F32) for i in range(2)]
        KTOK = A.alloc("KTOK", [128, 4, 512], BF16)

        def d_gen():
            for kc in range(KC):
                P.op("pe", lambda e, kc=kc: e.matmul(bank(6)[0:16, :], walr.t[:, kc, :], HTB.t[:, kc, :],
                                                     start=(kc == 0), stop=(kc == KC - 1)),
                     reads=[(walr, 0), (HTB, (kc, blk))], writes=[(psb[6], 0)])
            P.op("act", lambda e: e.activation(out=alrT.t[0:16, :], in_=bank(6)[0:16, :], func=AF.Copy),
                 reads=[(psb[6], 0)], writes=[(alrT, 0)])
            yield
            for j in range(4):
                pb = j % 2
                P.op("pe", lambda e, j=j, pb=pb: e.matmul(bank(pb), alrT.t[0:17, j * 128:(j + 1) * 128], wa2.t[0:17, :],
                                                          start=True, stop=True),
                     reads=[(alrT, 0), (wa2, 0)], writes=[(psb[pb], 0)])
                P.op("act", lambda e, j=j, pb=pb: e.activation(out=sp.t[:, j, :], in_=bank(pb), func=AF.Exp, scale=-1.0),
                     reads=[(psb[pb], 0)], writes=[(sp, j)])
                P.op("act", lambda e, j=j: e.activation(out=sp.t[:, j, :], in_=sp.t[:, j, :], func=AF.Ln, bias=1.0),
                     reads=[(sp, j)], writes=[(sp, j)])
                yield

            wq = load_w(w_in3, OFF_Q, 512, "wq")
            wk = load_w(w_in3, OFF_K, 512, "wk")
            if blk == NTB - 1:
                sample_hook(wq, 0, AF.Identity, 128.0 ** -0.5, 7)
                sample_hook(wk, 4, AF.Identity, 1.0, 7)
            for h in range(4):
                pbc = 2 + h % 2
                for j in range(4):
                    P.op("pe", lambda e, j=j, h=h, pbc=pbc: e.matmul(bank(pbc)[:, j * 128:(j + 1) * 128],
                                                                    sp.t[:, j, h * 128:(h + 1) * 128], ucum.t[:],
                                                                    start=True, stop=True),
                         reads=[(sp, j), (ucum, 0)], writes=[(psb[pbc], 0)])
                eq, ek = Eq[h % 2], Ek[h % 2]
                P.op("act", lambda e, eq=eq, pbc=pbc: e.activation(out=eq.t[:], in_=bank(pbc), func=AF.Exp, bias=LN_QSCALE),
                     reads=[(psb[pbc], 0)], writes=[(eq, 0)])
                P.op("act", lambda e, ek=ek, pbc=pbc: e.activation(out=ek.t[:], in_=bank(pbc), func=AF.Exp, scale=-1.0),
                     reads=[(psb[pbc], 0)], writes=[(ek, 0)])
                P.op("act", lambda e, h=h, pbc=pbc: e.activation(
                    out=elast.t[:, h, blk * 8:(blk + 1) * 8],
                    in_=bank(pbc).rearrange("p (c t) -> p c t", t=64)[:, :, 63], func=AF.Exp),
                    reads=[(psb[pbc], 0)], writes=[(elast, (h, blk))])
                yield
                proj_fm(wq, h * 128, HTB, hap, hkey, 0)
                P.op("dve", lambda e, h=h, eq=eq: e.tensor_tensor(out=QT.t[:, h, :], in0=bank(0), in1=eq.t[:], op=ALU.mult),
                     reads=[(psb[0], 0), (eq, 0)], writes=[(QT, h)])
                yield
                proj_fm(wk, h * 128, HTB, hap, hkey, 1)
                P.op("dve", lambda e, h=h, ek=ek: e.tensor_tensor(out=KT.t[:, h, :], in0=bank(1), in1=ek.t[:], op=ALU.mult),
                     reads=[(psb[1], 0), (ek, 0)], writes=[(KT, h)])
                yield
            A.free(wq); A.free(wk); A.free(sp)
            for b in Eq + Ek:
                A.free(b)

            for j in range(4):
                for h in range(4):
                    P.op("pe", lambda e, j=j, h=h: e.transpose(out=bank_bf(3)[:, h * 128:(h + 1) * 128],
                                                               in_=KT.t[:, h, j * 128:(j + 1) * 128], identity=identb.t[:]),
                         reads=[(KT, h), (identb, 0)], writes=[(psb[3], 0)])
                P.op("dve", lambda e, j=j: e.tensor_copy(out=KTOK.t[:, j, :], in_=bank_bf(3)[:, 0:512]),
                     reads=[(psb[3], 0)], writes=[(KTOK, j)])
                yield


        VTOK = A.alloc("VTOK", [128, 4, 1024], BF16)

        def v_gen():
            n = 0
            for cb in range(2):
                wvt = load_w(w_in3, OFF_V + cb * 512, 512, "wv")
                if blk == NTB - 1:
                    sample_hook(wvt, 8 + cb * 4, AF.Identity, 1.0, 7)
                for j in range(4):
                    pb = 4 + n % 2
                    for kc in range(KC):
                        P.op("pe", lambda e, j=j, kc=kc, pb=pb, wvt=wvt: e.matmul(
                            bank(pb), HTB.t[:, kc, j * 128:(j + 1) * 128], wvt.t[:, kc, :],
                            start=(kc == 0), stop=(kc == KC - 1)),
                            reads=[(HTB, (kc, blk)), (wvt, 0)], writes=[(psb[pb], 0)])
                    if n % 2 == 0:
                        P.op("act", lambda e, j=j, cb=cb, pb=pb: e.activation(out=VTOK.t[:, j, cb * 512:(cb + 1) * 512],
                                                                             in_=bank(pb), func=AF.Copy),
                             reads=[(psb[pb], 0)], writes=[(VTOK, (j, cb))])
                    else:
                        P.op("dve", lambda e, j=j, cb=cb, pb=pb: e.tensor_copy(out=VTOK.t[:, j, cb * 512:(cb + 1) * 512],
                                                                              in_=bank(pb)),
                             reads=[(psb[pb], 0)], writes=[(VTOK, (j, cb))])
                    n += 1
                    yield
                A.free(wvt)

        vg = v_gen()
        for _ in d_gen():
            next(vg, None)
        for _ in vg:
            pass

        O32 = A.alloc("O32", [128, 8, TBS], F32)
        sTb = [A.alloc(f"sT{h}", [128, 128], BF16) for h in range(4)]
        tmpS = [A.alloc(f"tmpS{h}", [128, 256], F32) for h in range(4)]
        def scan_gen():
            sc_ps = lambda h: PS[:, h * 512 + 256: h * 512 + 384]
            o_ps = lambda h, vb, c: PS[:, h * 512 + vb * 128 + c * 64: h * 512 + vb * 128 + c * 64 + 64]
            dS = lambda h: PS[:, (4 + h // 2) * 512 + (h % 2) * 256: (4 + h // 2) * 512 + (h % 2) * 256 + 256]
            for j in range(4):
                for h in range(4):
                    P.op("pe", lambda e, j=j, h=h: e.matmul(sc_ps(h), KT.t[:, h, j * 128:(j + 1) * 128],
                                                            QT.t[:, h, j * 128:(j + 1) * 128], start=True, stop=True),
                         reads=[(KT, h), (QT, h)], writes=[(psb[h], 0)])
                for h in range(4):
                    P.op("dve", lambda e, h=h: e.tensor_tensor(out=sTb[h].t[:], in0=sc_ps(h), in1=tri64.t[:], op=ALU.mult),
                         reads=[(psb[h], 0), (tri64, 0)], writes=[(sTb[h], 0)])
                yield
                for c in range(2):
                    chunk = blk * 8 + j * 2 + c
                    for h in range(4):
                        for vb in range(2):
                            P.op("pe", lambda e, j=j, h=h, vb=vb, c=c: e.matmul(
                                o_ps(h, vb, c), VTOK.t[:, j, h * 256 + vb * 128: h * 256 + (vb + 1) * 128],
                                sTb[h].t[:, c * 64:(c + 1) * 64], start=True, stop=False),
                                reads=[(VTOK, (j, h // 2)), (sTb[h], 0)], writes=[(psb[h], 0)])
                            P.op("pe", lambda e, j=j, h=h, vb=vb, c=c: e.matmul(
                                o_ps(h, vb, c), Sbf.t[:, h, vb * 128:(vb + 1) * 128],
                                QT.t[:, h, j * 128 + c * 64: j * 128 + (c + 1) * 64], start=False, stop=True),
                                reads=[(Sbf, h), (QT, h)], writes=[(psb[h], 0)])
                        P.op("pe", lambda e, j=j, h=h, c=c: e.matmul(
                            dS(h), KTOK.t[c * 64:(c + 1) * 64, j, h * 128:(h + 1) * 128],
                            VTOK.t[c * 64:(c + 1) * 64, j, h * 256:(h + 1) * 256], start=True, stop=True),
                            reads=[(KTOK, j), (VTOK, (j, h // 2))], writes=[(psb[4 + h // 2], 0)])
                    for h in range(4):
                        ts = tmpS[h]
                        P.op("dve", lambda e, ts=ts, h=h: e.tensor_tensor(out=ts.t[:], in0=dS(h), in1=S32.t[:, h, :], op=ALU.add),
                             reads=[(psb[4 + h // 2], 0), (S32, h)], writes=[(ts, 0)])
                        P.op("act", lambda e, ts=ts, h=h, chunk=chunk: e.activation(
                            out=Sbf.t[:, h, :], in_=ts.t[:], func=AF.Identity, scale=elast.t[:, h, chunk:chunk + 1]),
                            reads=[(ts, 0), (elast, (h, blk))], writes=[(Sbf, h)])
                        P.op("act", lambda e, ts=ts, h=h, chunk=chunk: e.activation(
                            out=S32.t[:, h, :], in_=ts.t[:], func=AF.Identity, scale=elast.t[:, h, chunk:chunk + 1]),
                            reads=[(ts, 0), (elast, (h, blk))], writes=[(S32, h)])
                    yield
                for h in range(4):
                    P.op("act", lambda e, j=j, h=h: e.activation(
                        out=O32.t[:, h * 2:(h + 1) * 2, j * 128:(j + 1) * 128],
                        in_=PS[:, h * 512: h * 512 + 256].rearrange("p (v t) -> p v t", t=128), func=AF.Copy),
                        reads=[(psb[h], 0)], writes=[(O32, (h, j))])
                yield
        SR = A.alloc("SR", [128, 8, TBS], BF16)
        TGA = A.alloc("TGA", [128, 8, TBS], BF16)
        UT = A.alloc("UT", [128, 8, TBS], BF16)

        def filler_gen():
            for (off, dst, func, fscale) in ((OFF_R, SR, AF.Silu, 1.0), (OFF_GA, TGA, AF.Tanh, 0.5), (OFF_U, UT, AF.Gelu, 1.0)):
                for half in range(2):
                    wt = load_w(w_in3, off + half * 512, 512, "wfill")
                    if blk == NTB - 1:
                        sample_hook(wt, {OFF_R: 16, OFF_GA: 24, OFF_U: 40}[off] + half * 4, func, fscale, 6 + half)
                    for dd in range(4):
                        d = half * 4 + dd
                        pb = 6 + d % 2
                        proj_fm(wt, dd * 128, HTB, hap, hkey, pb)
                        P.op("act", lambda e, d=d, pb=pb, dst=dst, func=func, fscale=fscale: e.activation(
                            out=dst.t[:, d, :], in_=bank(pb), func=func, scale=fscale),
                            reads=[(psb[pb], 0)], writes=[(dst, d)])
                        yield
                    A.free(wt)

        fg = filler_gen()
        nstep = 0
        for _ in scan_gen():
            nstep += 1
            for _k in range(2 if nstep % 2 == 1 else 1):
                next(fg, None)
        for _ in fg:
            pass
        for b in sTb + tmpS:
            A.free(b)
        A.free(QT); A.free(KT); A.free(KTOK); A.free(VTOK)

        sqo = A.alloc("sqo", [128, 2, TBS], BF16)
        rso = [A.alloc(f"rso{h}", [128, TBS], F32) for h in range(4)]
        for h in range(4):
            for vb in range(2):
                P.op("dve", lambda e, h=h, vb=vb: e.tensor_tensor(out=sqo.t[:, vb, :], in0=O32.t[:, h * 2 + vb, :],
                                                                 in1=O32.t[:, h * 2 + vb, :], op=ALU.mult),
                     reads=[(O32, (h, j)) for j in range(4)], writes=[(sqo, vb)])
            for vb in range(2):
                P.op("pe", lambda e, vb=vb: e.matmul(bank(7), onesb.t[:], sqo.t[:, vb, :], start=(vb == 0), stop=(vb == 1)),
                     reads=[(onesb, 0), (sqo, vb)], writes=[(psb[7], 0)])
            P.op("act", lambda e, h=h: e.activation(out=rso[h].t[:], in_=bank(7), func=AF.Ln, scale=1.0 / 256, bias=EPS),
                 reads=[(psb[7], 0)], writes=[(rso[h], 0)])
            P.op("act", lambda e, h=h: e.activation(out=rso[h].t[:], in_=rso[h].t[:], func=AF.Exp, scale=-0.5),
                 reads=[(rso[h], 0)], writes=[(rso[h], 0)])
        A.free(sqo)
        TGB = A.alloc("TGB", [128, 8, TBS], BF16)
        for half in range(2):
            wt = load_w(w_in3, OFF_GB + half * 512, 512, "wgb")
            if blk == NTB - 1:
                sample_hook(wt, 32 + half * 4, AF.Tanh, 0.5, 7)
            for dd in range(4):
                d = half * 4 + dd
                pb = d % 2
                proj_fm(wt, dd * 128, HTB, hap, hkey, pb)
                P.op("act", lambda e, d=d, pb=pb: e.activation(out=TGB.t[:, d, :], in_=bank(pb), func=AF.Tanh, scale=0.5),
                     reads=[(psb[pb], 0)], writes=[(TGB, d)])
            A.free(wt)
        OG = SR
        t1 = [A.alloc(f"t1{i}", [128, TBS], F32) for i in range(2)]
        for d in range(8):
            h = d // 2
            pb = d % 2
            P.op("dve", lambda e, d=d, h=h, pb=pb: e.scalar_tensor_tensor(
                out=t1[pb].t[:], in0=O32.t[:, d, :], scalar=modT2.t[:, 8 + d, 17:18], in1=rso[h].t[:],
                op0=ALU.mult, op1=ALU.mult),
                reads=[(O32, (h, j)) for j in range(4)] + [(modT2, 0), (rso[h], 0)], writes=[(t1[pb], 0)])
            P.op("dve", lambda e, d=d, pb=pb: e.tensor_tensor(out=OG.t[:, d, :], in0=t1[pb].t[:], in1=SR.t[:, d, :], op=ALU.mult),
                 reads=[(t1[pb], 0), (SR, d)], writes=[(OG, d)])
        A.free(O32)
        for b in rso:
            A.free(b)

        MA = A.alloc("MA", [128, 8, TBS], BF16)
        tg = []

        def gated_proj(w3, src, skeys, goff, accumulate, pre=None, unit_hook=None):
            for half in range(2):
                wp = load_w(w3, half * 512, 512, "wp")
                wg = load_w(w_in3, goff + half * 512, 512, "wg") if pre is None else None
                for dd in range(4):
                    d = half * 4 + dd
                    pb = d % 2
                    proj_fm(wp, dd * 128, src, lambda kc: src.t[:, kc, :], skeys, pb)
                    if pre is None:
                        proj_fm(wg, dd * 128, HTB, hap, hkey, 2 + pb)
                        P.op("act", lambda e, pb=pb: e.activation(out=tg[pb].t[:], in_=bank(2 + pb), func=AF.Tanh, scale=0.5),
                             reads=[(psb[2 + pb], 0)], writes=[(tg[pb], 0)])
                    if pre is not None and accumulate:
                        P.op("dve", lambda e, d=d, pb=pb: e.scalar_tensor_tensor(
                            out=t1[pb].t[:], in0=pre.t[:, d, :], scalar=1.0, in1=bank(pb), op0=ALU.add, op1=ALU.mult),
                            reads=[(pre, d), (psb[pb], 0)], writes=[(t1[pb], 0)])
                        P.op("dve", lambda e, d=d, pb=pb: e.tensor_tensor(out=MA.t[:, d, :], in0=MA.t[:, d, :], in1=t1[pb].t[:], op=ALU.add),
                             reads=[(t1[pb], 0), (MA, d)], writes=[(MA, d)])
                    elif pre is not None:
                        P.op("dve", lambda e, d=d, pb=pb: e.scalar_tensor_tensor(
                            out=MA.t[:, d, :], in0=pre.t[:, d, :], scalar=1.0, in1=bank(pb), op0=ALU.add, op1=ALU.mult),
                            reads=[(pre, d), (psb[pb], 0)], writes=[(MA, d)])
                    elif not accumulate:
                        P.op("dve", lambda e, d=d, pb=pb: e.scalar_tensor_tensor(
                            out=MA.t[:, d, :], in0=tg[pb].t[:], scalar=1.0, in1=bank(pb), op0=ALU.add, op1=ALU.mult),
                            reads=[(tg[pb], 0), (psb[pb], 0)], writes=[(MA, d)])
                    else:
                        P.op("dve", lambda e, d=d, pb=pb: e.scalar_tensor_tensor(
                            out=t1[pb].t[:], in0=tg[pb].t[:], scalar=1.0, in1=bank(pb), op0=ALU.add, op1=ALU.mult),
                            reads=[(tg[pb], 0), (psb[pb], 0)], writes=[(t1[pb], 0)])
                        P.op("dve", lambda e, d=d, pb=pb: e.tensor_tensor(out=MA.t[:, d, :], in0=MA.t[:, d, :], in1=t1[pb].t[:], op=ALU.add),
                             reads=[(t1[pb], 0), (MA, d)], writes=[(MA, d)])
                    if accumulate and unit_hook is not None:
                        unit_hook()
                A.free(wp)
                if wg is not None:
                    A.free(wg)

        GG = A.alloc("GG", [128, 4, 1024], F32)
        stats = A.alloc("bnst", [128, 4, 2, 6], F32)
        mv = A.alloc("bnmv", [128, 4, 2], F32)
        rs = A.alloc("bnrs", [128, 4], F32)
        wgv = [load_w(w_in3, OFF_GV + cb * 512, 512, "wgv") for cb in range(2)]
        if blk == NTB - 1:
            for cb in range(2):
                sample_hook(wgv[cb], 48 + cb * 4, AF.Gelu, 1.0, 7)
        n = 0
        for j in range(4):
            for cb in range(2):
                pb = n % 2
                for kc in range(KC):
                    P.op("pe", lambda e, j=j, cb=cb, kc=kc, pb=pb: e.matmul(
                        bank(pb), HTB.t[:, kc, j * 128:(j + 1) * 128], wgv[cb].t[:, kc, :],
                        start=(kc == 0), stop=(kc == KC - 1)),
                        reads=[(HTB, (kc, blk)), (wgv[cb], 0)], writes=[(psb[pb], 0)])
                P.op("act", lambda e, j=j, cb=cb, pb=pb: e.activation(out=GG.t[:, j, cb * 512:(cb + 1) * 512], in_=bank(pb), func=AF.Gelu),
                     reads=[(psb[pb], 0)], writes=[(GG, (j, cb))])
                P.op("dve", lambda e, j=j, cb=cb: e.bn_stats(out=stats.t[:, j, cb, :], in_=GG.t[:, j, cb * 512:(cb + 1) * 512]),
                     reads=[(GG, (j, cb))], writes=[(stats, (j, cb))])
                n += 1
            P.op("dve", lambda e, j=j: e.bn_aggr(out=mv.t[:, j, :], in_=stats.t[:, j, :, :]),
                 reads=[(stats, (j, 0)), (stats, (j, 1))], writes=[(mv, j)])
        for b in wgv:
            A.free(b)
        gated_proj(w_pa3, OG, lambda kc: kc, OFF_GA, False, pre=TGA)
        A.free(OG); A.free(TGA)

        ZT = A.alloc("ZT", [128, 4, 1024], BF16)
        P.op("act", lambda e: e.activation(out=rs.t[:], in_=mv.t[:, :, 1], func=AF.Ln, bias=EPS),
             reads=[(mv, j) for j in range(4)], writes=[(rs, 0)])
        P.op("act", lambda e: e.activation(out=rs.t[:], in_=rs.t[:], func=AF.Exp, scale=-0.5),
             reads=[(rs, 0)], writes=[(rs, 0)])
        nb_ = A.alloc("bnnb", [128, 4], F32)
        P.op("dve", lambda e: e.scalar_tensor_tensor(out=nb_.t[:], in0=mv.t[:, :, 0], scalar=-1.0, in1=rs.t[:],
                                                      op0=ALU.mult, op1=ALU.mult),
             reads=[(mv, j) for j in range(4)] + [(rs, 0)], writes=[(nb_, 0)])
        for j in range(4):
            P.op("act", lambda e, j=j: e.activation(out=ZT.t[:, j, :], in_=GG.t[:, j, :], func=AF.Identity,
                                                    scale=rs.t[:, j:j + 1], bias=nb_.t[:, j:j + 1]),
                 reads=[(GG, (j, 0)), (GG, (j, 1)), (nb_, 0), (rs, 0)], writes=[(ZT, j)])
        A.free(nb_)
        A.free(GG); A.free(stats)
        for cb in range(8):
            g = cb // 2
            pb = cb % 2
            for j in range(4):
                P.op("pe", lambda e, j=j, cb=cb, g=g, pb=pb: e.matmul(bank(pb)[:, j * 128:(j + 1) * 128],
                                                                   ZT.t[:, j, cb * 128:(cb + 1) * 128], wsTb.t[:, g, :],
                                                                   start=True, stop=True),
                     reads=[(ZT, j), (wsTb, 0)], writes=[(psb[pb], 0)])
            P.op("dve", lambda e, cb=cb, pb=pb: e.scalar_tensor_tensor(
                out=t1[pb].t[:].rearrange("p (j t) -> p j t", t=128), in0=bank(pb).rearrange("p (j t) -> p j t", t=128),
                scalar=modT2.t[:, 16 + cb, 17:18], in1=Bblk.t[:, cb, :].unsqueeze(1).broadcast_to([128, 4, 128]),
                op0=ALU.mult, op1=ALU.add),
                reads=[(psb[pb], 0), (modT2, 0), (Bblk, cb)], writes=[(t1[pb], 0)])
            P.op("dve", lambda e, cb=cb, pb=pb: e.tensor_tensor(out=UT.t[:, cb, :], in0=t1[pb].t[:], in1=UT.t[:, cb, :], op=ALU.mult),
                 reads=[(t1[pb], 0), (UT, cb)], writes=[(UT, cb)])
        A.free(ZT); A.free(mv); A.free(rs)

        gated_proj(w_pb3, UT, lambda kc: kc, OFF_GB, True, pre=TGB, unit_hook=unit_hook)
        A.free(UT); A.free(TGB)

        if pre_out_hook is not None:
            pre_out_hook()
        HTBn = None
        if blk + 1 < NTB:
            HTBn = A.alloc("HTB", [128, KC, TBS], BF16)
            tmpn = norm_tmp()
            norm_block(blk + 1, cols2, tmpn, HTBn, 0)
            free_norm_tmp(tmpn)
        for half in range(2):
            wo = load_w(w_o3, half * 512, 512, "wo")
            for dd in range(4):
                d = half * 4 + dd
                pb = d % 2
                proj_fm(wo, dd * 128, MA, lambda kc: MA.t[:, kc, :], lambda kc: kc, pb)
                P.op("dve", lambda e, d=d, pb=pb: e.scalar_tensor_tensor(
                    out=XT.t[:, d, blk * TBS:(blk + 1) * TBS], in0=bank(pb), scalar=g2h.t[:, d:d + 1],
                    in1=XT.t[:, d, blk * TBS:(blk + 1) * TBS], op0=ALU.mult, op1=ALU.add),
                    reads=[(psb[pb], 0), (XT, (d, blk)), (g2h, 0)], writes=[(XT, (d, blk))])
                if unit_hook is not None:
                    unit_hook()
            A.free(wo)
        A.free(MA)
        for b in tg + t1:
            A.free(b)
        A.free(HTB)
        return HTBn

    def half_gate(modT, name, scale):
        hg = A.alloc(name, [128, KC], F32)
        P.op("dve", lambda e: e.tensor_scalar(out=hg.t[:], in0=modT.t[:, 16:24, 16], scalar1=scale, scalar2=None, op0=ALU.mult),
             reads=[(modT, 0)], writes=[(hg, 0)])
        return hg

    NEXT = {}

    def ffn_phase(i, idx, mod, modT, next_mod=None, fin=None):
        cols = prompt_cols(modT, 0)
        hg = half_gate(modT, f"hg{i}", 0.5)
        samp = dict(hsT=sample_norm(mod, norm_w[i]), acts=A.alloc("acts", [NS, DFF], BF16), hgs=sample_gate(mod, "hgs", 0.5))
        A.free(mod)
        HT = A.alloc("HT", [128, KC, T], BF16)
        tmp = norm_tmp()
        norm_block(0, cols, tmp, HT, 0)
        hooks = {}

        def pre_tb(tb):
            if tb + 1 < NTB:
                norm_block(tb + 1, cols, tmp, HT, (tb + 1) * TBS)
            else:
                free_norm_tmp(tmp)
        hooks["pre_tb"] = pre_tb
        if next_mod is not None:
            mp = mod_pieces(next_mod[0], next_mod[1], NEXT)

            def down_unit(nd):
                if nd % 4 == 1:
                    next(mp, None)
            hooks["down_unit"] = down_unit
            hooks["drain"] = lambda: [None for _ in mp]
        if fin is not None:
            def fin_unit(nd):
                tb, d = (nd - 1) // 8, (nd - 1) % 8
                if tb >= 1 and d % 2 == 1:
                    if not fin:
                        fin.update(final_init())
                    t = (tb - 1) * 4 + d // 2
                    final_tile(fin, t, 6 if t % 2 == 0 else 2)
            hooks["down_unit"] = fin_unit
        ffn(idx, hg, HT, samp, hooks)
        if "drain" in hooks:
            hooks["drain"]()
        for bf in samp.values():
            A.free(bf)
        A.free(HT); A.free(modT); A.free(cols); A.free(hg)

    def mod0_alloc():
        mod = A.alloc("mod0", [32, 3 * D], F32)
        modT = A.alloc("modT0", [128, 24, 32], F32)
        bada = A.alloc("bada", [17, 3 * D], F32)
        P.op("dve", lambda e: e.memset(mod.t[:], 0.0), writes=[(mod, 0), (mod, 1)])
        return mod, modT, bada

    def mod0_loads(mod, bada):
        P.op("sp", lambda e: e.dma_start(out=bada.t[:], in_=b_ada[0:1, 0:3 * D].partition_broadcast(17)),
             writes=[(bada, 0)], dma=True)
        P.op("sp", lambda e: e.dma_start(out=mod.t[17:18, 0:D], in_=norm_w[0][0:1, :]), writes=[(mod, 1)], dma=True)

    def mod0_block(mod, blk):
        c0 = blk * 512
        wt = ring.alloc("wada", [128, KC, 512], BF16)
        P.op("pool", lambda e: e.dma_start(
            out=wt.t[:], in_=w_ada.rearrange("(k p) n -> p k n", p=128)[:, :, c0:c0 + 512]),
            writes=[(wt, 0)], dma=True)
        for kc in range(KC):
            P.op("pe", lambda e, kc=kc: e.matmul(bank(6)[0:17, :], scT.t[:, kc, 0:17], wt.t[:, kc, :],
                                                 start=(kc == 0), stop=(kc == KC - 1)),
                 reads=[(scT, 0), (wt, 0)], writes=[(psb[6], 0)])
        P.op("dve", lambda e: e.tensor_copy(out=mod.t[0:17, blk * 512:(blk + 1) * 512], in_=bank(6)[0:17, :]),
             reads=[(psb[6], 0), (mod, 0)], writes=[(mod, 0)])
        A.free(wt)

    xh = {}
    mod0 = A.alloc("mod0", [32, 3 * D], F32)
    modT0 = A.alloc("modT0", [128, 24, 32], F32)
    P.op("dve", lambda e: e.memset(mod0.t[:], 0.0), writes=[(mod0, 0), (mod0, 1)])

    def xhook(t):
        if 4 <= t < 10:
            mod_mm(mod0, mod_load(0, t - 4), t - 4)
    xh["fn"] = xhook
    load_x(xh)
    P.op("sp", lambda e: e.dma_start(out=mod0.t[17:18, 0:D], in_=norm_w[0][0:1, :]), writes=[(mod0, 0)], dma=True)
    mod_finish(mod0, modT0)

    if stage >= 1:
        ffn_phase(0, 0, mod0, modT0,
                  next_mod=(1, [(norm_w[1][0:1, :], 0), (gla_nw[0:1, :], 0), (gm_lnw[0:1, :], 0), (gm_lnb[0:1, :], 0)]))
    if stage >= 2:
        mod2, modT2 = NEXT["mm"]
        cols2 = prompt_cols(modT2, 0)
        g2h = half_gate(modT2, "g2h", 0.5)
        hsT2 = sample_norm(mod2, norm_w[1])
        g2s = sample_gate(mod2, "g2s", 0.5)
        A.free(mod2)
        PROJT = A.alloc("PROJT", [128, 56, NS], F32)
        SHOOK.update(hsT=hsT2, PROJT=PROJT)
        C = tm_consts(modT2)
        HTBc = A.alloc("HTB", [128, KC, TBS], BF16)
        tmp0 = norm_tmp()
        norm_block(0, cols2, tmp0, HTBc, 0)
        free_norm_tmp(tmp0)
        for blk in range(NTB):
            hook = None
            uh = None
            if blk == NTB - 1:
                def hook():
                    NEXT["stm"] = sample_token_mix(hsT2, g2s, PROJT)
                    next(NEXT["stm"])
                if stage >= 3:
                    mp3 = mod_pieces(2, [(norm_w[2][0:1, :], 0)], NEXT)
                    cnt = [0]

                    def uh():
                        cnt[0] += 1
                        if cnt[0] % 2 == 1:
                            next(mp3, None)
            HTBc = token_mix_block(blk, C, modT2, cols2, g2h, HTBc, hook, uh)
            if blk == NTB - 1 and stage >= 3:
                for _ in mp3:
                    pass
        P.op("sp", lambda e: e.dma_start(out=S_p.rearrange("h k v -> k h v"), in_=C["S32"].t[:]),
             reads=[(C["S32"], h) for h in range(4)], dma=True)
        for b in C.values():
            A.free(b)
        A.free(modT2); A.free(cols2); A.free(g2h)
        for _ in NEXT["stm"]:
            pass
        A.free(PROJT)
    fin = {}
    if stage < 3:
        fin.update(final_init())
    if stage >= 3:
        mod3, modT3 = NEXT["mm"]
        ffn_phase(2, 1, mod3, modT3, fin=fin)
        for t in range(12, 16):
            final_tile(fin, t, 6 if t % 2 == 0 else 2)
    else:
        for t in range(T // 128):
            final_tile(fin, t, 4 + 2 * (t % 2))
    final_sample()

    P.emit(nc)
    return nc


def make_in_maps(inp):
    f = lambda a: np.ascontiguousarray(np.asarray(a, dtype=np.float32))
    shared = {
        "w_ada": f(inp["w_ada"][0]),
        "b_ada": f(inp["b_ada"][0]).reshape(1, -1),
        "norm1_w": f(inp["norm1_w"][0]).reshape(1, -1),
        "norm2_w": f(inp["norm2_w"][0]).reshape(1, -1),
        "norm3_w": f(inp["norm3_w"][0]).reshape(1, -1),
        "normf_w": f(inp["normf_w"]).reshape(1, -1),
        "ffn1_w13": f(inp["ffn1_w13"][0]),
        "ffn1_w2": f(inp["ffn1_w2"][0]),
        "ffn2_w13": f(inp["ffn2_w13"][0]),
        "ffn2_w2": f(inp["ffn2_w2"][0]),
        "ident": np.eye(128, dtype=np.float32),
        "w_in": f(inp["w_in"][0]),
        "w_a2aug": f(np.concatenate([inp["w_a2"][0], inp["b_a"][0][None, :]], axis=0)),
        "gla_norm_w": f(inp["gla_norm_w"][0]).reshape(1, -1),
        "gm_ln_w": f(inp["gm_ln_w"][0]).reshape(1, -1),
        "gm_ln_b": f(inp["gm_ln_b"][0]).reshape(1, -1),
        "wsT": f(np.transpose(inp["gm_ws"][0], (2, 0, 1))),
        "gm_bs": f(inp["gm_bs"][0]).reshape(1, -1),
        "w_pa": f(inp["w_pa"][0]),
        "w_pb": f(inp["w_pb"][0]),
        "w_o": f(inp["w_o"][0]),
        "tri": _TRI,
        "tri64": _TRI64,
        "ucum": _UCUM,
        "ws00_row": f(np.repeat(inp["gm_ws"][0][:, 0, 0], 256)).reshape(1, -1),
        "bs0_row": f(np.repeat(inp["gm_bs"][0][:, 0], 256)).reshape(1, -1),
        "eye16": np.eye(16, dtype=np.float32).reshape(1, 256),
    }
    maps = []
    for c in range(NCORES):
        m = dict(shared)
        m["x_p"] = f(inp["x_prompt"][c])
        m["x_s"] = f(inp["x_sample"][c * NS:(c + 1) * NS, 0, :])
        m["state_s"] = f(inp["state_gla"][0, c * NS:(c + 1) * NS])
        m["c17"] = f(np.concatenate([inp["c_sample"][c * NS:(c + 1) * NS], inp["c_prompt"][c:c + 1]], axis=0))
        maps.append(m)
    return maps


_i = np.arange(128)
_TRI = (_i[:, None] <= _i[None, :]).astype(np.float32)
_TRI64 = (_TRI * ((_i[:, None] // 64) == (_i[None, :] // 64))).astype(np.float32)
_UCUM = (_TRI64 * (-1.0 / 16.0)).astype(np.float32)
_NC_CACHE = {}


def run(inp, stage=99, trace=False):
    if stage not in _NC_CACHE:
        _NC_CACHE[stage] = build(stage)
    nc = _NC_CACHE[stage]
    maps = make_in_maps(inp)
    used = set()
    return run_bass_kernel_spmd(nc, maps, core_ids=list(range(NCORES)), trace=trace)


def kernel(**inputs):
    res = run(inputs)
    r = res.results
    y_prompt = np.stack([r[c]["y_p"] for c in range(NCORES)], axis=0).astype(np.float32)
    y_sample = np.concatenate([r[c]["y_s"] for c in range(NCORES)], axis=0).reshape(NCORES * NS, 1, D).astype(np.float32)
    S_prompt = np.stack([r[c]["S_p"] for c in range(NCORES)], axis=0)[None].astype(np.float32)
    S_sample = np.concatenate([r[c]["S_s"] for c in range(NCORES)], axis=0)[None].astype(np.float32)
    gv_sample = np.concatenate([r[c]["gv_s"] for c in range(NCORES)], axis=0).reshape(1, NCORES * NS, 1, D).astype(np.float32)
    return (y_prompt, y_sample, S_prompt, S_sample, gv_sample)
```
